# Optimizing a Trainium2 kernel written in Bass

```python
import math
import jax
import jax.numpy as jnp
from jax import lax
import numpy as np

D_MODEL = 1024
BATCH = 4
SEQ = 4096
DEPTH = 4

CTX_LEN = 256
GRID_W = 64
EPS = 1e-6

A_HEADS = 4
A_QK = 128
A_V = 256
A_WIDTH = A_HEADS * A_V
A_CHUNK = 64
F_BIAS_LO = 3.0
F_BIAS_HI = 6.0

B_HEADS = 8
B_QK = 64
B_V = 128
B_WIDTH = B_HEADS * B_V
B_QBLOCK = 128
ROPE_AXIS_DIM = B_QK // 2
ROPE_THETA = 10000.0

C_GROUPS = 4
C_GROUP_DIM = 256
C_WIDTH = C_GROUPS * C_GROUP_DIM

N_BRANCH = 3

COL_SIZES = (
    A_HEADS * A_QK, A_HEADS * A_QK, A_WIDTH, 4 * A_HEADS, A_WIDTH, A_WIDTH,
    B_HEADS * 2 * B_QK, B_HEADS * 2 * B_QK, B_WIDTH, B_WIDTH,
    C_WIDTH, C_WIDTH,
    N_BRANCH * D_MODEL,
)
D_IN = 2 * A_HEADS * A_QK + 3 * A_WIDTH + 4 * A_HEADS + 4 * B_HEADS * B_QK + 2 * B_WIDTH + 2 * C_WIDTH + N_BRANCH * D_MODEL

kernel_name = "hybrid_mlstm_diffattn_fourier_prefix_block"


def rmsnorm(x, w):
    xf = x.astype(jnp.float32)
    y = xf * lax.rsqrt(jnp.mean(xf * xf, axis=-1, keepdims=True) + EPS)
    return (y * w.astype(jnp.float32)).astype(x.dtype)


def split_cols(p):
    parts, start = [], 0
    for size in COL_SIZES:
        parts.append(p[..., start:start + size])
        start += size
    return parts


def axial_rope_tables(n_tokens):
    rows = n_tokens // GRID_W
    row = jnp.repeat(jnp.arange(rows, dtype=jnp.float32), GRID_W)
    col = jnp.tile(jnp.arange(GRID_W, dtype=jnp.float32), rows)
    inv_freq = ROPE_THETA ** (-jnp.arange(0, ROPE_AXIS_DIM, 2, dtype=jnp.float32) / ROPE_AXIS_DIM)
    ang_r = row[:, None] * inv_freq
    ang_c = col[:, None] * inv_freq
    ex = lambda t: t[:, None, None, :]
    return (ex(jnp.cos(ang_r)), ex(jnp.sin(ang_r)), ex(jnp.cos(ang_c)), ex(jnp.sin(ang_c)))


def _rotate(x, cos, sin):
    x1, x2 = jnp.split(x, 2, axis=-1)
    return jnp.concatenate([x1 * cos - x2 * sin, x2 * cos + x1 * sin], axis=-1)


def apply_axial_rope(x, rope):
    cos_r, sin_r, cos_c, sin_c = rope
    xr, xc = jnp.split(x.astype(jnp.float32), 2, axis=-1)
    return jnp.concatenate([_rotate(xr, cos_r, sin_r), _rotate(xc, cos_c, sin_c)], axis=-1).astype(x.dtype)


def mlstm_prep(q, k, v, g, b_if):
    bsz, n, _ = q.shape
    heads = lambda t, d: t.astype(jnp.float32).reshape(bsz, n, A_HEADS, d).transpose(0, 2, 1, 3)
    qh = heads(q, A_QK)
    kh = heads(k, A_QK) * (A_QK ** -0.5)
    vh = heads(v, A_V)
    pre = (g.astype(jnp.float32).reshape(bsz, n, 4, A_HEADS)
           + b_if.astype(jnp.float32).reshape(4, A_HEADS)).transpose(2, 0, 3, 1)
    fwd = (pre[0], jax.nn.log_sigmoid(pre[1]))
    bwd = (pre[2], jax.nn.log_sigmoid(pre[3]))
    return qh, kh, vh, (fwd, bwd)


def mlstm_zero_state(bsz):
    return (jnp.zeros((bsz, A_HEADS, A_QK, A_V), jnp.float32),
            jnp.zeros((bsz, A_HEADS, A_QK), jnp.float32),
            jnp.zeros((bsz, A_HEADS), jnp.float32))


def mlstm_scan(q, k, v, ig, lf, state):
    bsz, nh, n, _ = q.shape
    nc = n // A_CHUNK

    def chunks(t):
        return jnp.moveaxis(t.reshape(bsz, nh, nc, A_CHUNK, *t.shape[3:]), 2, 0)

    tril = jnp.tril(jnp.ones((A_CHUNK, A_CHUNK), dtype=bool))

    def body(carry, inp):
        c_st, n_st, m_st = carry
        qc, kc, vc, ic, fc = inp
        b = jnp.cumsum(fc, axis=-1)
        dmat = jnp.where(tril, b[..., :, None] - b[..., None, :] + ic[..., None, :], -jnp.inf)
        inter = b + m_st[..., None]
        m_row = jnp.maximum(jnp.max(dmat, axis=-1), inter)
        a = jnp.exp(dmat - m_row[..., None]) * jnp.einsum('bhtd,bhsd->bhts', qc, kc)
        w_inter = jnp.exp(inter - m_row)
        num = jnp.einsum('bhts,bhsv->bhtv', a, vc) + w_inter[..., None] * jnp.einsum('bhtd,bhdv->bhtv', qc, c_st)
        den = jnp.sum(a, axis=-1) + w_inter * jnp.einsum('bhtd,bhd->bht', qc, n_st)
        h = num / jnp.maximum(jnp.abs(den), jnp.exp(-m_row))[..., None]
        b_last = b[..., -1]
        g = b_last[..., None] - b + ic
        m_new = jnp.maximum(b_last + m_st, jnp.max(g, axis=-1))
        wk = jnp.exp(g - m_new[..., None])
        decay = jnp.exp(b_last + m_st - m_new)
        c_new = decay[..., None, None] * c_st + jnp.einsum('bhsd,bhsv->bhdv', kc * wk[..., None], vc)
        n_new = decay[..., None] * n_st + jnp.einsum('bhs,bhsd->bhd', wk, kc)
        return (c_new, n_new, m_new), h

    state, h = lax.scan(body, state, (chunks(q), chunks(k), chunks(v), chunks(ig), chunks(lf)))
    return jnp.moveaxis(h, 0, 2).reshape(bsz, nh, n, v.shape[-1]), state


def mlstm_direction(q, k, v, ig, lf, state, reverse):
    if reverse:
        h, state = mlstm_scan(jnp.flip(q, 2), jnp.flip(k, 2), jnp.flip(v, 2), jnp.flip(ig, 2), jnp.flip(lf, 2), state)
        return jnp.flip(h, 2), state
    return mlstm_scan(q, k, v, ig, lf, state)


def mlstm_mixer(q, k, v, gates, q_c, k_c, v_c, gates_c):
    h_lat, h_ctx = [], []
    for d, reverse in enumerate((False, True)):
        ig_c, lf_c = gates_c[d]
        ig, lf = gates[d]
        hc_d, st = mlstm_direction(q_c, k_c, v_c, ig_c, lf_c, mlstm_zero_state(q_c.shape[0]), reverse)
        h_d, _ = mlstm_direction(q, k, v, ig, lf, st, reverse)
        h_lat.append(h_d)
        h_ctx.append(hc_d)
    return h_lat[0] + h_lat[1], h_ctx[0] + h_ctx[1]


def mlstm_out(h, o, z, a_norm_w):
    bsz, _, n, _ = h.shape
    hn = rmsnorm(h.transpose(0, 2, 1, 3), a_norm_w.reshape(A_HEADS, A_V)).reshape(bsz, n, A_WIDTH)
    return hn.astype(o.dtype) * jax.nn.sigmoid(o) * jax.nn.silu(z)


def diffattn_prep(q, k, v, q_norm_w, k_norm_w, rope):
    bsz, n, _ = q.shape
    qh = rmsnorm(q.reshape(bsz, n, B_HEADS, 2, B_QK), q_norm_w)
    kh = rmsnorm(k.reshape(bsz, n, B_HEADS, 2, B_QK), k_norm_w)
    if rope is not None:
        qh = apply_axial_rope(qh, rope)
        kh = apply_axial_rope(kh, rope)
    vh = v.reshape(bsz, n, B_HEADS, B_V).transpose(0, 2, 1, 3)
    return qh.transpose(0, 2, 3, 1, 4), kh.transpose(0, 2, 3, 1, 4), vh


def attend(q, k, v, lam):
    s = jnp.einsum('bhmqd,bhmkd->bhmqk', q, k, preferred_element_type=jnp.float32) * (B_QK ** -0.5)
    p = jax.nn.softmax(s, axis=-1)
    w = p[:, :, 0] - lam * p[:, :, 1]
    return jnp.einsum('bhqk,bhkv->bhqv', w.astype(v.dtype), v)


def attend_blocked(q, k, v, lam):
    bsz, nh, nm, n, d = q.shape
    nb = n // B_QBLOCK
    qb = jnp.moveaxis(q.reshape(bsz, nh, nm, nb, B_QBLOCK, d), 3, 0)
    o = lax.map(lambda blk: attend(blk, k, v, lam), qb)
    return jnp.moveaxis(o, 0, 2).reshape(bsz, nh, n, v.shape[-1])


def diffattn_out(o, subln_w, lam_init):
    bsz, _, n, _ = o.shape
    return (rmsnorm(o, subln_w) * (1.0 - lam_init)).transpose(0, 2, 1, 3).reshape(bsz, n, B_WIDTH)


def fourier_mix(u):
    bsz, n, _ = u.shape
    uf = u.astype(jnp.float32).reshape(bsz, n, C_GROUPS, C_GROUP_DIM)
    y = jnp.fft.fft2(uf, axes=(1, 3), norm="ortho").real
    return y.reshape(bsz, n, C_WIDTH).astype(u.dtype)


def merge(y_a, y_b, y_c, mg, w_a_out, w_b_out, w_c_out, w_out):
    g_a, g_b, g_c = jnp.split(jax.nn.sigmoid(mg), N_BRANCH, axis=-1)
    y = g_a * (y_a @ w_a_out) + g_b * (y_b @ w_b_out) + g_c * (y_c @ w_c_out)
    return y @ w_out


def hybrid_layer(x, ctx, sc, sc_ctx, rope, layer_idx, last, norm_w, w_ada, b_ada, w_in, b_if, a_norm_w,
                 q_norm_w, k_norm_w, lambda_q1, lambda_k1, lambda_q2, lambda_k2, subln_w,
                 w_a_out, w_b_out, w_c_out, w_out):
    shift, scale, gate = jnp.split((sc @ w_ada + b_ada)[:, None, :], 3, axis=-1)
    shift_c, scale_c, gate_c = jnp.split(sc_ctx @ w_ada + b_ada, 3)
    h = rmsnorm(x, norm_w) * (1.0 + scale) + shift
    hc = rmsnorm(ctx, norm_w) * (1.0 + scale_c) + shift_c
    aq, ak, av, ag, ao, az, bq, bk, bv, bz, cu, cz, mg = split_cols(h @ w_in)
    aq_c, ak_c, av_c, ag_c, ao_c, az_c, bq_c, bk_c, bv_c, bz_c, cu_c, cz_c, mg_c = split_cols(hc @ w_in)

    qa, ka, va, ga = mlstm_prep(aq, ak, av, ag, b_if)
    qa_c, ka_c, va_c, ga_c = mlstm_prep(aq_c, ak_c, av_c, ag_c, b_if)
    h_a, h_a_c = mlstm_mixer(qa, ka, va, ga, qa_c, ka_c, va_c, ga_c)
    y_a = mlstm_out(h_a, ao, az, a_norm_w)

    lam_init = 0.8 - 0.6 * math.exp(-0.3 * layer_idx)
    f32 = jnp.float32
    lam = (jnp.exp(jnp.sum(lambda_q1.astype(f32) * lambda_k1.astype(f32)))
           - jnp.exp(jnp.sum(lambda_q2.astype(f32) * lambda_k2.astype(f32))) + lam_init)
    qb, kb, vb = diffattn_prep(bq, bk, bv, q_norm_w, k_norm_w, rope)
    qb_c, kb_c, vb_c = diffattn_prep(bq_c, bk_c, bv_c, q_norm_w, k_norm_w, None)
    o_b = attend_blocked(qb, jnp.concatenate([kb_c, kb], axis=3), jnp.concatenate([vb_c, vb], axis=2), lam)
    y_b = diffattn_out(o_b, subln_w, lam_init) * jax.nn.silu(bz)

    y_c = fourier_mix(cu) * jax.nn.silu(cz)

    x = x + gate * merge(y_a, y_b, y_c, mg, w_a_out, w_b_out, w_c_out, w_out)
    if not last:
        y_a_c = mlstm_out(h_a_c, ao_c, az_c, a_norm_w)
        y_b_c = diffattn_out(attend(qb_c, kb_c, vb_c, lam), subln_w, lam_init) * jax.nn.silu(bz_c)
        y_c_c = fourier_mix(cu_c) * jax.nn.silu(cz_c)
        ctx = ctx + gate_c * merge(y_a_c, y_b_c, y_c_c, mg_c, w_a_out, w_b_out, w_c_out, w_out)
    return x, ctx


def setup_inputs(seed: int = 0) -> dict:
    key = jax.random.key(seed)
    ks = jax.random.split(key, 24)
    f32 = jnp.float32
    nrm = lambda k, shape, s: jax.random.normal(k, shape, f32) * s
    f_base = jnp.linspace(F_BIAS_LO, F_BIAS_HI, A_HEADS, dtype=f32)
    zeros_h = jnp.zeros((A_HEADS,), f32)
    b_if = jnp.concatenate([zeros_h, f_base, zeros_h, f_base])[None, :] + nrm(ks[8], (DEPTH, 4 * A_HEADS), 0.1)
    return {
        "x": nrm(ks[0], (BATCH, SEQ, D_MODEL), 1.0),
        "c": nrm(ks[1], (BATCH, D_MODEL), 1.0),
        "ctx": nrm(ks[2], (BATCH, CTX_LEN, D_MODEL), 1.0),
        "c_ctx": nrm(ks[3], (D_MODEL,), 1.0),
        "norm_w": 1.0 + nrm(ks[4], (DEPTH, D_MODEL), 0.02),
        "w_ada": nrm(ks[5], (DEPTH, D_MODEL, 3 * D_MODEL), 0.5 * D_MODEL ** -0.5),
        "b_ada": nrm(ks[6], (DEPTH, 3 * D_MODEL), 0.02),
        "w_in": nrm(ks[7], (DEPTH, D_MODEL, D_IN), D_MODEL ** -0.5),
        "b_if": b_if,
        "a_norm_w": 1.0 + nrm(ks[9], (DEPTH, A_WIDTH), 0.02),
        "q_norm_w": 1.0 + nrm(ks[10], (DEPTH, B_QK), 0.02),
        "k_norm_w": 1.0 + nrm(ks[11], (DEPTH, B_QK), 0.02),
        "lambda_q1": nrm(ks[12], (DEPTH, B_QK), 0.1),
        "lambda_k1": nrm(ks[13], (DEPTH, B_QK), 0.1),
        "lambda_q2": nrm(ks[14], (DEPTH, B_QK), 0.1),
        "lambda_k2": nrm(ks[15], (DEPTH, B_QK), 0.1),
        "subln_w": 1.0 + nrm(ks[16], (DEPTH, B_V), 0.02),
        "w_a_out": nrm(ks[17], (DEPTH, A_WIDTH, D_MODEL), A_WIDTH ** -0.5),
        "w_b_out": nrm(ks[18], (DEPTH, B_WIDTH, D_MODEL), B_WIDTH ** -0.5),
        "w_c_out": nrm(ks[19], (DEPTH, C_WIDTH, D_MODEL), C_WIDTH ** -0.5),
        "w_out": nrm(ks[20], (DEPTH, D_MODEL, D_MODEL), D_MODEL ** -0.5),
    }


def reference(x, c, ctx, c_ctx, norm_w, w_ada, b_ada, w_in, b_if, a_norm_w, q_norm_w, k_norm_w,
              lambda_q1, lambda_k1, lambda_q2, lambda_k2, subln_w, w_a_out, w_b_out, w_c_out, w_out):
    rope = axial_rope_tables(x.shape[1])
    sc = jax.nn.silu(c)
    sc_ctx = jax.nn.silu(c_ctx)
    for l in range(DEPTH):
        x, ctx = hybrid_layer(x, ctx, sc, sc_ctx, rope, l, l == DEPTH - 1,
                              norm_w[l], w_ada[l], b_ada[l], w_in[l], b_if[l], a_norm_w[l],
                              q_norm_w[l], k_norm_w[l], lambda_q1[l], lambda_k1[l], lambda_q2[l], lambda_k2[l],
                              subln_w[l], w_a_out[l], w_b_out[l], w_c_out[l], w_out[l])
    return x
```

```python
import math
from contextlib import ExitStack

import numpy as np
import concourse.bass as bass
import concourse.mybir as mybir
from concourse.bass_utils import run_bass_kernel_spmd

F32 = mybir.dt.float32
BF16 = mybir.dt.bfloat16
AF = mybir.ActivationFunctionType
ALU = mybir.AluOpType
AX = mybir.AxisListType

D = 1024
NCTX = 256
NLAT = 4096
NTOK = NCTX + NLAT
NT = NTOK // 128
NCH = NTOK // 64
DEPTH = 4
EPS = 1e-6
NSUB = 2
NA = 2
NB = 4
NG = 2
SUBW = 5128
SB_BASE = 16640
SB_END = 229376


class Slot:
    def __init__(self, S, name):
        self.sem = S.new_sem(name)
        self.total = 0


class Sched:
    ENG = ("pe", "act", "dve", "pool", "sp")

    def __init__(self, nc, stack):
        self.nc = nc
        self.stack = stack
        self.ops = {e: [] for e in self.ENG}
        self.sem = {e: self.new_sem("s_" + e) for e in self.ENG}
        self.cnt = {e: 0 for e in self.ENG}
        self.waited = {}
        self.lastw = {}
        self.reads = {}
        self.slots = {}
        self.same_engine_sync = True

    def new_sem(self, name):
        return self.stack.enter_context(self.nc.semaphore(name))

    def slot(self, name):
        if name not in self.slots:
            self.slots[name] = Slot(self, "d_" + name)
        return self.slots[name]

    def _need(self, eng, toks, tok):
        if tok is None:
            return
        sem, val, src = tok
        if src == eng and (eng == "pe" or not self.same_engine_sync):
            return
        if self.waited.get((eng, id(sem)), 0) >= val:
            return
        cur = toks.get(id(sem))
        if cur is None or cur[1] < val:
            toks[id(sem)] = (sem, val)

    def _deps(self, eng, reads, writes):
        toks = {}
        for k in reads:
            self._need(eng, toks, self.lastw.get(k))
        for k in writes:
            self._need(eng, toks, self.lastw.get(k))
            for t in self.reads.get(k, ()):
                self._need(eng, toks, t)
        waits = list(toks.values())
        for sem, val in waits:
            self.waited[(eng, id(sem))] = val
        return waits

    def _commit(self, tok, reads, writes):
        for k in reads:
            self.reads.setdefault(k, []).append(tok)
        for k in writes:
            self.lastw[k] = tok
            self.reads[k] = []

    def op(self, eng, fn, reads=(), writes=()):
        waits = self._deps(eng, reads, writes)
        self.cnt[eng] += 1
        tok = (self.sem[eng], self.cnt[eng], eng)
        self.ops[eng].append((waits, fn, (self.sem[eng], 1)))
        self._commit(tok, reads, writes)
        return tok

    def dma(self, eng, slotname, fn, reads=(), writes=(), inc=16):
        slot = self.slot(slotname)
        waits = self._deps(eng, reads, writes)
        slot.total += inc
        tok = (slot.sem, slot.total, "dma")
        self.ops[eng].append((waits, fn, (slot.sem, inc)))
        self._commit(tok, reads, writes)
        return tok

    def barrier(self):
        toks = [(self.sem[e], self.cnt[e], "x") for e in self.ENG if self.cnt[e] > 0]
        toks += [(s.sem, s.total, "dma") for s in self.slots.values() if s.total > 0]
        for e in self.ENG:
            need = {}
            for t in toks:
                if t[0] is self.sem[e]:
                    continue
                self._need(e, need, t)
            waits = list(need.values())
            for sem, val in waits:
                self.waited[(e, id(sem))] = val
            if waits:
                self.ops[e].append((waits, None, None))

    def emit(self):
        nc = self.nc
        with nc.Block() as block:
            def mk(e):
                def body(h):
                    for waits, fn, inc in self.ops[e]:
                        for sem, val in waits:
                            h.wait_ge(sem, val)
                        if fn is not None:
                            fn(h).then_inc(inc[0], inc[1])
                return body
            block.tensor(mk("pe"))
            block.scalar(mk("act"))
            block.vector(mk("dve"))
            block.gpsimd(mk("pool"))
            block.sync(mk("sp"))


def T(name, idxs):
    return [(name, i) for i in idxs]


def hk(js, k):
    return [("hT", j, k) for j in js]


def tiles_of(t0, n):
    return range(t0 // 128, (t0 + n + 127) // 128)


def make_consts():
    c = {}
    p = np.arange(128)
    c["ident"] = np.eye(128, dtype=np.float32)
    tok = (np.arange(32)[None, :] * 128 + p[:, None]).astype(np.float32)
    row = np.floor(tok / 64.0).astype(np.float32)
    col = (tok - row * 64.0).astype(np.float32)
    inv = (10000.0 ** (-np.arange(0, 32, 2, dtype=np.float32) / 32.0)).astype(np.float32)
    ar = (row[..., None] * inv).astype(np.float32)
    ac = (col[..., None] * inv).astype(np.float32)
    cosT = np.concatenate([np.cos(ar), np.cos(ar), np.cos(ac), np.cos(ac)], axis=-1)
    sinT = np.concatenate([-np.sin(ar), np.sin(ar), -np.sin(ac), np.sin(ac)], axis=-1)
    c["ropec"] = cosT.astype(np.float32)
    c["ropes"] = sinT.astype(np.float32)
    s = p % 64
    t = np.arange(64)
    c["maskf"] = (s[:, None] <= t[None, :]).astype(np.float32)
    c["maskb"] = (s[:, None] >= t[None, :]).astype(np.float32)
    same = (p[:, None] // 64) == (p[None, :] // 64)
    c["trif"] = (same & (p[:, None] <= p[None, :])).astype(np.float32)
    c["trib"] = (same & (p[:, None] >= p[None, :])).astype(np.float32)
    c["chsel"] = ((p[:, None] // 64) == np.arange(2)[None, :]).astype(np.float32)
    sel = np.zeros((NA, NA, 128), np.float32)
    for h in range(NA):
        sel[h, h, :] = 1.0
    c["sel"] = sel
    dl = np.zeros((NA, NT, NA), np.float32)
    for h in range(NA):
        dl[h, :, h] = 1.0
    c["dlt"] = dl
    c["ones2"] = np.ones((NA, 64), np.float32)
    d = np.arange(256)
    ang = 2 * np.pi * np.outer(d, d) / 256.0
    f256 = np.concatenate([np.cos(ang), -np.sin(ang)], axis=1) / 16.0
    c["f256"] = f256.reshape(2, 128, 512).transpose(1, 0, 2).astype(np.float32)
    n = np.arange(64)
    a64 = 2 * np.pi * np.outer(n, n) / 64.0
    C64, S64 = np.cos(a64) / 8.0, np.sin(a64) / 8.0
    c["rb1"] = np.concatenate([C64, -S64], axis=1).astype(np.float32)
    c["rb2"] = np.concatenate([S64, C64], axis=1).astype(np.float32)
    n2 = (p % 64)[:, None, None]
    k1 = np.arange(64)[None, :, None]
    k2 = np.arange(64)[None, None, :]
    ag = 2 * np.pi * (n2 * (k1 + 64 * k2) % 4096) / 4096.0
    c["gc2"] = (np.cos(ag) / 16.0).astype(np.float32)
    c["gs2"] = (np.sin(ag) / 16.0).astype(np.float32)
    nn = np.arange(256)
    a256 = 2 * np.pi * (np.outer(nn, nn) % 256) / 256.0
    c["cc"] = (np.cos(a256) / 32.0).reshape(2, 128, 256).transpose(1, 0, 2).astype(np.float32)
    c["sc"] = (np.sin(a256) / 32.0).reshape(2, 128, 256).transpose(1, 0, 2).astype(np.float32)
    return c


def win_cols(subsets):
    cols = []
    for s in subsets:
        for i in range(NA):
            h = NA * s + i
            cols += list(range(0 + 128 * h, 128 * h + 128))
            cols += list(range(512 + 128 * h, 512 + 128 * h + 128))
            cols += list(range(1024 + 256 * h, 1024 + 256 * h + 256))
            cols += list(range(2064 + 256 * h, 2064 + 256 * h + 256))
            cols += list(range(3088 + 256 * h, 3088 + 256 * h + 256))
        for kind in range(4):
            for i in range(NA):
                cols.append(2048 + kind * 4 + NA * s + i)
        for i in range(NB):
            h = NB * s + i
            cols += list(range(4112 + 128 * h, 4112 + 128 * h + 128))
            cols += list(range(5136 + 128 * h, 5136 + 128 * h + 128))
            cols += list(range(6160 + 128 * h, 6160 + 128 * h + 128))
            cols += list(range(7184 + 128 * h, 7184 + 128 * h + 128))
        for i in range(NG):
            g = NG * s + i
            cols += list(range(8208 + 256 * g, 8208 + 256 * g + 256))
            cols += list(range(9232 + 256 * g, 9232 + 256 * g + 256))
    cols += list(range(10256, 13328))
    assert len(cols) == len(subsets) * SUBW + 3 * D
    return np.asarray(cols)


def gate_bias_idx(subsets):
    idx = []
    for s in subsets:
        for kind in range(4):
            for i in range(NA):
                idx.append(kind * 4 + NA * s + i)
    return np.asarray(idx)


def feat_rows(subsets):
    return np.concatenate([np.arange(512 * s, 512 * s + 512) for s in subsets])


class StopBuild(Exception):
    pass


class Ctx:
    def cut(self, name):
        if self.cfg.get("cut") == name:
            raise StopBuild()

    def __init__(self, nc, st, cfg):
        self.nc = nc
        self.st = st
        self.cfg = cfg
        self.S = Sched(nc, st)
        self.pair = bool(cfg.get("pair", False))
        self.nsub = 1 if self.pair else NSUB
        self.mg0 = self.nsub * SUBW
        self.pers = SB_BASE
        self.scr0 = None
        self.scr = None
        self.uid = 0
        self.d = {}
        self.ps = {}

    def _alloc(self, name, shape, dt, off):
        self.uid += 1
        t = self.nc.alloc_sbuf_tensor_at(f"{name}_{self.uid}", list(shape), dt, offset=off)
        return t.ap()

    @staticmethod
    def _bytes(shape, dt):
        n = 1
        for s in shape[1:]:
            n *= s
        b = n * (2 if dt == BF16 else 4)
        return (b + 63) // 64 * 64

    def pb(self, name, shape, dt):
        assert self.scr0 is None
        off = self.pers
        self.pers += self._bytes(shape, dt)
        assert self.pers <= SB_END, "persistent SBUF overflow"
        return self._alloc(name, shape, dt, off)

    def freeze(self):
        self.scr0 = self.pers
        self.scr = self.scr0

    def reset(self):
        self.S.barrier()
        self.scr = self.scr0

    def sb(self, name, shape, dt):
        off = self.scr
        self.scr += self._bytes(shape, dt)
        assert self.scr <= SB_END, f"scratch SBUF overflow at {name}: {self.scr - SB_END}"
        return self._alloc(name, shape, dt, off)

    def mm(self, out, lhsT, rhs, start, stop, r, w, **kw):
        return self.S.op("pe", lambda h: h.matmul(out, lhsT, rhs, start=start, stop=stop, **kw), r, w)

    def tr(self, out, in_, ident, r, w):
        return self.S.op("pe", lambda h: h.transpose(out, in_, ident), r, w)

    def act(self, out, in_, func, r, w, scale=1.0, bias=0.0, accum_out=None):
        if accum_out is None:
            return self.S.op("act", lambda h: h.activation(out=out, in_=in_, func=func, bias=bias, scale=scale), r, w)
        return self.S.op("act", lambda h: h.activation(out=out, in_=in_, func=func, bias=bias, scale=scale,
                                                       accum_out=accum_out), r, w)

    def tt(self, eng, out, in0, in1, op, r, w):
        return self.S.op(eng, lambda h: h.tensor_tensor(out, in0, in1, op), r, w)

    def ts(self, eng, out, in0, s1, s2, op0, op1, r, w):
        if s2 is None:
            return self.S.op(eng, lambda h: h.tensor_scalar(out, in0, s1, None, op0), r, w)
        return self.S.op(eng, lambda h: h.tensor_scalar(out, in0, s1, s2, op0, op1), r, w)

    def stt(self, out, in0, scalar, in1, op0, op1, r, w, accum_out=None):
        if accum_out is None:
            return self.S.op("dve", lambda h: h.scalar_tensor_tensor(out, in0, scalar, in1, op0, op1), r, w)
        return self.S.op("dve", lambda h: h.scalar_tensor_tensor(out, in0, scalar, in1, op0, op1,
                                                                 accum_out=accum_out), r, w)

    def cp(self, eng, out, in_, r, w):
        if eng == "act":
            return self.S.op("act", lambda h: h.copy(out, in_), r, w)
        return self.S.op(eng, lambda h: h.tensor_copy(out, in_), r, w)

    def memset(self, eng, ap, val, w):
        return self.S.op(eng, lambda h: h.memset(ap, val), (), w)

    def dma(self, eng, slot, out, in_, r, w, **kw):
        return self.S.dma(eng, slot, lambda h: h.dma_start(out=out, in_=in_, **kw), r, w)

    def rstd(self, out, ss, n, r, w):
        self.ts("dve", out, ss, 1.0 / n, EPS, ALU.mult, ALU.add, r, w)
        self.act(out, out, AF.Sqrt, w, w)
        return self.S.op("dve", lambda h: h.reciprocal(out, out), w, w)


def wslice(K, l, c0, n):
    return K.d["win"][l].rearrange("(k p) n -> p k n", p=128)[:, :, c0:c0 + n]


CONST_SHAPES = dict(ident=[128, 128], ropec=[128, 32, 64], ropes=[128, 32, 64], maskf=[128, 64], maskb=[128, 64],
                    trif=[128, 128], trib=[128, 128], chsel=[128, 2], sel=[NA, NA, 128], dlt=[NA, NT, NA],
                    ones2=[NA, 64], f256=[128, 2, 512], rb1=[64, 128], rb2=[64, 128], gc2=[128, 64, 64],
                    gs2=[128, 64, 64], cc=[128, 2, 256], sc=[128, 2, 256])


def declare_dram(K):
    nc, d = K.nc, K.d
    DEPTH = K.cfg.get("layers", 4)
    inp = lambda n, s: nc.dram_tensor(n, list(s), F32, kind="ExternalInput").ap()
    d["xin"] = inp("xin", [NTOK, D])
    d["cvec"] = inp("cvec", [128, 8, 2])
    d["normw"] = inp("normw", [DEPTH, 128, 8])
    d["wada"] = inp("wada", [DEPTH, D, 3 * D])
    d["bada"] = inp("bada", [DEPTH, 128, 24])
    ns = K.nsub
    d["win"] = inp("win", [DEPTH, D, ns * SUBW + 3 * D])
    d["bif"] = inp("bif", [DEPTH, ns * 4 * NA])
    d["anw"] = inp("anw", [DEPTH, ns * 512])
    d["qknw"] = inp("qknw", [DEPTH, 256])
    d["lam4"] = inp("lam4", [DEPTH, 256])
    d["subw"] = inp("subw", [DEPTH, 128])
    for n in ("wao", "wbo", "wco"):
        d[n] = inp(n, [DEPTH, ns * 512, D])
    d["wo"] = inp("wo", [DEPTH, D, D])
    for n, s in CONST_SHAPES.items():
        d["c_" + n] = inp("c_" + n, s)
    d["out"] = nc.dram_tensor("out", [NLAT, D], F32, kind="ExternalOutput").ap()
    d["xctx"] = nc.dram_tensor("xctx", [NCTX, D], F32, kind="Internal").ap()
    d["yT"] = nc.dram_tensor("yT", [3, 4, 128, NTOK], BF16, kind="Internal").ap()
    d["gsc"] = nc.dram_tensor("gsc", [2, D], F32, kind="Internal").ap()
    if K.pair:
        d["xpart"] = nc.dram_tensor("xpart", [NTOK, D], F32, kind="Internal").ap()
        d["xcur"] = nc.dram_tensor("xcur", [NTOK, D], F32, kind="Internal").ap()
    if K.cfg.get("dbg"):
        d["dbg_hT"] = nc.dram_tensor("dbg_hT", [128, 8, NTOK], BF16, kind="ExternalOutput").ap()
        d["dbg_yT"] = nc.dram_tensor("dbg_yT", [K.nsub, 3, 4, 128, NTOK], BF16, kind="ExternalOutput").ap()
        d["dbg_ctx"] = nc.dram_tensor("dbg_ctx", [NCTX, D], F32, kind="ExternalOutput").ap()


def xrows(K, l, j, write=False):
    if K.pair:
        if write:
            return K.d["xpart"][j * 128:(j + 1) * 128, :], ("xp", j)
        src = K.d["xin"] if l == 0 else K.d["xcur"]
        return src[j * 128:(j + 1) * 128, :], ("xd", j)
    if j < 2:
        src = K.d["xin"] if (l == 0 and not write) else K.d["xctx"]
        return src[j * 128:(j + 1) * 128, :], ("xd", j)
    jj = j - 2
    if l == 0 and not write:
        return K.d["xin"][NCTX + jj * 128:NCTX + (jj + 1) * 128, :], ("xd", j)
    return K.d["out"][jj * 128:(jj + 1) * 128, :], ("xd", j)


def setup(K):
    S, d = K.S, K.d
    K.hT = K.pb("hT", [128, 8, NTOK], BF16)
    K.ident = K.pb("ident", [128, 128], BF16)
    K.identf = K.pb("identf", [128, 128], F32)
    K.c_mhalf = K.pb("mhalf", [128, 64], F32)
    K.scT = K.pb("scT", [128, 8, 2], BF16)
    K.A1 = K.pb("A1", [128, 8, 2], F32)
    K.B1 = K.pb("B1", [128, 8, 2], F32)
    K.gateb = K.pb("gateb", [128, 2, D], F32)
    K.lam = K.pb("lam", [128, 1], F32)
    K.freeze()
    nc = K.nc
    K.pT = [K.st.enter_context(nc.psum_tensor(f"pT{i}", [128, 1024], BF16)) for i in range(2)]
    K.pf = [K.st.enter_context(nc.psum_tensor(f"pf{i}", [128, 512], F32)) for i in range(6)]
    K.dma("pool", "c0", K.ident, d["c_ident"], (), ["ident"])
    K.dma("sp", "c1", K.identf, d["c_ident"], (), ["identf"])
    K.memset("dve", K.c_mhalf, -0.5, ["mhalf"])
    cv = K.sb("cv", [128, 8, 2], F32)
    th = K.sb("cth", [128, 8, 2], F32)
    K.dma("sp", "c2", cv, d["cvec"], (), ["cv"])
    K.act(th, cv, AF.Tanh, ["cv"], ["cth"], scale=0.5)
    K.stt(th, th, 1.0, cv, ALU.add, ALU.mult, ["cth", "cv"], ["cth"])
    K.ts("dve", K.scT, th, 0.5, None, ALU.mult, None, ["cth"], ["scT"])


def phase_norm(K, l):
    S, d = K.S, K.d
    K.reset()
    normw = K.sb("normw", [128, 8], F32)
    bada = K.sb("bada", [128, 24], F32)
    modT = K.sb("modT", [128, 24, 2], F32)
    wad = K.sb("wad", [128, 8, D], BF16)
    K.dma("sp", "sv0", normw, d["normw"][l], (), ["normw"])
    K.dma("sp", "sv1", bada, d["bada"][l], (), ["bada"])
    pm = K.pf[0][:, 0:48]
    wv = d["wada"][l].rearrange("(k p) n -> p k n", p=128)
    for third in range(3):
        K.dma("pool", "wad", wad, wv[:, :, third * D:(third + 1) * D], (), ["wad"], max_dma_last_dim=4096)
        for m in range(8):
            g = third * 8 + m
            for k in range(8):
                K.mm(pm[:, 2 * g:2 * g + 2], wad[:, k, m * 128:(m + 1) * 128], K.scT[:, k, :], k == 0, k == 7,
                     ["wad", "scT"], ["pf0"])
    K.tt("dve", modT, K.pf[0][:, 0:48].rearrange("p (g j) -> p g j", j=2),
         bada.unsqueeze(2).to_broadcast([128, 24, 2]), ALU.add, ["pf0", "bada"], ["modT"])
    K.stt(K.A1, modT[:, 8:16, :], 1.0, normw.unsqueeze(2).to_broadcast([128, 8, 2]), ALU.add, ALU.mult,
          ["modT", "normw"], ["A1"])
    K.cp("dve", K.B1, modT[:, 0:8, :], ["modT"], ["B1"])
    gt = K.sb("gt", [128, 2, 8], F32)
    gtT = K.sb("gtT", [8, 2, 128], F32)
    for j in range(2):
        K.ts("dve", gt[:, j, :], modT[:, 16:24, j], 0.5, None, ALU.mult, None, ["modT"], ["gt"])
    for j in range(2):
        K.tr(K.pf[1][0:8, j * 128:(j + 1) * 128], gt[:, j, :], K.identf, ["gt", "identf"], ["pf1"])
    K.cp("dve", gtT, K.pf[1][0:8, 0:256].rearrange("p (j f) -> p j f", j=2), ["pf1"], ["gtT"])
    K.dma("sp", "gsc", d["gsc"].rearrange("j (k p) -> k j p", p=128), gtT, ["gtT"], ["gsc"])
    for j in range(2):
        K.dma("sp", "gsc", K.gateb[:, j, :], d["gsc"][j].partition_broadcast(128), ["gsc"], ["gateb"])
    K.cut("p0")
    NXB = 3
    xt = [K.sb(f"xt{i}", [128, D], F32) for i in range(NXB)]
    xn = [K.sb(f"xn{i}", [128, D], BF16) for i in range(2)]
    junk = K.sb("junk", [128, D], F32)
    ss = K.sb("ss", [128, NT], F32)
    rs = K.sb("rs", [128, NT], F32)
    for j in range(NT):
        b = j % NXB
        src, xkey = xrows(K, l, j)
        K.dma("sp", f"xt{b}", xt[b], src, [xkey], [f"xt{b}"])
        K.act(junk, xt[b], AF.Square, [f"xt{b}"], ["junk"])
        K.S.op("dve", lambda h, j=j: h.tensor_reduce(ss[:, j:j + 1], junk, AX.X, ALU.add), ["junk"], [("ss", j)])
        K.cut("p1a")
        K.rstd(rs[:, j:j + 1], ss[:, j:j + 1], D, [("ss", j)], [("rs", j)])
        K.cut("p1b")
        nb = j % 2
        K.ts("dve", xn[nb], xt[b], rs[:, j:j + 1], None, ALU.mult, None, [f"xt{b}", ("rs", j)], [f"xn{nb}"])
        K.cut("p1c")
        for k in range(8):
            K.tr(K.pT[k // 4][:, (k % 4) * 128:(k % 4 + 1) * 128], xn[nb][:, k * 128:(k + 1) * 128], K.ident,
                 [f"xn{nb}", "ident"], [f"pT{k // 4}"])
        K.cut("p1d")
        jc = 1 if j < 2 else 0
        for k in range(8):
            o = K.hT[:, k, j * 128:(j + 1) * 128]
            i_ = K.pT[k // 4][:, (k % 4) * 128:(k % 4 + 1) * 128]
            if k < 4:
                K.act(o, i_, AF.Identity, [f"pT{k // 4}", "A1", "B1"], [("hT", j, k)], scale=K.A1[:, k, jc:jc + 1],
                      bias=K.B1[:, k, jc:jc + 1])
            else:
                K.ts("dve", o, i_, K.A1[:, k, jc:jc + 1], K.B1[:, k, jc:jc + 1], ALU.mult, ALU.add,
                     [f"pT{k // 4}", "A1", "B1"], [("hT", j, k)])
            if k == 0:
                K.cut("p1k0")
            if k == 1:
                K.cut("p1k1")
        K.cut(f"p1e{j}")
    K.cut("p1f")


def finish(K):
    K.S.barrier()


def build(cfg):
    nc = bass.Bass("TRN2", target_bir_lowering=False)
    with ExitStack() as st:
        K = Ctx(nc, st, cfg)
        declare_dram(K)
        try:
            build_body(K, cfg)
        except StopBuild:
            pass
        finish(K)
        K.S.emit()
    return nc


def build_body(K, cfg):
    if True:
        setup(K)
        K.cut("setup")
        for l in range(cfg.get("layers", DEPTH)):
            phase_norm(K, l)
            if cfg.get("dbg") == "norm":
                K.dma("sp", "dbg", K.d["dbg_hT"], K.hT, [("hT", j, k) for j in range(NT) for k in range(8)], ["dbg"])
                break
            for s in range(K.nsub):
                if "mlstm" not in cfg.get("skip", ()):
                    phase_mlstm(K, l, s)
                if "attn" not in cfg.get("skip", ()):
                    phase_attn(K, l, s)
                if "fft" not in cfg.get("skip", ()):
                    phase_fft(K, l, s)
                if cfg.get("dbg"):
                    K.S.barrier()
                    K.dma("sp", "dbg", K.d["dbg_yT"][s], K.d["yT"], ["yT"], ["dbg"])
                if "merge" not in cfg.get("skip", ()):
                    phase_merge(K, l, s)
            if K.pair and l == cfg.get("layers", DEPTH) - 1:
                K.dma("sp", "fin", K.d["out"], K.d["xcur"][NCTX:NTOK, :], [("xd", j) for j in range(2, NT)], ["outfin"])
        if cfg.get("dbg"):
            K.S.barrier()
            K.dma("sp", "dbg", K.d["dbg_ctx"], K.d["xctx"], [("xd", 0), ("xd", 1)], ["dbg"])


def prep_shared(inp, DEPTH=DEPTH, subsets=(0, 1)):
    f = lambda a: np.ascontiguousarray(np.asarray(a, dtype=np.float32))
    inp = {k: (np.asarray(v)[:DEPTH] if k not in ("x", "c", "ctx", "c_ctx") else v) for k, v in inp.items()}
    subsets = list(subsets)
    sh = {}
    sh["normw"] = f(inp["norm_w"].reshape(DEPTH, 8, 128).transpose(0, 2, 1))
    sh["wada"] = f(inp["w_ada"])
    sh["bada"] = f(inp["b_ada"].reshape(DEPTH, 24, 128).transpose(0, 2, 1))
    sh["win"] = f(inp["w_in"][:, :, win_cols(subsets)])
    sh["bif"] = f(inp["b_if"][:, gate_bias_idx(subsets)])
    fr = feat_rows(subsets)
    sh["anw"] = f(inp["a_norm_w"][:, fr])
    sh["qknw"] = f(np.concatenate([inp["q_norm_w"], inp["q_norm_w"], inp["k_norm_w"], inp["k_norm_w"]], axis=1))
    sh["lam4"] = f(np.concatenate([inp["lambda_q1"], inp["lambda_k1"], inp["lambda_q2"], inp["lambda_k2"]], axis=1))
    sh["subw"] = f(inp["subln_w"])
    sh["wao"] = f(inp["w_a_out"][:, fr]); sh["wbo"] = f(inp["w_b_out"][:, fr]); sh["wco"] = f(inp["w_c_out"][:, fr])
    sh["wo"] = f(inp["w_out"])
    for n, v in make_consts().items():
        assert list(v.shape) == CONST_SHAPES[n], (n, v.shape)
        sh["c_" + n] = f(v)
    return sh


def prep_core(inp, b):
    m = {}
    m["xin"] = np.ascontiguousarray(np.concatenate([inp["ctx"][b], inp["x"][b]], axis=0).astype(np.float32))
    cv = np.stack([np.asarray(inp["c"][b]), np.asarray(inp["c_ctx"])], axis=-1)
    m["cvec"] = np.ascontiguousarray(cv.reshape(8, 128, 2).transpose(1, 0, 2).astype(np.float32))
    return m


_NC_CACHE = {}


PAIR = True


def kernel(**inputs):
    cfg = {"pair": PAIR}
    if "full" not in _NC_CACHE:
        _NC_CACHE["full"] = build(cfg)
    nc = _NC_CACHE["full"]
    in_maps = []
    if PAIR:
        shs = [prep_shared(inputs, DEPTH, (s,)) for s in range(NSUB)]
        cores = [prep_core(inputs, b) for b in range(4)]
        for core in range(8):
            m = dict(shs[core % 2])
            m.update(cores[core // 2])
            in_maps.append(m)
        res = run_bass_kernel_spmd(nc, in_maps, core_ids=list(range(8)))
        out = np.stack([np.asarray(res.results[2 * b]["out"]) for b in range(4)], axis=0)
    else:
        sh = prep_shared(inputs)
        for core in range(8):
            m = dict(sh)
            m.update(prep_core(inputs, core % 4))
            in_maps.append(m)
        res = run_bass_kernel_spmd(nc, in_maps, core_ids=list(range(8)))
        out = np.stack([np.asarray(res.results[b]["out"]) for b in range(4)], axis=0)
    return out.astype(np.float32)


def phase_merge(K, l, s):
    S, d = K.S, K.d
    last = (l == K.cfg.get("nlayers_total", DEPTH) - 1)
    K.reset()
    TG = 256
    wmg = K.sb("wmg", [128, 8, 3 * D], BF16)
    wbr = [K.sb(f"wbr{b}", [128, 4, D], BF16) for b in range(3)]
    wo = K.sb("wo", [128, 8, D], BF16)
    for b in range(3):
        K.dma("pool", f"wmg{b}", wmg[:, :, b * D:(b + 1) * D], wslice(K, l, K.mg0 + b * D, D), (), [("wmg", b)],
              max_dma_last_dim=4096)
        srcw = d[("wao", "wbo", "wco")[b]][l][512 * s:512 * s + 512, :].rearrange("(k p) n -> p k n", p=128)
        K.dma("pool", f"wbr{b}", wbr[b], srcw, (), [("wbr", b)], max_dma_last_dim=4096)
    K.dma("pool", "wo", wo, d["wo"][l].rearrange("(k p) n -> p k n", p=128), (), ["wo"], max_dma_last_dim=4096)
    ybr = [[K.sb(f"ybr{b}_{i}", [128, 4, TG], BF16) for b in range(3)] for i in range(2)]
    th = [K.sb(f"mth{i}", [128, TG], F32) for i in range(2)]
    tmp = [K.sb(f"mtmp{i}", [128, TG], F32) for i in range(2)]
    yacc = K.sb("yacc", [128, 8, TG], F32)
    yTb = K.sb("yTb", [128, 8, TG], BF16)
    xt = [K.sb(f"mxt{i}", [128, D], F32) for i in range(2)]
    tmp2 = K.sb("mtmp2", [128, 512], F32)
    it = 0
    xi = 0
    pend = []
    for tg in range(NTOK // TG):
        if tg == 0 and last:
            continue
        t0 = tg * TG
        jc = 1 if tg == 0 else 0
        yb = ybr[tg % 2]
        for b in range(3):
            K.dma("sp", f"ybr{b}_{tg % 2}", yb[b], d["yT"][b, :, :, t0:t0 + TG].rearrange("c p t -> p c t"),
                  ["yT"], [f"ybr{b}_{tg % 2}"])
        tls = tiles_of(t0, TG)
        for m in range(8):
            for b in range(3):
                pg = K.pf[it % 2]
                pp = K.pf[2 + it % 2]
                for k in range(8):
                    K.mm(pg[:, 0:TG], wmg[:, k, b * D + m * 128:b * D + (m + 1) * 128], K.hT[:, k, t0:t0 + TG],
                         k == 0, k == 7, [("wmg", b)] + hk(tls, k), [f"pf{it % 2}"])
                K.act(th[it % 2], pg[:, 0:TG], AF.Tanh, [f"pf{it % 2}"], [f"mth{it % 2}"], scale=0.5)
                for kc in range(4):
                    K.mm(pp[:, 0:TG], wbr[b][:, kc, m * 128:(m + 1) * 128], yb[b][:, kc, :], kc == 0, kc == 3,
                         [("wbr", b), f"ybr{b}_{tg % 2}"], [f"pf{2 + it % 2}"])
                if b == 0:
                    K.stt(yacc[:, m, :], th[it % 2], 1.0, pp[:, 0:TG], ALU.add, ALU.mult,
                          [f"mth{it % 2}", f"pf{2 + it % 2}"], [("yacc", m)])
                else:
                    K.stt(tmp[it % 2], th[it % 2], 1.0, pp[:, 0:TG], ALU.add, ALU.mult,
                          [f"mth{it % 2}", f"pf{2 + it % 2}"], [f"mtmp{it % 2}"])
                    if b == 1:
                        K.tt("pool", yacc[:, m, :], yacc[:, m, :], tmp[it % 2], ALU.add,
                             [f"mtmp{it % 2}", ("yacc", m)], [("yacc", m)])
                    else:
                        K.tt("pool", yTb[:, m, :], yacc[:, m, :], tmp[it % 2], ALU.add,
                             [f"mtmp{it % 2}", ("yacc", m)], [("yTb", m)])
                it += 1
        for tsub in range(TG // 128):
            j = t0 // 128 + tsub
            xb = xt[xi % 2]
            src, xkey = xrows(K, l if s == 0 else 99, j)
            K.dma("sp", f"mxt{xi % 2}", xb, src, [xkey], [f"mxt{xi % 2}"])
            if K.pair:
                K.ts("pool", xb, xb, 0.5, None, ALU.mult, None, [f"mxt{xi % 2}"], [f"mxt{xi % 2}"])
            for half in range(2):
                po = K.pf[4 + half]
                for m in range(8):
                    K.mm(po[:, 0:512], yTb[:, m, tsub * 128:(tsub + 1) * 128], wo[:, m, half * 512:(half + 1) * 512],
                         m == 0, m == 7, [("yTb", m), "wo"], [f"pf{4 + half}"])
                K.tt("dve", tmp2, po[:, 0:512], K.gateb[:, jc, half * 512:(half + 1) * 512], ALU.mult,
                     [f"pf{4 + half}", "gateb"], ["mtmp2"])
                K.tt("pool", xb[:, half * 512:(half + 1) * 512], xb[:, half * 512:(half + 1) * 512], tmp2, ALU.add,
                     ["mtmp2", f"mxt{xi % 2}"], [f"mxt{xi % 2}"])
            dst, xkey = xrows(K, l, j, write=True)
            K.dma("sp", f"mxt{xi % 2}", dst, xb, [f"mxt{xi % 2}"], [xkey])
            xi += 1
        if K.pair:
            pend.append(tg)
    if K.pair:
        K.S.barrier()
        for tg in pend:
            pair_exchange(K, tg)
        K.S.barrier()


def phase_attn(K, l, s):
    S, d = K.S, K.d
    last = (l == K.cfg.get("nlayers_total", DEPTH) - 1)
    lam_init = 0.8 - 0.6 * math.exp(-0.3 * l)
    K.reset()
    ropec = K.sb("ropec", [128, 32, 64], F32)
    ropes = K.sb("ropes", [128, 32, 64], F32)
    nwb = K.sb("nwb", [128, 256], F32)
    subwb = K.sb("subwb", [128, 128], F32)
    lam4 = K.sb("lam4", [128, 256], F32)
    lsum = K.sb("lsum", [128, 2], F32)
    neglam = K.sb("neglam", [128, 1], F32)
    K.dma("sp", "a0", ropec, d["c_ropec"], (), ["ropec"])
    K.dma("sp", "a1", ropes, d["c_ropes"], (), ["ropes"])
    K.dma("sp", "a2", nwb, d["qknw"][l].partition_broadcast(128), (), ["nwb"])
    K.dma("sp", "a3", subwb, d["subw"][l].partition_broadcast(128), (), ["subwb"])
    K.dma("sp", "a4", lam4, d["lam4"][l].partition_broadcast(128), (), ["lam4"])
    K.ts("dve", nwb[:, 0:128], nwb[:, 0:128], 0.125, None, ALU.mult, None, ["nwb"], ["nwb"])
    K.ts("dve", subwb, subwb, (1.0 - lam_init) * 0.5, None, ALU.mult, None, ["subwb"], ["subwb"])
    lp = K.sb("lp", [128, 2, 64], F32)
    l4 = lam4.rearrange("p (a b c) -> p a b c", a=2, b=2)
    K.tt("dve", lp, l4[:, :, 0, :], l4[:, :, 1, :], ALU.mult, ["lam4"], ["lp"])
    K.S.op("dve", lambda h: h.tensor_reduce(lsum, lp, AX.X, ALU.add), ["lp"], ["lsum"])
    K.act(lsum, lsum, AF.Exp, ["lsum"], ["lsum"])
    K.stt(neglam, lsum[:, 1:2], -lam_init, lsum[:, 0:1], ALU.add, ALU.subtract, ["lsum"], ["neglam"])
    wB = K.sb("wB", [128, 8, 512], BF16)
    qT = K.sb("aqT", [128, NTOK], BF16)
    kT = K.sb("akT", [128, NTOK], BF16)
    vaug = K.sb("avaug", [128, NT, 129], BF16)
    K.memset("pool", vaug[:, :, 128:129], 1.0, T("avaug", range(NT)))
    qk32 = K.sb("qk32", [128, 256], F32)
    sq = K.sb("asq", [128, 256], F32)
    ss4 = K.sb("ss4", [128, 4], F32)
    rs4 = K.sb("rs4", [128, 4], F32)
    xn = K.sb("axn", [128, 256], F32)
    t1 = K.sb("at1", [128, 256], F32)
    t2 = K.sb("at2", [128, 256], F32)
    qkr = [K.sb(f"qkr{i}", [128, 256], BF16) for i in range(2)]
    NE = 4
    ET = [K.sb(f"ET{i}", [128, 512], BF16) for i in range(NE)]
    rr = K.sb("arr", [128, 2], F32)
    tt_ = K.sb("att", [128, 128], F32)
    o32 = K.sb("ao32", [128, 128], F32)
    junk = K.sb("ajunk", [128, 128], F32)
    ss1 = K.sb("ass1", [128, 1], F32)
    rs1 = K.sb("ars1", [128, 1], F32)
    thz = K.sb("athz", [128, 128], F32)
    wz = K.sb("awz", [128, 128], F32)
    y1 = K.sb("ay1", [128, 128], F32)
    y2 = K.sb("ay2", [128, 128], BF16)
    ybT = [K.sb(f"aybT{i}", [128, 512], BF16) for i in range(2)]
    ei = 0
    yi = 0
    for i in range(NB):
        c0 = s * SUBW + 2056 + i * 512
        K.dma("pool", "wB", wB, wslice(K, l, c0, 512), (), ["wB"], max_dma_last_dim=2048)
        for j in range(NT):
            pqk = K.pf[5]
            for k in range(8):
                K.mm(pqk[:, 0:256], K.hT[:, k, j * 128:(j + 1) * 128], wB[:, k, 0:256], k == 0, k == 7,
                     ["wB"] + hk([j], k), ["pf5"])
            for k in range(8):
                K.mm(pqk[:, 256:384], K.hT[:, k, j * 128:(j + 1) * 128], wB[:, k, 256:384], k == 0, k == 7,
                     ["wB"] + hk([j], k), ["pf5"])
            K.cp("dve", vaug[:, j, 0:128], pqk[:, 256:384], ["pf5"], [("avaug", j)])
            K.cp("dve", qk32, pqk[:, 0:256], ["pf5"], ["qk32"])
            K.act(sq, qk32, AF.Square, ["qk32"], ["asq"])
            K.S.op("dve", lambda h: h.tensor_reduce(ss4, sq.rearrange("p (a b) -> p a b", a=4), AX.X, ALU.add),
                   ["asq"], ["ss4"])
            K.rstd(rs4, ss4, 64, ["ss4"], ["rs4"])
            K.tt("dve", xn.rearrange("p (a b) -> p a b", a=4), qk32.rearrange("p (a b) -> p a b", a=4),
                 rs4.unsqueeze(2).to_broadcast([128, 4, 64]), ALU.mult, ["qk32", "rs4"], ["axn"])
            qb = qkr[j % 2]
            if j >= 2:
                jl = j - 2
                K.tt("pool", xn, xn, nwb, ALU.mult, ["axn", "nwb"], ["axn"])
                K.tt("dve", t1.rearrange("p (a b) -> p a b", a=4), xn.rearrange("p (a b) -> p a b", a=4),
                     ropec[:, jl, :].unsqueeze(1).to_broadcast([128, 4, 64]), ALU.mult, ["axn", "ropec"], ["at1"])
                x5 = xn.rearrange("p (a h i) -> p a h i", h=2, i=16)
                t5 = t2.rearrange("p (a h i) -> p a h i", h=2, i=16)
                s5 = ropes[:, jl, :].rearrange("p (r h i) -> p r h i", h=2, i=16)
                for hh in range(2):
                    sin_b = s5[:, :, hh, :].unsqueeze(1).to_broadcast([128, 4, 2, 16])
                    K.tt("pool", t5[:, :, hh, :].rearrange("p (s r) i -> p s r i", r=2),
                         x5[:, :, 1 - hh, :].rearrange("p (s r) i -> p s r i", r=2), sin_b, ALU.mult,
                         ["axn", "ropes"], ["at2"])
                K.tt("dve", qb, t1, t2, ALU.add, ["at1", "at2"], [f"qkr{j % 2}"])
            else:
                K.tt("dve", qb, xn, nwb, ALU.mult, ["axn", "nwb"], [f"qkr{j % 2}"])
            pt = K.pT[j % 2]
            for hf in range(2):
                K.tr(pt[:, hf * 128:(hf + 1) * 128], qb[:, hf * 128:(hf + 1) * 128], K.ident,
                     [f"qkr{j % 2}", "ident"], [f"pT{j % 2}"])
            if j % 2 == 0:
                K.cp("dve", qT[:, j * 128:(j + 1) * 128], pt[:, 0:128], [f"pT{j % 2}"], [("aqT", j)])
                K.cp("dve", kT[:, j * 128:(j + 1) * 128], pt[:, 128:256], [f"pT{j % 2}"], [("akT", j)])
            else:
                K.cp("act", qT[:, j * 128:(j + 1) * 128], pt[:, 0:128], [f"pT{j % 2}"], [("aqT", j)])
                K.cp("act", kT[:, j * 128:(j + 1) * 128], pt[:, 128:256], [f"pT{j % 2}"], [("akT", j)])
        groups = [(256 + 512 * g, 4, list(range(NT))) for g in range(8)]
        if not last:
            groups.append((0, 2, [0, 1]))
        for (q0, nq, kcs) in groups:
            nqc = nq * 128
            qtl = list(tiles_of(q0, nqc))

            started = set()

            def acc(m, qs):
                a = m * 4 + qs
                first = (a // 3) not in started
                return K.pf[2 + a // 3][:, (a % 3) * 129:(a % 3) * 129 + 129], f"pf{2 + a // 3}", first
            for ki, kc in enumerate(kcs):
                for m in range(2):
                    pS = K.pf[m]
                    K.mm(pS[:, 0:nqc], kT[64 * m:64 * m + 64, kc * 128:(kc + 1) * 128],
                         qT[64 * m:64 * m + 64, q0:q0 + nqc], True, True,
                         [("akT", kc)] + T("aqT", qtl), [f"pf{m}"])
                    e = ET[ei % NE]
                    K.act(e[:, 0:nqc], pS[:, 0:nqc], AF.Exp, [f"pf{m}"], [f"ET{ei % NE}"])
                    for qs in range(nq):
                        ap_, key, first = acc(m, qs)
                        if ki == 0:
                            started.add((m * 4 + qs) // 3)
                        K.mm(ap_, e[:, qs * 128:(qs + 1) * 128], vaug[:, kc, :], (ki == 0 and first),
                             ki == len(kcs) - 1, [f"ET{ei % NE}", ("avaug", kc)], [(key, m, qs)],
                             skip_group_check=True)
                    ei += 1
            yb = ybT[yi % 2]
            allacc = [(acc(m_, q_)[1], m_, q_) for m_ in range(2) for q_ in range(nq)]
            for qs in range(nq):
                a0, k0, _ = acc(0, qs)
                a1, k1, _ = acc(1, qs)
                jq = q0 // 128 + qs
                K.S.op("dve", lambda h, a0=a0: h.reciprocal(rr[:, 0:1], a0[:, 128:129]), allacc, ["arr"])
                K.S.op("dve", lambda h, a1=a1: h.reciprocal(rr[:, 1:2], a1[:, 128:129]), [(k1, 1, qs)], ["arr"])
                K.tt("dve", rr[:, 1:2], rr[:, 1:2], neglam, ALU.mult, ["arr", "neglam"], ["arr"])
                K.ts("dve", tt_, a1[:, 0:128], rr[:, 1:2], None, ALU.mult, None, [(k1, 1, qs), "arr"], ["att"])
                K.stt(o32, a0[:, 0:128], rr[:, 0:1], tt_, ALU.mult, ALU.add, [(k0, 0, qs), "arr", "att"], ["ao32"])
                K.act(junk, o32, AF.Square, ["ao32"], ["ajunk"])
                K.S.op("dve", lambda h: h.tensor_reduce(ss1, junk, AX.X, ALU.add), ["ajunk"], ["ass1"])
                K.rstd(rs1, ss1, 128, ["ass1"], ["ars1"])
                pz = K.pf[5]
                for k in range(8):
                    K.mm(pz[:, 0:128], K.hT[:, k, jq * 128:(jq + 1) * 128], wB[:, k, 384:512], k == 0, k == 7,
                         ["wB"] + hk([jq], k), ["pf5"])
                K.act(thz, pz[:, 0:128], AF.Tanh, ["pf5"], ["athz"], scale=0.5)
                K.stt(wz, thz, 1.0, pz[:, 0:128], ALU.add, ALU.mult, ["athz", "pf5"], ["awz"])
                K.stt(y1, o32, rs1, subwb, ALU.mult, ALU.mult, ["ao32", "ars1", "subwb"], ["ay1"])
                K.tt("dve", y2, y1, wz, ALU.mult, ["ay1", "awz"], ["ay2"])
                pt = K.pT[qs % 2]
                K.tr(pt[:, 0:128], y2, K.ident, ["ay2", "ident"], [f"pT{qs % 2}"])
                K.cp("dve", yb[:, qs * 128:(qs + 1) * 128], pt[:, 0:128], [f"pT{qs % 2}"], [f"aybT{yi % 2}"])
            K.dma("sp", f"aybT{yi % 2}", d["yT"][1, i, :, q0:q0 + nqc], yb[:, 0:nqc], [f"aybT{yi % 2}"],
                  [("yT", 1, i, q0)])
            yi += 1


def phase_fft(K, l, s):
    S, d = K.S, K.d
    K.reset()
    f256 = K.sb("f256", [128, 2, 512], BF16)
    rb1 = K.sb("rb1", [64, 128], BF16)
    rb2 = K.sb("rb2", [64, 128], BF16)
    gc2 = K.sb("gc2", [128, 64, 64], BF16)
    gs2 = K.sb("gs2", [128, 64, 64], BF16)
    cc = K.sb("fcc", [128, 2, 256], BF16)
    sc = K.sb("fsc", [128, 2, 256], BF16)
    for nm, t_ in (("f256", f256), ("rb1", rb1), ("rb2", rb2), ("gc2", gc2), ("gs2", gs2), ("cc", cc), ("sc", sc)):
        K.dma("pool", "fc_" + nm, t_, d["c_" + nm], (), ["fc_" + nm], max_dma_last_dim=4096)
    fck = ["fc_f256", "fc_rb1", "fc_rb2", "fc_gc2", "fc_gs2", "fc_cc", "fc_sc"]
    wC = K.sb("wC", [128, 8, 512], BF16)
    cuT = K.sb("cuT", [128, 2, NTOK], BF16)
    gz = K.sb("fgz", [128, NTOK], BF16)
    thz = K.sb("fthz", [128, 512], F32)
    Zt = K.sb("Zt", [64, 2, 64, 2, 64], BF16)
    AT = K.sb("AT", [128, 64, 2, 64], BF16)
    Zc = K.sb("Zc", [128, 2, 2, 128], BF16)
    ycT = K.sb("ycT", [128, NTOK], BF16)
    TGS = [(t0, min(512, NTOK - t0)) for t0 in range(0, NTOK, 512)]
    for i in range(NG):
        c0 = s * SUBW + 4104 + i * 512
        K.dma("pool", "wC", wC, wslice(K, l, c0, 512), (), ["wC"], max_dma_last_dim=2048)
        for dc in range(2):
            for gi, (t0, n) in enumerate(TGS):
                pc = K.pf[gi % 2]
                for k in range(8):
                    K.mm(pc[:, 0:n], wC[:, k, dc * 128:(dc + 1) * 128], K.hT[:, k, t0:t0 + n], k == 0, k == 7,
                         ["wC"] + hk(tiles_of(t0, n), k), [f"pf{gi % 2}"])
                if gi % 2 == 0:
                    K.cp("act", cuT[:, dc, t0:t0 + n], pc[:, 0:n], [f"pf{gi % 2}"], [("cuT", dc)])
                else:
                    K.cp("dve", cuT[:, dc, t0:t0 + n], pc[:, 0:n], [f"pf{gi % 2}"], [("cuT", dc)])
        for e in range(2):
            for gi, (t0, n) in enumerate(TGS):
                pc = K.pf[gi % 2]
                for k in range(8):
                    K.mm(pc[:, 0:n], wC[:, k, 256 + e * 128:256 + (e + 1) * 128], K.hT[:, k, t0:t0 + n], k == 0,
                         k == 7, ["wC"] + hk(tiles_of(t0, n), k), [f"pf{gi % 2}"])
                K.act(thz[:, 0:n], pc[:, 0:n], AF.Tanh, [f"pf{gi % 2}"], ["fthz"], scale=0.5)
                K.stt(gz[:, t0:t0 + n], thz[:, 0:n], 1.0, pc[:, 0:n], ALU.add, ALU.mult, ["fthz", f"pf{gi % 2}"],
                      ["fgz"])
            rcols = f256[:, :, :].rearrange("p c (r x) -> p c r x", r=2)[:, :, :, e * 128:(e + 1) * 128]
            for jc in range(2):
                pa = K.pf[2]
                for dc in range(2):
                    K.mm(pa[:, 0:256].rearrange("p (r x) -> p r x", r=2), cuT[:, dc, jc * 128:(jc + 1) * 128],
                         rcols[:, dc], dc == 0, dc == 1, [("cuT", 0), ("cuT", 1), "fc_f256"], ["pf2"])
                K.cp("dve", Zc[:, jc].rearrange("p r x -> p (r x)"), pa[:, 0:256], ["pf2"], ["Zc"])
            pcx = K.pf[3]
            n_mm = 0
            for jc in range(2):
                for r, tab in ((0, cc), (1, sc)):
                    K.mm(pcx[:, 0:256], Zc[:, jc, r, :], tab[:, jc, :], n_mm == 0, n_mm == 3,
                         ["Zc", "fc_cc", "fc_sc"], ["pf3"])
                    n_mm += 1
            K.tt("dve", ycT[:, 0:256], pcx[:, 0:256], gz[:, 0:256], ALU.mult, ["pf3", "fgz"], ["ycT"])
            for np_ in range(32):
                pa = K.pf[2 + np_ % 2]
                for q in range(2):
                    n2 = 2 * np_ + q
                    for dc in range(2):
                        lat = cuT[:, dc, NCTX:NTOK].rearrange("p (a b) -> p a b", b=64)[:, :, n2]
                        K.mm(pa[0:64, q * 256:(q + 1) * 256].rearrange("p (r x) -> p r x", r=2), lat, rcols[:, dc],
                             dc == 0, dc == 1, [("cuT", 0), ("cuT", 1), "fc_f256"], [f"pf{2 + np_ % 2}"])
                src_ = pa[0:64, 0:512].rearrange("p (q r m x) -> p q r m x", q=2, r=2, m=2)
                for m_ in range(2):
                    dst_ = Zt[:, :, :, m_, 2 * np_:2 * np_ + 2].rearrange("p r x q -> p q r x")
                    if np_ % 2 == 0:
                        K.cp("dve", dst_, src_[:, :, :, m_, :], [f"pf{2 + np_ % 2}"], ["Zt"])
                    else:
                        K.cp("act", dst_, src_[:, :, :, m_, :], [f"pf{2 + np_ % 2}"], ["Zt"])
            for ib in range(16):
                pb_ = K.pf[4 + ib % 2]
                for q in range(4):
                    ii = 4 * ib + q
                    for r, rb in ((0, rb1), (1, rb2)):
                        lhs = Zt[:, r, ii, :, :].rearrange("p m n -> p (m n)")
                        K.mm(pb_[:, q * 128:(q + 1) * 128], lhs, rb, r == 0, r == 1, ["Zt", "fc_rb1", "fc_rb2"],
                             [f"pf{4 + ib % 2}"])
                src_ = pb_[:, 0:512].rearrange("p (q r k) -> p q r k", q=4, r=2)
                dst_ = AT[:, :, :, 4 * ib:4 * ib + 4].rearrange("p k r q -> p q r k")
                if ib % 2 == 0:
                    K.cp("dve", dst_, src_, [f"pf{4 + ib % 2}"], ["AT"])
                else:
                    K.cp("act", dst_, src_, [f"pf{4 + ib % 2}"], ["AT"])
            for kb in range(8):
                pc = K.pf[kb % 2]
                for q in range(8):
                    k1 = 8 * kb + q
                    for m in range(2):
                        rows = slice(64 * m, 64 * m + 64)
                        for r, tab in ((0, gc2), (1, gs2)):
                            K.mm(pc[rows, q * 64:(q + 1) * 64], AT[rows, k1, r, :], tab[rows, k1, :], r == 0, r == 1,
                                 ["AT", "fc_gc2", "fc_gs2"], [f"pf{kb % 2}"], skip_group_check=True)
                lat_y = ycT[:, NCTX:NTOK].rearrange("p (k2 k1) -> p k1 k2", k1=64)[:, 8 * kb:8 * kb + 8, :]
                lat_g = gz[:, NCTX:NTOK].rearrange("p (k2 k1) -> p k1 k2", k1=64)[:, 8 * kb:8 * kb + 8, :]
                K.tt("dve", lat_y, pc[:, 0:512].rearrange("p (q k) -> p q k", q=8), lat_g, ALU.mult,
                     [f"pf{kb % 2}", "fgz"], ["ycT"])
            K.dma("sp", "ycT", d["yT"][2, 2 * i + e, :, :], ycT, ["ycT"], [("yT", 2, i, e)])


ORD = (list(range(NCH)), [3, 2, 1, 0] + list(range(NCH - 1, 3, -1)))


def phase_mlstm(K, l, s):
    S, d = K.S, K.d
    K.reset()
    cf = {}
    for nm, shp in (("maskf", [128, 64]), ("maskb", [128, 64]), ("trif", [128, 128]), ("trib", [128, 128]),
                    ("chsel", [128, 2]), ("sel", [NA, NA, 128]), ("dlt", [NA, NT, NA]), ("ones2", [NA, 64])):
        cf[nm] = K.sb("m_" + nm, shp, F32)
        K.dma("sp", "mc_" + nm, cf[nm], d["c_" + nm], (), ["m_" + nm])
    wg = K.sb("wg", [128, 8, 4 * NA], BF16)
    bifb = K.sb("bifb", [128, 4 * NA], F32)
    anwb = K.sb("anwb", [128, 512], F32)
    K.dma("pool", "wg", wg, wslice(K, l, s * SUBW + 2048, 4 * NA), (), ["wg"])
    K.dma("sp", "bifb", bifb, d["bif"][l, s * 4 * NA:(s + 1) * 4 * NA].partition_broadcast(128), (), ["bifb"])
    K.dma("sp", "anwb", anwb, d["anw"][l, 512 * s:512 * s + 512].partition_broadcast(128), (), ["anwb"])
    K.ts("dve", anwb, anwb, 0.25, None, ALU.mult, None, ["anwb"], ["anwb"])
    G = 4 * NA
    Gt = K.sb("Gt", [128, NT, G], F32)
    pgt = K.pf[0][:, 0:NT * G]
    for j in range(NT):
        for k in range(8):
            K.mm(pgt[:, j * G:(j + 1) * G], K.hT[:, k, j * 128:(j + 1) * 128], wg[:, k, :], k == 0, k == 7,
                 ["wg"] + hk([j], k), ["pf0"])
    K.tt("dve", Gt, pgt.rearrange("p (j g) -> p j g", g=G), bifb.unsqueeze(1).to_broadcast([128, NT, G]), ALU.add,
         ["pf0", "bifb"], ["Gt"])
    WK = K.sb("WK", [128, 2, NT, NA], F32)
    EC = K.sb("EC", [128, 2, NT, NA], F32)
    decB = K.sb("decB", [128, 2, NA, NCH], F32)
    sh3 = [128, NT, NA]
    gA = K.sb("gA", sh3, F32); gE = K.sb("gE", sh3, F32); gL = K.sb("gL", sh3, F32); Fg = K.sb("Fg", sh3, F32)
    Bs = K.sb("Bs", sh3, F32); U = K.sb("U", sh3, F32); g1 = K.sb("g1", sh3, F32); g2 = K.sb("g2", sh3, F32)
    blastF = K.sb("blastF", [NA, NCH], F32); umaxF = K.sb("umaxF", [NA, NCH], F32); mst = K.sb("mst", [NA, NCH], F32)
    Mc = K.sb("Mc", [NA, NCH], F32); dd = K.sb("dd", [NA, NCH], F32); dec = K.sb("dec", [NA, NCH], F32)
    Rp = [K.sb(f"Rp{i}", [NA, NT, NA], F32) for i in range(2)]
    fl = lambda t_: t_.rearrange("p j h -> p (j h)")
    for D_ in range(2):
        PI = Gt[:, :, 2 * D_ * NA:2 * D_ * NA + NA]
        PF = Gt[:, :, (2 * D_ + 1) * NA:(2 * D_ + 2) * NA]
        tri = cf["trif" if D_ == 0 else "trib"]
        K.stt(gA, PF, -1.0, PF, ALU.mult, ALU.max, ["Gt"], ["gA"])
        K.act(gE, gA, AF.Exp, ["gA"], ["gE"], scale=-1.0)
        K.act(gL, gE, AF.Ln, ["gE"], ["gL"], bias=1.0)
        K.stt(Fg, PF, 0.0, gL, ALU.min, ALU.subtract, ["Gt", "gL"], ["Fg"])
        K.mm(K.pf[1][:, 0:NT * NA], tri, fl(Fg), True, True, ["Fg", "m_trif", "m_trib"], ["pf1"])
        K.cp("dve", fl(Bs), K.pf[1][:, 0:NT * NA], ["pf1"], ["Bs"])
        K.tt("dve", U, PI, Bs, ALU.subtract, ["Gt", "Bs"], ["U"])
        for j in range(NT):
            K.mm(K.pf[2][0:NA, 2 * j:2 * j + 2], Fg[:, j, :], cf["chsel"], True, True, ["Fg", "m_chsel"], ["pf2"])
        K.cp("dve", blastF, K.pf[2][0:NA, 0:NCH], ["pf2"], ["blastF"])
        for jb in range(0, NT, 4):
            nb_ = min(4, NT - jb)
            pu = K.pf[3 + (jb // 4) % 2]
            for q in range(nb_):
                K.tr(pu[0:NA, q * 128:(q + 1) * 128], U[:, jb + q, :], K.identf, ["U", "identf"],
                     [f"pf{3 + (jb // 4) % 2}"])
            K.S.op("dve", lambda h, pu=pu, jb=jb, nb_=nb_: h.tensor_reduce(
                umaxF[:, 2 * jb:2 * jb + 2 * nb_], pu[0:NA, 0:nb_ * 128].rearrange("p (c t) -> p c t", t=64),
                AX.X, ALU.max), [f"pf{3 + (jb // 4) % 2}"], ["umaxF"])
        od = ORD[D_]
        K.memset("dve", mst[:, od[0]:od[0] + 1], 0.0, ["mst"])
        for ix in range(NCH - 1):
            c, nx = od[ix], od[ix + 1]
            K.ts("dve", mst[:, nx:nx + 1], mst[:, c:c + 1], umaxF[:, c:c + 1], blastF[:, c:c + 1], ALU.max, ALU.add,
                 ["mst", "umaxF", "blastF"], ["mst"])
        K.tt("dve", Mc, mst, umaxF, ALU.max, ["mst", "umaxF"], ["Mc"])
        K.tt("dve", dd, mst, Mc, ALU.subtract, ["mst", "Mc"], ["dd"])
        K.act(dec, dd, AF.Exp, ["dd"], ["dec"])
        for h_ in range(NA):
            K.mm(K.pf[5][:, h_ * NCH:(h_ + 1) * NCH], cf["sel"][:, h_, :], dec, True, True, ["dec", "m_sel"], ["pf5"])
        K.cp("dve", decB[:, D_].rearrange("p h c -> p (h c)"), K.pf[5][:, 0:NA * NCH], ["pf5"], [("decB", D_)])
        for pi in range(2):
            K.tt("dve", Rp[pi], cf["dlt"], Mc[:, pi::2].unsqueeze(2).to_broadcast([NA, NT, NA]), ALU.mult,
                 ["Mc", "m_dlt"], [f"Rp{pi}"])
            K.mm(K.pf[1][64 * pi:64 * pi + 64, 0:NT * NA], cf["ones2"], fl(Rp[pi]), True, True,
                 [f"Rp{pi}", "m_ones2"], ["pf1"])
        mt = K.pf[1][:, 0:NT * NA]
        K.tt("dve", fl(g1), fl(U), mt, ALU.subtract, ["U", "pf1"], ["g1"])
        K.act(fl(WK[:, D_]), fl(g1), AF.Exp, ["g1"], [("WK", D_)])
        K.stt(fl(g2), fl(Bs), -1.0, mt, ALU.mult, ALU.subtract, ["Bs", "pf1"], ["g2"])
        K.act(fl(EC[:, D_]), fl(g2), AF.Exp, ["g2"], [("EC", D_)])
    wA = K.sb("wA", [128, 8, 1024], BF16)
    qT = K.sb("mqT", [128, NTOK], BF16)
    kT = K.sb("mkT", [128, NTOK], BF16)
    vaug = K.sb("mvaug", [128, NT, 257], BF16)
    Kp = [K.sb(f"Kp{i}", [128, NT, 128], BF16) for i in range(2)]
    hacc = K.sb("hacc", [128, NT, 256], F32)
    Cst = K.sb("Cst", [128, 257], F32)
    Cdb = [K.sb(f"Cdb{i}", [128, 257], BF16) for i in range(2)]
    aT = [K.sb(f"maT{i}", [128, 64], BF16) for i in range(2)]
    dm = K.sb("mdm", [128, 2], F32)
    tho = K.sb("tho", [128, 512], F32)
    w1 = K.sb("mw1", [128, 256], F32)
    w2 = K.sb("mw2", [128, 256], F32)
    y1 = K.sb("my1", [128, 256], F32)
    y2 = K.sb("my2", [128, 256], BF16)
    junk = K.sb("mjunk", [128, 256], F32)
    ssA = K.sb("ssA", [128, NT], F32)
    rsA = K.sb("rsA", [128, NT], F32)
    ybuf = [K.sb(f"mybuf{i}", [128, 2, 128], BF16) for i in range(2)]
    K.memset("pool", vaug[:, :, 256:257], 1.0, T("mvaug", range(NT)))
    TGS = [(t0, min(512, NTOK - t0)) for t0 in range(0, NTOK, 512)]
    mask = (cf["maskf"], cf["maskb"])
    for i in range(NA):
        K.dma("pool", "wA", wA, wslice(K, l, s * SUBW + i * 1024, 1024), (), ["wA"], max_dma_last_dim=4096)
        for gi, (t0, n) in enumerate(TGS):
            tl = tiles_of(t0, n)
            for k in range(8):
                K.mm(K.pf[0][:, 0:n], wA[:, k, 0:128], K.hT[:, k, t0:t0 + n], k == 0, k == 7, ["wA"] + hk(tl, k),
                     ["pf0"])
            K.cp("act", qT[:, t0:t0 + n], K.pf[0][:, 0:n], ["pf0"], T("mqT", tl))
            for k in range(8):
                K.mm(K.pf[1][:, 0:n], wA[:, k, 128:256], K.hT[:, k, t0:t0 + n], k == 0, k == 7, ["wA"] + hk(tl, k),
                     ["pf1"])
            K.ts("dve", kT[:, t0:t0 + n], K.pf[1][:, 0:n], 128 ** -0.5, None, ALU.mult, None, ["pf1"], T("mkT", tl))
        for j in range(NT):
            pkv = K.pf[2 + j % 2]
            for k in range(8):
                K.mm(pkv[:, 0:384], K.hT[:, k, j * 128:(j + 1) * 128], wA[:, k, 128:512], k == 0, k == 7,
                     ["wA"] + hk([j], k), [f"pf{2 + j % 2}"])
            for D_ in range(2):
                K.ts("dve", Kp[D_][:, j, :], pkv[:, 0:128], WK[:, D_, j, i:i + 1], 128 ** -0.5, ALU.mult, ALU.mult,
                     [f"pf{2 + j % 2}", ("WK", D_)], [("Kp", D_, j)])
            K.cp("dve", vaug[:, j, 0:256], pkv[:, 128:384], [f"pf{2 + j % 2}"], [("mvaug", j)])
        it = 0
        for D_ in range(2):
            K.memset("pool", Cst, 0.0, ["Cst"])
            for c in ORD[D_]:
                j, pi = divmod(c, 2)
                rows = slice(64 * pi, 64 * pi + 64)
                t0 = 64 * c
                b2 = it % 2
                dsc = decB[:, D_, i, c:c + 1]
                K.ts("pool", Cdb[b2], Cst, dsc, None, ALU.mult, None, ["Cst", ("decB", D_)], [f"Cdb{b2}"])
                pqk = K.pf[b2]
                K.mm(pqk[rows, 0:64], kT[:, t0:t0 + 64], qT[:, t0:t0 + 64], True, True, [("mkT", j), ("mqT", j)],
                     [f"pf{b2}"])
                K.stt(aT[b2][rows, :], pqk[rows, 0:64], WK[rows, D_, j, i:i + 1], mask[D_][rows, :], ALU.mult,
                      ALU.mult, [f"pf{b2}", ("WK", D_), "m_maskf", "m_maskb"], [f"maT{b2}"])
                pnum = K.pf[2 + b2]
                K.mm(pnum[rows, 0:257], aT[b2][rows, :], vaug[rows, j, :], True, False, [f"maT{b2}", ("mvaug", j)],
                     [f"pf{2 + b2}"])
                K.mm(pnum[rows, 0:257], qT[:, t0:t0 + 64], Cdb[b2], False, True, [("mqT", j), f"Cdb{b2}"],
                     [f"pf{2 + b2}"])
                pdc = K.pf[4 + b2]
                K.mm(pdc[:, 0:257], Kp[D_][rows, j, :], vaug[rows, j, :], True, True, [("Kp", D_, j), ("mvaug", j)],
                     [f"pf{4 + b2}"])
                K.stt(Cst, Cst, dsc, pdc[:, 0:257], ALU.mult, ALU.add, ["Cst", ("decB", D_), f"pf{4 + b2}"], ["Cst"])
                K.ts("dve", dm[rows, 0:1], pnum[rows, 256:257], EC[rows, D_, j, i:i + 1], None, ALU.max, None,
                     [f"pf{2 + b2}", ("EC", D_)], ["mdm"])
                K.stt(dm[rows, 0:1], pnum[rows, 256:257], -1.0, dm[rows, 0:1], ALU.mult, ALU.max,
                      [f"pf{2 + b2}", "mdm"], ["mdm"])
                K.S.op("dve", lambda h, rows=rows: h.reciprocal(dm[rows, 1:2], dm[rows, 0:1]), ["mdm"], ["mdm"])
                if D_ == 0:
                    K.act(hacc[rows, j, :], pnum[rows, 0:256], AF.Copy, [f"pf{2 + b2}", "mdm"], [("hacc", j)],
                          scale=dm[rows, 1:2])
                else:
                    K.stt(hacc[rows, j, :], pnum[rows, 0:256], dm[rows, 1:2], hacc[rows, j, :], ALU.mult, ALU.add,
                          [f"pf{2 + b2}", "mdm", ("hacc", j)], [("hacc", j)])
                it += 1
        for j in range(NT):
            K.act(junk, hacc[:, j, :], AF.Square, [("hacc", j)], ["mjunk"])
            K.S.op("dve", lambda h, j=j: h.tensor_reduce(ssA[:, j:j + 1], junk, AX.X, ALU.add), ["mjunk"], [("ssA", j)])
            K.rstd(rsA[:, j:j + 1], ssA[:, j:j + 1], 256, [("ssA", j)], [("rsA", j)])
            poz = K.pf[j % 2]
            for k in range(8):
                K.mm(poz[:, 0:512], K.hT[:, k, j * 128:(j + 1) * 128], wA[:, k, 512:1024], k == 0, k == 7,
                     ["wA"] + hk([j], k), [f"pf{j % 2}"])
            K.act(tho, poz[:, 0:512], AF.Tanh, [f"pf{j % 2}"], ["tho"], scale=0.5)
            K.stt(w1, tho[:, 256:512], 1.0, poz[:, 256:512], ALU.add, ALU.mult, ["tho", f"pf{j % 2}"], ["mw1"])
            K.stt(w2, tho[:, 0:256], 1.0, w1, ALU.add, ALU.mult, ["tho", "mw1"], ["mw2"])
            K.stt(y1, hacc[:, j, :], rsA[:, j:j + 1], anwb[:, i * 256:(i + 1) * 256], ALU.mult, ALU.mult,
                  [("hacc", j), ("rsA", j), "anwb"], ["my1"])
            K.tt("dve", y2, y1, w2, ALU.mult, ["my1", "mw2"], ["my2"])
            pt = K.pT[j % 2]
            for c2 in range(2):
                K.tr(pt[:, c2 * 128:(c2 + 1) * 128], y2[:, c2 * 128:(c2 + 1) * 128], K.ident, ["my2", "ident"],
                     [f"pT{j % 2}"])
            yb = ybuf[j % 2]
            K.cp("dve", yb.rearrange("p c t -> p (c t)"), pt[:, 0:256], [f"pT{j % 2}"], [f"mybuf{j % 2}"])
            K.dma("sp", f"mybuf{j % 2}", d["yT"][0, 2 * i:2 * i + 2, :, j * 128:(j + 1) * 128].rearrange("c p t -> p c t"),
                  yb, [f"mybuf{j % 2}"], [("yT", 0, i, j)])


def pair_exchange(K, tg):
    d = K.d
    groups = [[0, 1], [2, 3], [4, 5], [6, 7]]
    r0 = tg * 256
    rk = [("xp", 2 * tg), ("xp", 2 * tg + 1)]
    wk = [("xd", 2 * tg), ("xd", 2 * tg + 1)]
    K.S.dma("pool", "cc", lambda h: h.collective_compute("AllReduce", ALU.add, replica_groups=groups,
                                                         ins=[d["xpart"][r0:r0 + 256, :].opt()],
                                                         outs=[d["xcur"][r0:r0 + 256, :].opt()]),
            rk, wk, inc=1)
```

```python
import math
from contextlib import ExitStack

import numpy as np
import concourse.bass as bass
import concourse.mybir as mybir
from concourse.bass_utils import run_bass_kernel_spmd

F32 = mybir.dt.float32
BF16 = mybir.dt.bfloat16
AF = mybir.ActivationFunctionType
ALU = mybir.AluOpType
AX = mybir.AxisListType

D = 1024
NCTX = 256
NLAT = 4096
NTOK = NCTX + NLAT
NT = NTOK // 128
NCH = NTOK // 64
DEPTH = 4
EPS = 1e-6
NSUB = 2
NA = 2
NB = 4
NG = 2
SUBW = 5128
SB_BASE = 16640
SB_END = 229376


class Slot:
    def __init__(self, S, name):
        self.sem = S.new_sem(name)
        self.total = 0


class Sched:
    ENG = ("pe", "act", "dve", "pool", "sp")

    def __init__(self, nc, stack):
        self.nc = nc
        self.stack = stack
        self.ops = {e: [] for e in self.ENG}
        self.sem = {e: self.new_sem("s_" + e) for e in self.ENG}
        self.cnt = {e: 0 for e in self.ENG}
        self.waited = {}
        self.lastw = {}
        self.reads = {}
        self.slots = {}
        self.same_engine_sync = True

    def new_sem(self, name):
        return self.stack.enter_context(self.nc.semaphore(name))

    def slot(self, name):
        if name not in self.slots:
            self.slots[name] = Slot(self, "d_" + name)
        return self.slots[name]

    def _need(self, eng, toks, tok):
        if tok is None:
            return
        sem, val, src = tok
        if src == eng and (eng == "pe" or not self.same_engine_sync):
            return
        if self.waited.get((eng, id(sem)), 0) >= val:
            return
        cur = toks.get(id(sem))
        if cur is None or cur[1] < val:
            toks[id(sem)] = (sem, val)

    def _deps(self, eng, reads, writes):
        toks = {}
        for k in reads:
            self._need(eng, toks, self.lastw.get(k))
        for k in writes:
            self._need(eng, toks, self.lastw.get(k))
            for t in self.reads.get(k, ()):
                self._need(eng, toks, t)
        waits = list(toks.values())
        for sem, val in waits:
            self.waited[(eng, id(sem))] = val
        return waits

    def _commit(self, tok, reads, writes):
        for k in reads:
            self.reads.setdefault(k, []).append(tok)
        for k in writes:
            self.lastw[k] = tok
            self.reads[k] = []

    def op(self, eng, fn, reads=(), writes=()):
        waits = self._deps(eng, reads, writes)
        self.cnt[eng] += 1
        tok = (self.sem[eng], self.cnt[eng], eng)
        self.ops[eng].append((waits, fn, (self.sem[eng], 1)))
        self._commit(tok, reads, writes)
        return tok

    def dma(self, eng, slotname, fn, reads=(), writes=(), inc=16):
        slot = self.slot(slotname)
        waits = self._deps(eng, reads, writes)
        slot.total += inc
        tok = (slot.sem, slot.total, "dma")
        self.ops[eng].append((waits, fn, (slot.sem, inc)))
        self._commit(tok, reads, writes)
        return tok

    def barrier(self):
        toks = [(self.sem[e], self.cnt[e], "x") for e in self.ENG if self.cnt[e] > 0]
        toks += [(s.sem, s.total, "dma") for s in self.slots.values() if s.total > 0]
        for e in self.ENG:
            need = {}
            for t in toks:
                if t[0] is self.sem[e]:
                    continue
                self._need(e, need, t)
            waits = list(need.values())
            for sem, val in waits:
                self.waited[(e, id(sem))] = val
            if waits:
                self.ops[e].append((waits, None, None))

    def emit(self):
        nc = self.nc
        with nc.Block() as block:
            def mk(e):
                def body(h):
                    for waits, fn, inc in self.ops[e]:
                        for sem, val in waits:
                            h.wait_ge(sem, val)
                        if fn is not None:
                            fn(h).then_inc(inc[0], inc[1])
                return body
            block.tensor(mk("pe"))
            block.scalar(mk("act"))
            block.vector(mk("dve"))
            block.gpsimd(mk("pool"))
            block.sync(mk("sp"))


def T(name, idxs):
    return [(name, i) for i in idxs]


def hk(js, k):
    return [("hT", j, k) for j in js]


def tiles_of(t0, n):
    return range(t0 // 128, (t0 + n + 127) // 128)


def make_consts():
    c = {}
    p = np.arange(128)
    c["ident"] = np.eye(128, dtype=np.float32)
    tok = (np.arange(32)[None, :] * 128 + p[:, None]).astype(np.float32)
    row = np.floor(tok / 64.0).astype(np.float32)
    col = (tok - row * 64.0).astype(np.float32)
    inv = (10000.0 ** (-np.arange(0, 32, 2, dtype=np.float32) / 32.0)).astype(np.float32)
    ar = (row[..., None] * inv).astype(np.float32)
    ac = (col[..., None] * inv).astype(np.float32)
    cosT = np.concatenate([np.cos(ar), np.cos(ar), np.cos(ac), np.cos(ac)], axis=-1)
    sinT = np.concatenate([-np.sin(ar), np.sin(ar), -np.sin(ac), np.sin(ac)], axis=-1)
    c["ropec"] = cosT.astype(np.float32)
    c["ropes"] = sinT.astype(np.float32)
    s = p % 64
    t = np.arange(64)
    c["maskf"] = (s[:, None] <= t[None, :]).astype(np.float32)
    c["maskb"] = (s[:, None] >= t[None, :]).astype(np.float32)
    same = (p[:, None] // 64) == (p[None, :] // 64)
    c["trif"] = (same & (p[:, None] <= p[None, :])).astype(np.float32)
    c["trib"] = (same & (p[:, None] >= p[None, :])).astype(np.float32)
    c["chsel"] = ((p[:, None] // 64) == np.arange(2)[None, :]).astype(np.float32)
    sel = np.zeros((NA, NA, 128), np.float32)
    for h in range(NA):
        sel[h, h, :] = 1.0
    c["sel"] = sel
    dl = np.zeros((NA, NT, NA), np.float32)
    for h in range(NA):
        dl[h, :, h] = 1.0
    c["dlt"] = dl
    c["ones2"] = np.ones((NA, 64), np.float32)
    d = np.arange(256)
    ang = 2 * np.pi * np.outer(d, d) / 256.0
    f256 = np.concatenate([np.cos(ang), -np.sin(ang)], axis=1) / 16.0
    c["f256"] = f256.reshape(2, 128, 512).transpose(1, 0, 2).astype(np.float32)
    n = np.arange(64)
    a64 = 2 * np.pi * np.outer(n, n) / 64.0
    C64, S64 = np.cos(a64) / 8.0, np.sin(a64) / 8.0
    c["rb1"] = np.concatenate([C64, -S64], axis=1).astype(np.float32)
    c["rb2"] = np.concatenate([S64, C64], axis=1).astype(np.float32)
    n2 = (p % 64)[:, None, None]
    k1 = np.arange(64)[None, :, None]
    k2 = np.arange(64)[None, None, :]
    ag = 2 * np.pi * (n2 * (k1 + 64 * k2) % 4096) / 4096.0
    c["gc2"] = (np.cos(ag) / 16.0).astype(np.float32)
    c["gs2"] = (np.sin(ag) / 16.0).astype(np.float32)
    nn = np.arange(256)
    a256 = 2 * np.pi * (np.outer(nn, nn) % 256) / 256.0
    c["cc"] = (np.cos(a256) / 32.0).reshape(2, 128, 256).transpose(1, 0, 2).astype(np.float32)
    c["sc"] = (np.sin(a256) / 32.0).reshape(2, 128, 256).transpose(1, 0, 2).astype(np.float32)
    return c


def win_cols(subsets):
    cols = []
    for s in subsets:
        for i in range(NA):
            h = NA * s + i
            cols += list(range(0 + 128 * h, 128 * h + 128))
            cols += list(range(512 + 128 * h, 512 + 128 * h + 128))
            cols += list(range(1024 + 256 * h, 1024 + 256 * h + 256))
            cols += list(range(2064 + 256 * h, 2064 + 256 * h + 256))
            cols += list(range(3088 + 256 * h, 3088 + 256 * h + 256))
        for kind in range(4):
            for i in range(NA):
                cols.append(2048 + kind * 4 + NA * s + i)
        for i in range(NB):
            h = NB * s + i
            cols += list(range(4112 + 128 * h, 4112 + 128 * h + 128))
            cols += list(range(5136 + 128 * h, 5136 + 128 * h + 128))
            cols += list(range(6160 + 128 * h, 6160 + 128 * h + 128))
            cols += list(range(7184 + 128 * h, 7184 + 128 * h + 128))
        for i in range(NG):
            g = NG * s + i
            cols += list(range(8208 + 256 * g, 8208 + 256 * g + 256))
            cols += list(range(9232 + 256 * g, 9232 + 256 * g + 256))
    cols += list(range(10256, 13328))
    assert len(cols) == len(subsets) * SUBW + 3 * D
    return np.asarray(cols)


def gate_bias_idx(subsets):
    idx = []
    for s in subsets:
        for kind in range(4):
            for i in range(NA):
                idx.append(kind * 4 + NA * s + i)
    return np.asarray(idx)


def feat_rows(subsets):
    return np.concatenate([np.arange(512 * s, 512 * s + 512) for s in subsets])


class StopBuild(Exception):
    pass


class Ctx:
    def cut(self, name):
        if self.cfg.get("cut") == name:
            raise StopBuild()

    def __init__(self, nc, st, cfg):
        self.nc = nc
        self.st = st
        self.cfg = cfg
        self.S = Sched(nc, st)
        self.pair = bool(cfg.get("pair", False))
        self.nsub = 1 if self.pair else NSUB
        self.mg0 = self.nsub * SUBW
        self.pers = SB_BASE
        self.scr0 = None
        self.scr = None
        self.uid = 0
        self.d = {}
        self.ps = {}

    def _alloc(self, name, shape, dt, off):
        self.uid += 1
        t = self.nc.alloc_sbuf_tensor_at(f"{name}_{self.uid}", list(shape), dt, offset=off)
        return t.ap()

    @staticmethod
    def _bytes(shape, dt):
        n = 1
        for s in shape[1:]:
            n *= s
        b = n * (2 if dt == BF16 else 4)
        return (b + 63) // 64 * 64

    def pb(self, name, shape, dt):
        assert self.scr0 is None
        off = self.pers
        self.pers += self._bytes(shape, dt)
        assert self.pers <= SB_END, "persistent SBUF overflow"
        return self._alloc(name, shape, dt, off)

    def freeze(self):
        self.scr0 = self.pers
        self.scr = self.scr0

    def reset(self):
        self.S.barrier()
        self.scr = self.scr0

    def sb(self, name, shape, dt):
        off = self.scr
        self.scr += self._bytes(shape, dt)
        assert self.scr <= SB_END, f"scratch SBUF overflow at {name}: {self.scr - SB_END}"
        return self._alloc(name, shape, dt, off)

    def mm(self, out, lhsT, rhs, start, stop, r, w, **kw):
        return self.S.op("pe", lambda h: h.matmul(out, lhsT, rhs, start=start, stop=stop, **kw), r, w)

    def tr(self, out, in_, ident, r, w):
        return self.S.op("pe", lambda h: h.transpose(out, in_, ident), r, w)

    def act(self, out, in_, func, r, w, scale=1.0, bias=0.0, accum_out=None):
        if accum_out is None:
            return self.S.op("act", lambda h: h.activation(out=out, in_=in_, func=func, bias=bias, scale=scale), r, w)
        return self.S.op("act", lambda h: h.activation(out=out, in_=in_, func=func, bias=bias, scale=scale,
                                                       accum_out=accum_out), r, w)

    def tt(self, eng, out, in0, in1, op, r, w):
        return self.S.op(eng, lambda h: h.tensor_tensor(out, in0, in1, op), r, w)

    def ts(self, eng, out, in0, s1, s2, op0, op1, r, w):
        if s2 is None:
            return self.S.op(eng, lambda h: h.tensor_scalar(out, in0, s1, None, op0), r, w)
        return self.S.op(eng, lambda h: h.tensor_scalar(out, in0, s1, s2, op0, op1), r, w)

    def stt(self, out, in0, scalar, in1, op0, op1, r, w, accum_out=None):
        if accum_out is None:
            return self.S.op("dve", lambda h: h.scalar_tensor_tensor(out, in0, scalar, in1, op0, op1), r, w)
        return self.S.op("dve", lambda h: h.scalar_tensor_tensor(out, in0, scalar, in1, op0, op1,
                                                                 accum_out=accum_out), r, w)

    def cp(self, eng, out, in_, r, w):
        if eng == "act":
            return self.S.op("act", lambda h: h.copy(out, in_), r, w)
        return self.S.op(eng, lambda h: h.tensor_copy(out, in_), r, w)

    def memset(self, eng, ap, val, w):
        return self.S.op(eng, lambda h: h.memset(ap, val), (), w)

    def dma(self, eng, slot, out, in_, r, w, **kw):
        return self.S.dma(eng, slot, lambda h: h.dma_start(out=out, in_=in_, **kw), r, w)

    def rstd(self, out, ss, n, r, w):
        self.ts("dve", out, ss, 1.0 / n, EPS, ALU.mult, ALU.add, r, w)
        self.act(out, out, AF.Sqrt, w, w)
        return self.S.op("dve", lambda h: h.reciprocal(out, out), w, w)


def wslice(K, l, c0, n):
    return K.d["win"][l].rearrange("(k p) n -> p k n", p=128)[:, :, c0:c0 + n]


CONST_SHAPES = dict(ident=[128, 128], ropec=[128, 32, 64], ropes=[128, 32, 64], maskf=[128, 64], maskb=[128, 64],
                    trif=[128, 128], trib=[128, 128], chsel=[128, 2], sel=[NA, NA, 128], dlt=[NA, NT, NA],
                    ones2=[NA, 64], f256=[128, 2, 512], rb1=[64, 128], rb2=[64, 128], gc2=[128, 64, 64],
                    gs2=[128, 64, 64], cc=[128, 2, 256], sc=[128, 2, 256])


def declare_dram(K):
    nc, d = K.nc, K.d
    DEPTH = K.cfg.get("layers", 4)
    inp = lambda n, s: nc.dram_tensor(n, list(s), F32, kind="ExternalInput").ap()
    d["xin"] = inp("xin", [NTOK, D])
    d["cvec"] = inp("cvec", [128, 8, 2])
    d["normw"] = inp("normw", [DEPTH, 128, 8])
    d["wada"] = inp("wada", [DEPTH, D, 3 * D])
    d["bada"] = inp("bada", [DEPTH, 128, 24])
    ns = K.nsub
    d["win"] = inp("win", [DEPTH, D, ns * SUBW + 3 * D])
    d["bif"] = inp("bif", [DEPTH, ns * 4 * NA])
    d["anw"] = inp("anw", [DEPTH, ns * 512])
    d["qknw"] = inp("qknw", [DEPTH, 256])
    d["lam4"] = inp("lam4", [DEPTH, 256])
    d["subw"] = inp("subw", [DEPTH, 128])
    for n in ("wao", "wbo", "wco"):
        d[n] = inp(n, [DEPTH, ns * 512, D])
    d["wo"] = inp("wo", [DEPTH, D, D])
    for n, s in CONST_SHAPES.items():
        d["c_" + n] = inp("c_" + n, s)
    d["out"] = nc.dram_tensor("out", [NLAT, D], F32, kind="ExternalOutput").ap()
    d["xctx"] = nc.dram_tensor("xctx", [NCTX, D], F32, kind="Internal").ap()
    d["yT"] = nc.dram_tensor("yT", [3, 4, 128, NTOK], BF16, kind="Internal").ap()
    d["gsc"] = nc.dram_tensor("gsc", [2, D], F32, kind="Internal").ap()
    if K.pair:
        d["xpart"] = nc.dram_tensor("xpart", [NTOK, D], F32, kind="Internal").ap()
        d["xcur"] = nc.dram_tensor("xcur", [NTOK, D], F32, kind="Internal").ap()
    if K.cfg.get("dbg"):
        d["dbg_hT"] = nc.dram_tensor("dbg_hT", [128, 8, NTOK], BF16, kind="ExternalOutput").ap()
        d["dbg_yT"] = nc.dram_tensor("dbg_yT", [K.nsub, 3, 4, 128, NTOK], BF16, kind="ExternalOutput").ap()
        d["dbg_ctx"] = nc.dram_tensor("dbg_ctx", [NCTX, D], F32, kind="ExternalOutput").ap()


def xrows(K, l, j, write=False):
    if K.pair:
        if write:
            return K.d["xpart"][j * 128:(j + 1) * 128, :], ("xp", j)
        src = K.d["xin"] if l == 0 else K.d["xcur"]
        return src[j * 128:(j + 1) * 128, :], ("xd", j)
    if j < 2:
        src = K.d["xin"] if (l == 0 and not write) else K.d["xctx"]
        return src[j * 128:(j + 1) * 128, :], ("xd", j)
    jj = j - 2
    if l == 0 and not write:
        return K.d["xin"][NCTX + jj * 128:NCTX + (jj + 1) * 128, :], ("xd", j)
    return K.d["out"][jj * 128:(jj + 1) * 128, :], ("xd", j)


def setup(K):
    S, d = K.S, K.d
    K.hT = K.pb("hT", [128, 8, NTOK], BF16)
    K.ident = K.pb("ident", [128, 128], BF16)
    K.identf = K.pb("identf", [128, 128], F32)
    K.c_mhalf = K.pb("mhalf", [128, 64], F32)
    K.scT = K.pb("scT", [128, 8, 2], BF16)
    K.A1 = K.pb("A1", [128, 8, 2], F32)
    K.B1 = K.pb("B1", [128, 8, 2], F32)
    K.gateb = K.pb("gateb", [128, 2, D], F32)
    K.lam = K.pb("lam", [128, 1], F32)
    K.freeze()
    nc = K.nc
    K.pT = [K.st.enter_context(nc.psum_tensor(f"pT{i}", [128, 1024], BF16)) for i in range(2)]
    K.pf = [K.st.enter_context(nc.psum_tensor(f"pf{i}", [128, 512], F32)) for i in range(6)]
    K.dma("pool", "c0", K.ident, d["c_ident"], (), ["ident"])
    K.dma("sp", "c1", K.identf, d["c_ident"], (), ["identf"])
    K.memset("dve", K.c_mhalf, -0.5, ["mhalf"])
    cv = K.sb("cv", [128, 8, 2], F32)
    th = K.sb("cth", [128, 8, 2], F32)
    K.dma("sp", "c2", cv, d["cvec"], (), ["cv"])
    K.act(th, cv, AF.Tanh, ["cv"], ["cth"], scale=0.5)
    K.stt(th, th, 1.0, cv, ALU.add, ALU.mult, ["cth", "cv"], ["cth"])
    K.ts("dve", K.scT, th, 0.5, None, ALU.mult, None, ["cth"], ["scT"])


def phase_norm(K, l):
    S, d = K.S, K.d
    K.reset()
    normw = K.sb("normw", [128, 8], F32)
    bada = K.sb("bada", [128, 24], F32)
    modT = K.sb("modT", [128, 24, 2], F32)
    wad = K.sb("wad", [128, 8, D], BF16)
    K.dma("sp", "sv0", normw, d["normw"][l], (), ["normw"])
    K.dma("sp", "sv1", bada, d["bada"][l], (), ["bada"])
    pm = K.pf[0][:, 0:48]
    wv = d["wada"][l].rearrange("(k p) n -> p k n", p=128)
    for third in range(3):
        K.dma("pool", "wad", wad, wv[:, :, third * D:(third + 1) * D], (), ["wad"], max_dma_last_dim=4096)
        for m in range(8):
            g = third * 8 + m
            for k in range(8):
                K.mm(pm[:, 2 * g:2 * g + 2], wad[:, k, m * 128:(m + 1) * 128], K.scT[:, k, :], k == 0, k == 7,
                     ["wad", "scT"], ["pf0"])
    K.tt("dve", modT, K.pf[0][:, 0:48].rearrange("p (g j) -> p g j", j=2),
         bada.unsqueeze(2).to_broadcast([128, 24, 2]), ALU.add, ["pf0", "bada"], ["modT"])
    K.stt(K.A1, modT[:, 8:16, :], 1.0, normw.unsqueeze(2).to_broadcast([128, 8, 2]), ALU.add, ALU.mult,
          ["modT", "normw"], ["A1"])
    K.cp("dve", K.B1, modT[:, 0:8, :], ["modT"], ["B1"])
    gt = K.sb("gt", [128, 2, 8], F32)
    gtT = K.sb("gtT", [8, 2, 128], F32)
    for j in range(2):
        K.ts("dve", gt[:, j, :], modT[:, 16:24, j], 0.5, None, ALU.mult, None, ["modT"], ["gt"])
    for j in range(2):
        K.tr(K.pf[1][0:8, j * 128:(j + 1) * 128], gt[:, j, :], K.identf, ["gt", "identf"], ["pf1"])
    K.cp("dve", gtT, K.pf[1][0:8, 0:256].rearrange("p (j f) -> p j f", j=2), ["pf1"], ["gtT"])
    K.dma("sp", "gsc", d["gsc"].rearrange("j (k p) -> k j p", p=128), gtT, ["gtT"], ["gsc"])
    for j in range(2):
        K.dma("sp", "gsc", K.gateb[:, j, :], d["gsc"][j].partition_broadcast(128), ["gsc"], ["gateb"])
    K.cut("p0")
    NXB = 3
    xt = [K.sb(f"xt{i}", [128, D], F32) for i in range(NXB)]
    xn = [K.sb(f"xn{i}", [128, D], BF16) for i in range(2)]
    junk = K.sb("junk", [128, D], F32)
    ss = K.sb("ss", [128, NT], F32)
    rs = K.sb("rs", [128, NT], F32)
    for j in range(NT):
        b = j % NXB
        src, xkey = xrows(K, l, j)
        K.dma("sp", f"xt{b}", xt[b], src, [xkey], [f"xt{b}"])
        K.act(junk, xt[b], AF.Square, [f"xt{b}"], ["junk"])
        K.S.op("dve", lambda h, j=j: h.tensor_reduce(ss[:, j:j + 1], junk, AX.X, ALU.add), ["junk"], [("ss", j)])
        K.cut("p1a")
        K.rstd(rs[:, j:j + 1], ss[:, j:j + 1], D, [("ss", j)], [("rs", j)])
        K.cut("p1b")
        nb = j % 2
        K.ts("dve", xn[nb], xt[b], rs[:, j:j + 1], None, ALU.mult, None, [f"xt{b}", ("rs", j)], [f"xn{nb}"])
        K.cut("p1c")
        for k in range(8):
            K.tr(K.pT[k // 4][:, (k % 4) * 128:(k % 4 + 1) * 128], xn[nb][:, k * 128:(k + 1) * 128], K.ident,
                 [f"xn{nb}", "ident"], [f"pT{k // 4}"])
        K.cut("p1d")
        jc = 1 if j < 2 else 0
        for k in range(8):
            o = K.hT[:, k, j * 128:(j + 1) * 128]
            i_ = K.pT[k // 4][:, (k % 4) * 128:(k % 4 + 1) * 128]
            if k < 4:
                K.act(o, i_, AF.Identity, [f"pT{k // 4}", "A1", "B1"], [("hT", j, k)], scale=K.A1[:, k, jc:jc + 1],
                      bias=K.B1[:, k, jc:jc + 1])
            else:
                K.ts("dve", o, i_, K.A1[:, k, jc:jc + 1], K.B1[:, k, jc:jc + 1], ALU.mult, ALU.add,
                     [f"pT{k // 4}", "A1", "B1"], [("hT", j, k)])
            if k == 0:
                K.cut("p1k0")
            if k == 1:
                K.cut("p1k1")
        K.cut(f"p1e{j}")
    K.cut("p1f")


def finish(K):
    K.S.barrier()


def build(cfg):
    nc = bass.Bass("TRN2", target_bir_lowering=False)
    with ExitStack() as st:
        K = Ctx(nc, st, cfg)
        declare_dram(K)
        try:
            build_body(K, cfg)
        except StopBuild:
            pass
        finish(K)
        K.S.emit()
    return nc


def build_body(K, cfg):
    if True:
        setup(K)
        K.cut("setup")
        for l in range(cfg.get("layers", DEPTH)):
            phase_norm(K, l)
            if cfg.get("dbg") == "norm":
                K.dma("sp", "dbg", K.d["dbg_hT"], K.hT, [("hT", j, k) for j in range(NT) for k in range(8)], ["dbg"])
                break
            for s in range(K.nsub):
                if "mlstm" not in cfg.get("skip", ()):
                    phase_mlstm(K, l, s)
                if "attn" not in cfg.get("skip", ()):
                    phase_attn(K, l, s)
                if "fft" not in cfg.get("skip", ()):
                    phase_fft(K, l, s)
                if cfg.get("dbg"):
                    K.S.barrier()
                    K.dma("sp", "dbg", K.d["dbg_yT"][s], K.d["yT"], ["yT"], ["dbg"])
                if "merge" not in cfg.get("skip", ()):
                    phase_merge(K, l, s)
            if K.pair and l == cfg.get("layers", DEPTH) - 1:
                K.dma("sp", "fin", K.d["out"], K.d["xcur"][NCTX:NTOK, :], [("xd", j) for j in range(2, NT)], ["outfin"])
        if cfg.get("dbg"):
            K.S.barrier()
            K.dma("sp", "dbg", K.d["dbg_ctx"], K.d["xctx"], [("xd", 0), ("xd", 1)], ["dbg"])


def prep_shared(inp, DEPTH=DEPTH, subsets=(0, 1)):
    f = lambda a: np.ascontiguousarray(np.asarray(a, dtype=np.float32))
    inp = {k: (np.asarray(v)[:DEPTH] if k not in ("x", "c", "ctx", "c_ctx") else v) for k, v in inp.items()}
    subsets = list(subsets)
    sh = {}
    sh["normw"] = f(inp["norm_w"].reshape(DEPTH, 8, 128).transpose(0, 2, 1))
    sh["wada"] = f(inp["w_ada"])
    sh["bada"] = f(inp["b_ada"].reshape(DEPTH, 24, 128).transpose(0, 2, 1))
    sh["win"] = f(inp["w_in"][:, :, win_cols(subsets)])
    sh["bif"] = f(inp["b_if"][:, gate_bias_idx(subsets)])
    fr = feat_rows(subsets)
    sh["anw"] = f(inp["a_norm_w"][:, fr])
    sh["qknw"] = f(np.concatenate([inp["q_norm_w"], inp["q_norm_w"], inp["k_norm_w"], inp["k_norm_w"]], axis=1))
    sh["lam4"] = f(np.concatenate([inp["lambda_q1"], inp["lambda_k1"], inp["lambda_q2"], inp["lambda_k2"]], axis=1))
    sh["subw"] = f(inp["subln_w"])
    sh["wao"] = f(inp["w_a_out"][:, fr]); sh["wbo"] = f(inp["w_b_out"][:, fr]); sh["wco"] = f(inp["w_c_out"][:, fr])
    sh["wo"] = f(inp["w_out"])
    for n, v in make_consts().items():
        assert list(v.shape) == CONST_SHAPES[n], (n, v.shape)
        sh["c_" + n] = f(v)
    return sh


def prep_core(inp, b):
    m = {}
    m["xin"] = np.ascontiguousarray(np.concatenate([inp["ctx"][b], inp["x"][b]], axis=0).astype(np.float32))
    cv = np.stack([np.asarray(inp["c"][b]), np.asarray(inp["c_ctx"])], axis=-1)
    m["cvec"] = np.ascontiguousarray(cv.reshape(8, 128, 2).transpose(1, 0, 2).astype(np.float32))
    return m


_NC_CACHE = {}


PAIR = True


def kernel(**inputs):
    cfg = {"pair": PAIR}
    if "full" not in _NC_CACHE:
        _NC_CACHE["full"] = build(cfg)
    nc = _NC_CACHE["full"]
    in_maps = []
    if PAIR:
        shs = [prep_shared(inputs, DEPTH, (s,)) for s in range(NSUB)]
        cores = [prep_core(inputs, b) for b in range(4)]
        for core in range(8):
            m = dict(shs[core % 2])
            m.update(cores[core // 2])
            in_maps.append(m)
        res = run_bass_kernel_spmd(nc, in_maps, core_ids=list(range(8)))
        out = np.stack([np.asarray(res.results[2 * b]["out"]) for b in range(4)], axis=0)
    else:
        sh = prep_shared(inputs)
        for core in range(8):
            m = dict(sh)
            m.update(prep_core(inputs, core % 4))
            in_maps.append(m)
        res = run_bass_kernel_spmd(nc, in_maps, core_ids=list(range(8)))
        out = np.stack([np.asarray(res.results[b]["out"]) for b in range(4)], axis=0)
    return out.astype(np.float32)


def phase_merge(K, l, s):
    S, d = K.S, K.d
    last = (l == K.cfg.get("nlayers_total", DEPTH) - 1)
    K.reset()
    TG = 256
    wmg = K.sb("wmg", [128, 8, 3 * D], BF16)
    wbr = [K.sb(f"wbr{b}", [128, 4, D], BF16) for b in range(3)]
    wo = K.sb("wo", [128, 8, D], BF16)
    for b in range(3):
        K.dma("pool", f"wmg{b}", wmg[:, :, b * D:(b + 1) * D], wslice(K, l, K.mg0 + b * D, D), (), [("wmg", b)],
              max_dma_last_dim=4096)
        srcw = d[("wao", "wbo", "wco")[b]][l][512 * s:512 * s + 512, :].rearrange("(k p) n -> p k n", p=128)
        K.dma("pool", f"wbr{b}", wbr[b], srcw, (), [("wbr", b)], max_dma_last_dim=4096)
    K.dma("pool", "wo", wo, d["wo"][l].rearrange("(k p) n -> p k n", p=128), (), ["wo"], max_dma_last_dim=4096)
    ybr = [[K.sb(f"ybr{b}_{i}", [128, 4, TG], BF16) for b in range(3)] for i in range(2)]
    th = [K.sb(f"mth{i}", [128, TG], F32) for i in range(2)]
    tmp = [K.sb(f"mtmp{i}", [128, TG], F32) for i in range(2)]
    yacc = K.sb("yacc", [128, 8, TG], F32)
    yTb = K.sb("yTb", [128, 8, TG], BF16)
    xt = [K.sb(f"mxt{i}", [128, D], F32) for i in range(2)]
    tmp2 = K.sb("mtmp2", [128, 512], F32)
    it = 0
    xi = 0
    pend = []
    for tg in range(NTOK // TG):
        if tg == 0 and last:
            continue
        t0 = tg * TG
        jc = 1 if tg == 0 else 0
        yb = ybr[tg % 2]
        for b in range(3):
            K.dma("sp", f"ybr{b}_{tg % 2}", yb[b], d["yT"][b, :, :, t0:t0 + TG].rearrange("c p t -> p c t"),
                  ["yT"], [f"ybr{b}_{tg % 2}"])
        tls = tiles_of(t0, TG)
        for m in range(8):
            for b in range(3):
                pg = K.pf[it % 2]
                pp = K.pf[2 + it % 2]
                for k in range(8):
                    K.mm(pg[:, 0:TG], wmg[:, k, b * D + m * 128:b * D + (m + 1) * 128], K.hT[:, k, t0:t0 + TG],
                         k == 0, k == 7, [("wmg", b)] + hk(tls, k), [f"pf{it % 2}"])
                K.act(th[it % 2], pg[:, 0:TG], AF.Tanh, [f"pf{it % 2}"], [f"mth{it % 2}"], scale=0.5)
                for kc in range(4):
                    K.mm(pp[:, 0:TG], wbr[b][:, kc, m * 128:(m + 1) * 128], yb[b][:, kc, :], kc == 0, kc == 3,
                         [("wbr", b), f"ybr{b}_{tg % 2}"], [f"pf{2 + it % 2}"])
                if b == 0:
                    K.stt(yacc[:, m, :], th[it % 2], 1.0, pp[:, 0:TG], ALU.add, ALU.mult,
                          [f"mth{it % 2}", f"pf{2 + it % 2}"], [("yacc", m)])
                else:
                    K.stt(tmp[it % 2], th[it % 2], 1.0, pp[:, 0:TG], ALU.add, ALU.mult,
                          [f"mth{it % 2}", f"pf{2 + it % 2}"], [f"mtmp{it % 2}"])
                    if b == 1:
                        K.tt("pool", yacc[:, m, :], yacc[:, m, :], tmp[it % 2], ALU.add,
                             [f"mtmp{it % 2}", ("yacc", m)], [("yacc", m)])
                    else:
                        K.tt("pool", yTb[:, m, :], yacc[:, m, :], tmp[it % 2], ALU.add,
                             [f"mtmp{it % 2}", ("yacc", m)], [("yTb", m)])
                it += 1
        for tsub in range(TG // 128):
            j = t0 // 128 + tsub
            xb = xt[xi % 2]
            src, xkey = xrows(K, l if s == 0 else 99, j)
            K.dma("sp", f"mxt{xi % 2}", xb, src, [xkey], [f"mxt{xi % 2}"])
            if K.pair:
                K.ts("pool", xb, xb, 0.5, None, ALU.mult, None, [f"mxt{xi % 2}"], [f"mxt{xi % 2}"])
            for half in range(2):
                po = K.pf[4 + half]
                for m in range(8):
                    K.mm(po[:, 0:512], yTb[:, m, tsub * 128:(tsub + 1) * 128], wo[:, m, half * 512:(half + 1) * 512],
                         m == 0, m == 7, [("yTb", m), "wo"], [f"pf{4 + half}"])
                K.tt("dve", tmp2, po[:, 0:512], K.gateb[:, jc, half * 512:(half + 1) * 512], ALU.mult,
                     [f"pf{4 + half}", "gateb"], ["mtmp2"])
                K.tt("pool", xb[:, half * 512:(half + 1) * 512], xb[:, half * 512:(half + 1) * 512], tmp2, ALU.add,
                     ["mtmp2", f"mxt{xi % 2}"], [f"mxt{xi % 2}"])
            dst, xkey = xrows(K, l, j, write=True)
            K.dma("sp", f"mxt{xi % 2}", dst, xb, [f"mxt{xi % 2}"], [xkey])
            xi += 1
        if K.pair:
            pend.append(tg)
    if K.pair:
        K.S.barrier()
        for tg in pend:
            pair_exchange(K, tg)
        K.S.barrier()


def phase_attn(K, l, s):
    S, d = K.S, K.d
    last = (l == K.cfg.get("nlayers_total", DEPTH) - 1)
    lam_init = 0.8 - 0.6 * math.exp(-0.3 * l)
    K.reset()
    ropec = K.sb("ropec", [128, 32, 64], F32)
    ropes = K.sb("ropes", [128, 32, 64], F32)
    nwb = K.sb("nwb", [128, 256], F32)
    subwb = K.sb("subwb", [128, 128], F32)
    lam4 = K.sb("lam4", [128, 256], F32)
    lsum = K.sb("lsum", [128, 2], F32)
    neglam = K.sb("neglam", [128, 1], F32)
    K.dma("sp", "a0", ropec, d["c_ropec"], (), ["ropec"])
    K.dma("sp", "a1", ropes, d["c_ropes"], (), ["ropes"])
    K.dma("sp", "a2", nwb, d["qknw"][l].partition_broadcast(128), (), ["nwb"])
    K.dma("sp", "a3", subwb, d["subw"][l].partition_broadcast(128), (), ["subwb"])
    K.dma("sp", "a4", lam4, d["lam4"][l].partition_broadcast(128), (), ["lam4"])
    K.ts("dve", nwb[:, 0:128], nwb[:, 0:128], 0.125, None, ALU.mult, None, ["nwb"], ["nwb"])
    K.ts("dve", subwb, subwb, (1.0 - lam_init) * 0.5, None, ALU.mult, None, ["subwb"], ["subwb"])
    lp = K.sb("lp", [128, 2, 64], F32)
    l4 = lam4.rearrange("p (a b c) -> p a b c", a=2, b=2)
    K.tt("dve", lp, l4[:, :, 0, :], l4[:, :, 1, :], ALU.mult, ["lam4"], ["lp"])
    K.S.op("dve", lambda h: h.tensor_reduce(lsum, lp, AX.X, ALU.add), ["lp"], ["lsum"])
    K.act(lsum, lsum, AF.Exp, ["lsum"], ["lsum"])
    K.stt(neglam, lsum[:, 1:2], -lam_init, lsum[:, 0:1], ALU.add, ALU.subtract, ["lsum"], ["neglam"])
    wB = K.sb("wB", [128, 8, 512], BF16)
    qT = K.sb("aqT", [128, NTOK], BF16)
    kT = K.sb("akT", [128, NTOK], BF16)
    vaug = K.sb("avaug", [128, NT, 129], BF16)
    K.memset("pool", vaug[:, :, 128:129], 1.0, T("avaug", range(NT)))
    qk32 = K.sb("qk32", [128, 256], F32)
    sq = K.sb("asq", [128, 256], F32)
    ss4 = K.sb("ss4", [128, 4], F32)
    rs4 = K.sb("rs4", [128, 4], F32)
    xn = K.sb("axn", [128, 256], F32)
    t1 = K.sb("at1", [128, 256], F32)
    t2 = K.sb("at2", [128, 256], F32)
    qkr = [K.sb(f"qkr{i}", [128, 256], BF16) for i in range(2)]
    NE = 4
    ET = [K.sb(f"ET{i}", [128, 512], BF16) for i in range(NE)]
    rr = K.sb("arr", [128, 2], F32)
    tt_ = K.sb("att", [128, 128], F32)
    o32 = K.sb("ao32", [128, 128], F32)
    junk = K.sb("ajunk", [128, 128], F32)
    ss1 = K.sb("ass1", [128, 1], F32)
    rs1 = K.sb("ars1", [128, 1], F32)
    thz = K.sb("athz", [128, 128], F32)
    wz = K.sb("awz", [128, 128], F32)
    y1 = K.sb("ay1", [128, 128], F32)
    y2 = K.sb("ay2", [128, 128], BF16)
    ybT = [K.sb(f"aybT{i}", [128, 512], BF16) for i in range(2)]
    ei = 0
    yi = 0
    for i in range(NB):
        c0 = s * SUBW + 2056 + i * 512
        K.dma("pool", "wB", wB, wslice(K, l, c0, 512), (), ["wB"], max_dma_last_dim=2048)
        for j in range(NT):
            pqk = K.pf[5]
            for k in range(8):
                K.mm(pqk[:, 0:256], K.hT[:, k, j * 128:(j + 1) * 128], wB[:, k, 0:256], k == 0, k == 7,
                     ["wB"] + hk([j], k), ["pf5"])
            for k in range(8):
                K.mm(pqk[:, 256:384], K.hT[:, k, j * 128:(j + 1) * 128], wB[:, k, 256:384], k == 0, k == 7,
                     ["wB"] + hk([j], k), ["pf5"])
            K.cp("dve", vaug[:, j, 0:128], pqk[:, 256:384], ["pf5"], [("avaug", j)])
            K.cp("dve", qk32, pqk[:, 0:256], ["pf5"], ["qk32"])
            K.act(sq, qk32, AF.Square, ["qk32"], ["asq"])
            K.S.op("dve", lambda h: h.tensor_reduce(ss4, sq.rearrange("p (a b) -> p a b", a=4), AX.X, ALU.add),
                   ["asq"], ["ss4"])
            K.rstd(rs4, ss4, 64, ["ss4"], ["rs4"])
            K.tt("dve", xn.rearrange("p (a b) -> p a b", a=4), qk32.rearrange("p (a b) -> p a b", a=4),
                 rs4.unsqueeze(2).to_broadcast([128, 4, 64]), ALU.mult, ["qk32", "rs4"], ["axn"])
            qb = qkr[j % 2]
            if j >= 2:
                jl = j - 2
                K.tt("pool", xn, xn, nwb, ALU.mult, ["axn", "nwb"], ["axn"])
                K.tt("dve", t1.rearrange("p (a b) -> p a b", a=4), xn.rearrange("p (a b) -> p a b", a=4),
                     ropec[:, jl, :].unsqueeze(1).to_broadcast([128, 4, 64]), ALU.mult, ["axn", "ropec"], ["at1"])
                x5 = xn.rearrange("p (a h i) -> p a h i", h=2, i=16)
                t5 = t2.rearrange("p (a h i) -> p a h i", h=2, i=16)
                s5 = ropes[:, jl, :].rearrange("p (r h i) -> p r h i", h=2, i=16)
                for hh in range(2):
                    sin_b = s5[:, :, hh, :].unsqueeze(1).to_broadcast([128, 4, 2, 16])
                    K.tt("pool", t5[:, :, hh, :].rearrange("p (s r) i -> p s r i", r=2),
                         x5[:, :, 1 - hh, :].rearrange("p (s r) i -> p s r i", r=2), sin_b, ALU.mult,
                         ["axn", "ropes"], ["at2"])
                K.tt("dve", qb, t1, t2, ALU.add, ["at1", "at2"], [f"qkr{j % 2}"])
            else:
                K.tt("dve", qb, xn, nwb, ALU.mult, ["axn", "nwb"], [f"qkr{j % 2}"])
            pt = K.pT[j % 2]
            for hf in range(2):
                K.tr(pt[:, hf * 128:(hf + 1) * 128], qb[:, hf * 128:(hf + 1) * 128], K.ident,
                     [f"qkr{j % 2}", "ident"], [f"pT{j % 2}"])
            if j % 2 == 0:
                K.cp("dve", qT[:, j * 128:(j + 1) * 128], pt[:, 0:128], [f"pT{j % 2}"], [("aqT", j)])
                K.cp("dve", kT[:, j * 128:(j + 1) * 128], pt[:, 128:256], [f"pT{j % 2}"], [("akT", j)])
            else:
                K.cp("act", qT[:, j * 128:(j + 1) * 128], pt[:, 0:128], [f"pT{j % 2}"], [("aqT", j)])
                K.cp("act", kT[:, j * 128:(j + 1) * 128], pt[:, 128:256], [f"pT{j % 2}"], [("akT", j)])
        groups = [(256 + 512 * g, 4, list(range(NT))) for g in range(8)]
        if not last:
            groups.append((0, 2, [0, 1]))
        for (q0, nq, kcs) in groups:
            nqc = nq * 128
            qtl = list(tiles_of(q0, nqc))

            started = set()

            def acc(m, qs):
                a = m * 4 + qs
                first = (a // 3) not in started
                return K.pf[2 + a // 3][:, (a % 3) * 129:(a % 3) * 129 + 129], f"pf{2 + a // 3}", first
            iters = [(ki, kc, m) for ki, kc in enumerate(kcs) for m in range(2)]

            def emit_qk(it_):
                _, kc_, m_ = iters[it_]
                K.mm(K.pf[m_][:, 0:nqc], kT[64 * m_:64 * m_ + 64, kc_ * 128:(kc_ + 1) * 128],
                     qT[64 * m_:64 * m_ + 64, q0:q0 + nqc], True, True,
                     [("akT", kc_)] + T("aqT", qtl), [f"pf{m_}"])
            emit_qk(0)
            for it_ in range(len(iters)):
                if it_ + 1 < len(iters):
                    emit_qk(it_ + 1)
                ki, kc, m = iters[it_]
                pS = K.pf[m]
                e = ET[ei % NE]
                K.act(e[:, 0:nqc], pS[:, 0:nqc], AF.Exp, [f"pf{m}"], [f"ET{ei % NE}"])
                for qs in range(nq):
                    ap_, key, first = acc(m, qs)
                    if ki == 0:
                        started.add((m * 4 + qs) // 3)
                    K.mm(ap_, e[:, qs * 128:(qs + 1) * 128], vaug[:, kc, :], (ki == 0 and first),
                         ki == len(kcs) - 1, [f"ET{ei % NE}", ("avaug", kc)], [(key, m, qs)],
                         skip_group_check=True)
                ei += 1
            yb = ybT[yi % 2]
            allacc = [(acc(m_, q_)[1], m_, q_) for m_ in range(2) for q_ in range(nq)]
            for qs in range(nq):
                a0, k0, _ = acc(0, qs)
                a1, k1, _ = acc(1, qs)
                jq = q0 // 128 + qs
                K.S.op("dve", lambda h, a0=a0: h.reciprocal(rr[:, 0:1], a0[:, 128:129]), allacc, ["arr"])
                K.S.op("dve", lambda h, a1=a1: h.reciprocal(rr[:, 1:2], a1[:, 128:129]), [(k1, 1, qs)], ["arr"])
                K.tt("dve", rr[:, 1:2], rr[:, 1:2], neglam, ALU.mult, ["arr", "neglam"], ["arr"])
                K.ts("dve", tt_, a1[:, 0:128], rr[:, 1:2], None, ALU.mult, None, [(k1, 1, qs), "arr"], ["att"])
                K.stt(o32, a0[:, 0:128], rr[:, 0:1], tt_, ALU.mult, ALU.add, [(k0, 0, qs), "arr", "att"], ["ao32"])
                K.act(junk, o32, AF.Square, ["ao32"], ["ajunk"])
                K.S.op("dve", lambda h: h.tensor_reduce(ss1, junk, AX.X, ALU.add), ["ajunk"], ["ass1"])
                K.rstd(rs1, ss1, 128, ["ass1"], ["ars1"])
                pz = K.pf[5]
                for k in range(8):
                    K.mm(pz[:, 0:128], K.hT[:, k, jq * 128:(jq + 1) * 128], wB[:, k, 384:512], k == 0, k == 7,
                         ["wB"] + hk([jq], k), ["pf5"])
                K.act(thz, pz[:, 0:128], AF.Tanh, ["pf5"], ["athz"], scale=0.5)
                K.stt(wz, thz, 1.0, pz[:, 0:128], ALU.add, ALU.mult, ["athz", "pf5"], ["awz"])
                K.stt(y1, o32, rs1, subwb, ALU.mult, ALU.mult, ["ao32", "ars1", "subwb"], ["ay1"])
                K.tt("dve", y2, y1, wz, ALU.mult, ["ay1", "awz"], ["ay2"])
                pt = K.pT[qs % 2]
                K.tr(pt[:, 0:128], y2, K.ident, ["ay2", "ident"], [f"pT{qs % 2}"])
                K.cp("dve", yb[:, qs * 128:(qs + 1) * 128], pt[:, 0:128], [f"pT{qs % 2}"], [f"aybT{yi % 2}"])
            K.dma("sp", f"aybT{yi % 2}", d["yT"][1, i, :, q0:q0 + nqc], yb[:, 0:nqc], [f"aybT{yi % 2}"],
                  [("yT", 1, i, q0)])
            yi += 1


def phase_fft(K, l, s):
    S, d = K.S, K.d
    K.reset()
    f256 = K.sb("f256", [128, 2, 512], BF16)
    rb1 = K.sb("rb1", [64, 128], BF16)
    rb2 = K.sb("rb2", [64, 128], BF16)
    gc2 = K.sb("gc2", [128, 64, 64], BF16)
    gs2 = K.sb("gs2", [128, 64, 64], BF16)
    cc = K.sb("fcc", [128, 2, 256], BF16)
    sc = K.sb("fsc", [128, 2, 256], BF16)
    for nm, t_ in (("f256", f256), ("rb1", rb1), ("rb2", rb2), ("gc2", gc2), ("gs2", gs2), ("cc", cc), ("sc", sc)):
        K.dma("pool", "fc_" + nm, t_, d["c_" + nm], (), ["fc_" + nm], max_dma_last_dim=4096)
    fck = ["fc_f256", "fc_rb1", "fc_rb2", "fc_gc2", "fc_gs2", "fc_cc", "fc_sc"]
    wC = K.sb("wC", [128, 8, 512], BF16)
    cuT = K.sb("cuT", [128, 2, NTOK], BF16)
    gz = K.sb("fgz", [128, NTOK], BF16)
    thz = K.sb("fthz", [128, 512], F32)
    Zt = K.sb("Zt", [64, 2, 64, 2, 64], BF16)
    AT = K.sb("AT", [128, 64, 2, 64], BF16)
    Zc = K.sb("Zc", [128, 2, 2, 128], BF16)
    ycT = K.sb("ycT", [128, NTOK], BF16)
    TGS = [(t0, min(512, NTOK - t0)) for t0 in range(0, NTOK, 512)]
    for i in range(NG):
        c0 = s * SUBW + 4104 + i * 512
        K.dma("pool", "wC", wC, wslice(K, l, c0, 512), (), ["wC"], max_dma_last_dim=2048)
        for dc in range(2):
            for gi, (t0, n) in enumerate(TGS):
                pc = K.pf[gi % 2]
                for k in range(8):
                    K.mm(pc[:, 0:n], wC[:, k, dc * 128:(dc + 1) * 128], K.hT[:, k, t0:t0 + n], k == 0, k == 7,
                         ["wC"] + hk(tiles_of(t0, n), k), [f"pf{gi % 2}"])
                if gi % 2 == 0:
                    K.cp("act", cuT[:, dc, t0:t0 + n], pc[:, 0:n], [f"pf{gi % 2}"], [("cuT", dc)])
                else:
                    K.cp("dve", cuT[:, dc, t0:t0 + n], pc[:, 0:n], [f"pf{gi % 2}"], [("cuT", dc)])
        for e in range(2):
            for gi, (t0, n) in enumerate(TGS):
                pc = K.pf[gi % 2]
                for k in range(8):
                    K.mm(pc[:, 0:n], wC[:, k, 256 + e * 128:256 + (e + 1) * 128], K.hT[:, k, t0:t0 + n], k == 0,
                         k == 7, ["wC"] + hk(tiles_of(t0, n), k), [f"pf{gi % 2}"])
                K.act(thz[:, 0:n], pc[:, 0:n], AF.Tanh, [f"pf{gi % 2}"], ["fthz"], scale=0.5)
                K.stt(gz[:, t0:t0 + n], thz[:, 0:n], 1.0, pc[:, 0:n], ALU.add, ALU.mult, ["fthz", f"pf{gi % 2}"],
                      ["fgz"])
            rcols = f256[:, :, :].rearrange("p c (r x) -> p c r x", r=2)[:, :, :, e * 128:(e + 1) * 128]
            for jc in range(2):
                pa = K.pf[2]
                for dc in range(2):
                    K.mm(pa[:, 0:256].rearrange("p (r x) -> p r x", r=2), cuT[:, dc, jc * 128:(jc + 1) * 128],
                         rcols[:, dc], dc == 0, dc == 1, [("cuT", 0), ("cuT", 1), "fc_f256"], ["pf2"])
                K.cp("dve", Zc[:, jc].rearrange("p r x -> p (r x)"), pa[:, 0:256], ["pf2"], ["Zc"])
            pcx = K.pf[3]
            n_mm = 0
            for jc in range(2):
                for r, tab in ((0, cc), (1, sc)):
                    K.mm(pcx[:, 0:256], Zc[:, jc, r, :], tab[:, jc, :], n_mm == 0, n_mm == 3,
                         ["Zc", "fc_cc", "fc_sc"], ["pf3"])
                    n_mm += 1
            K.tt("dve", ycT[:, 0:256], pcx[:, 0:256], gz[:, 0:256], ALU.mult, ["pf3", "fgz"], ["ycT"])
            for np_ in range(32):
                pa = K.pf[2 + np_ % 2]
                for q in range(2):
                    n2 = 2 * np_ + q
                    for dc in range(2):
                        lat = cuT[:, dc, NCTX:NTOK].rearrange("p (a b) -> p a b", b=64)[:, :, n2]
                        K.mm(pa[0:64, q * 256:(q + 1) * 256].rearrange("p (r x) -> p r x", r=2), lat, rcols[:, dc],
                             dc == 0, dc == 1, [("cuT", 0), ("cuT", 1), "fc_f256"], [f"pf{2 + np_ % 2}"])
                src_ = pa[0:64, 0:512].rearrange("p (q r m x) -> p q r m x", q=2, r=2, m=2)
                for m_ in range(2):
                    dst_ = Zt[:, :, :, m_, 2 * np_:2 * np_ + 2].rearrange("p r x q -> p q r x")
                    if np_ % 2 == 0:
                        K.cp("dve", dst_, src_[:, :, :, m_, :], [f"pf{2 + np_ % 2}"], ["Zt"])
                    else:
                        K.cp("act", dst_, src_[:, :, :, m_, :], [f"pf{2 + np_ % 2}"], ["Zt"])
            for ib in range(16):
                pb_ = K.pf[4 + ib % 2]
                for q in range(4):
                    ii = 4 * ib + q
                    for r, rb in ((0, rb1), (1, rb2)):
                        lhs = Zt[:, r, ii, :, :].rearrange("p m n -> p (m n)")
                        K.mm(pb_[:, q * 128:(q + 1) * 128], lhs, rb, r == 0, r == 1, ["Zt", "fc_rb1", "fc_rb2"],
                             [f"pf{4 + ib % 2}"])
                src_ = pb_[:, 0:512].rearrange("p (q r k) -> p q r k", q=4, r=2)
                dst_ = AT[:, :, :, 4 * ib:4 * ib + 4].rearrange("p k r q -> p q r k")
                if ib % 2 == 0:
                    K.cp("dve", dst_, src_, [f"pf{4 + ib % 2}"], ["AT"])
                else:
                    K.cp("act", dst_, src_, [f"pf{4 + ib % 2}"], ["AT"])
            for kb in range(8):
                pc = K.pf[kb % 2]
                for q in range(8):
                    k1 = 8 * kb + q
                    for m in range(2):
                        rows = slice(64 * m, 64 * m + 64)
                        for r, tab in ((0, gc2), (1, gs2)):
                            K.mm(pc[rows, q * 64:(q + 1) * 64], AT[rows, k1, r, :], tab[rows, k1, :], r == 0, r == 1,
                                 ["AT", "fc_gc2", "fc_gs2"], [f"pf{kb % 2}"], skip_group_check=True)
                lat_y = ycT[:, NCTX:NTOK].rearrange("p (k2 k1) -> p k1 k2", k1=64)[:, 8 * kb:8 * kb + 8, :]
                lat_g = gz[:, NCTX:NTOK].rearrange("p (k2 k1) -> p k1 k2", k1=64)[:, 8 * kb:8 * kb + 8, :]
                K.tt("dve", lat_y, pc[:, 0:512].rearrange("p (q k) -> p q k", q=8), lat_g, ALU.mult,
                     [f"pf{kb % 2}", "fgz"], ["ycT"])
            K.dma("sp", "ycT", d["yT"][2, 2 * i + e, :, :], ycT, ["ycT"], [("yT", 2, i, e)])


ORD = (list(range(NCH)), [3, 2, 1, 0] + list(range(NCH - 1, 3, -1)))


def phase_mlstm(K, l, s):
    S, d = K.S, K.d
    K.reset()
    cf = {}
    for nm, shp in (("maskf", [128, 64]), ("maskb", [128, 64]), ("trif", [128, 128]), ("trib", [128, 128]),
                    ("chsel", [128, 2]), ("sel", [NA, NA, 128]), ("dlt", [NA, NT, NA]), ("ones2", [NA, 64])):
        cf[nm] = K.sb("m_" + nm, shp, F32)
        K.dma("sp", "mc_" + nm, cf[nm], d["c_" + nm], (), ["m_" + nm])
    wg = K.sb("wg", [128, 8, 4 * NA], BF16)
    bifb = K.sb("bifb", [128, 4 * NA], F32)
    anwb = K.sb("anwb", [128, 512], F32)
    K.dma("pool", "wg", wg, wslice(K, l, s * SUBW + 2048, 4 * NA), (), ["wg"])
    K.dma("sp", "bifb", bifb, d["bif"][l, s * 4 * NA:(s + 1) * 4 * NA].partition_broadcast(128), (), ["bifb"])
    K.dma("sp", "anwb", anwb, d["anw"][l, 512 * s:512 * s + 512].partition_broadcast(128), (), ["anwb"])
    K.ts("dve", anwb, anwb, 0.25, None, ALU.mult, None, ["anwb"], ["anwb"])
    G = 4 * NA
    Gt = K.sb("Gt", [128, NT, G], F32)
    pgt = K.pf[0][:, 0:NT * G]
    for j in range(NT):
        for k in range(8):
            K.mm(pgt[:, j * G:(j + 1) * G], K.hT[:, k, j * 128:(j + 1) * 128], wg[:, k, :], k == 0, k == 7,
                 ["wg"] + hk([j], k), ["pf0"])
    K.tt("dve", Gt, pgt.rearrange("p (j g) -> p j g", g=G), bifb.unsqueeze(1).to_broadcast([128, NT, G]), ALU.add,
         ["pf0", "bifb"], ["Gt"])
    WK = K.sb("WK", [128, 2, NT, NA], F32)
    EC = K.sb("EC", [128, 2, NT, NA], F32)
    decB = K.sb("decB", [128, 2, NA, NCH], F32)
    sh3 = [128, NT, NA]
    gA = K.sb("gA", sh3, F32); gE = K.sb("gE", sh3, F32); gL = K.sb("gL", sh3, F32); Fg = K.sb("Fg", sh3, F32)
    Bs = K.sb("Bs", sh3, F32); U = K.sb("U", sh3, F32); g1 = K.sb("g1", sh3, F32); g2 = K.sb("g2", sh3, F32)
    blastF = K.sb("blastF", [NA, NCH], F32); umaxF = K.sb("umaxF", [NA, NCH], F32); mst = K.sb("mst", [NA, NCH], F32)
    Mc = K.sb("Mc", [NA, NCH], F32); dd = K.sb("dd", [NA, NCH], F32); dec = K.sb("dec", [NA, NCH], F32)
    Rp = [K.sb(f"Rp{i}", [NA, NT, NA], F32) for i in range(2)]
    fl = lambda t_: t_.rearrange("p j h -> p (j h)")
    for D_ in range(2):
        PI = Gt[:, :, 2 * D_ * NA:2 * D_ * NA + NA]
        PF = Gt[:, :, (2 * D_ + 1) * NA:(2 * D_ + 2) * NA]
        tri = cf["trif" if D_ == 0 else "trib"]
        K.stt(gA, PF, -1.0, PF, ALU.mult, ALU.max, ["Gt"], ["gA"])
        K.act(gE, gA, AF.Exp, ["gA"], ["gE"], scale=-1.0)
        K.act(gL, gE, AF.Ln, ["gE"], ["gL"], bias=1.0)
        K.stt(Fg, PF, 0.0, gL, ALU.min, ALU.subtract, ["Gt", "gL"], ["Fg"])
        K.mm(K.pf[1][:, 0:NT * NA], tri, fl(Fg), True, True, ["Fg", "m_trif", "m_trib"], ["pf1"])
        K.cp("dve", fl(Bs), K.pf[1][:, 0:NT * NA], ["pf1"], ["Bs"])
        K.tt("dve", U, PI, Bs, ALU.subtract, ["Gt", "Bs"], ["U"])
        for j in range(NT):
            K.mm(K.pf[2][0:NA, 2 * j:2 * j + 2], Fg[:, j, :], cf["chsel"], True, True, ["Fg", "m_chsel"], ["pf2"])
        K.cp("dve", blastF, K.pf[2][0:NA, 0:NCH], ["pf2"], ["blastF"])
        for jb in range(0, NT, 4):
            nb_ = min(4, NT - jb)
            pu = K.pf[3 + (jb // 4) % 2]
            for q in range(nb_):
                K.tr(pu[0:NA, q * 128:(q + 1) * 128], U[:, jb + q, :], K.identf, ["U", "identf"],
                     [f"pf{3 + (jb // 4) % 2}"])
            K.S.op("dve", lambda h, pu=pu, jb=jb, nb_=nb_: h.tensor_reduce(
                umaxF[:, 2 * jb:2 * jb + 2 * nb_], pu[0:NA, 0:nb_ * 128].rearrange("p (c t) -> p c t", t=64),
                AX.X, ALU.max), [f"pf{3 + (jb // 4) % 2}"], ["umaxF"])
        od = ORD[D_]
        K.memset("dve", mst[:, od[0]:od[0] + 1], 0.0, ["mst"])
        for ix in range(NCH - 1):
            c, nx = od[ix], od[ix + 1]
            K.ts("dve", mst[:, nx:nx + 1], mst[:, c:c + 1], umaxF[:, c:c + 1], blastF[:, c:c + 1], ALU.max, ALU.add,
                 ["mst", "umaxF", "blastF"], ["mst"])
        K.tt("dve", Mc, mst, umaxF, ALU.max, ["mst", "umaxF"], ["Mc"])
        K.tt("dve", dd, mst, Mc, ALU.subtract, ["mst", "Mc"], ["dd"])
        K.act(dec, dd, AF.Exp, ["dd"], ["dec"])
        for h_ in range(NA):
            K.mm(K.pf[5][:, h_ * NCH:(h_ + 1) * NCH], cf["sel"][:, h_, :], dec, True, True, ["dec", "m_sel"], ["pf5"])
        K.cp("dve", decB[:, D_].rearrange("p h c -> p (h c)"), K.pf[5][:, 0:NA * NCH], ["pf5"], [("decB", D_)])
        for pi in range(2):
            K.tt("dve", Rp[pi], cf["dlt"], Mc[:, pi::2].unsqueeze(2).to_broadcast([NA, NT, NA]), ALU.mult,
                 ["Mc", "m_dlt"], [f"Rp{pi}"])
            K.mm(K.pf[1][64 * pi:64 * pi + 64, 0:NT * NA], cf["ones2"], fl(Rp[pi]), True, True,
                 [f"Rp{pi}", "m_ones2"], ["pf1"])
        mt = K.pf[1][:, 0:NT * NA]
        K.tt("dve", fl(g1), fl(U), mt, ALU.subtract, ["U", "pf1"], ["g1"])
        K.act(fl(WK[:, D_]), fl(g1), AF.Exp, ["g1"], [("WK", D_)])
        K.stt(fl(g2), fl(Bs), -1.0, mt, ALU.mult, ALU.subtract, ["Bs", "pf1"], ["g2"])
        K.act(fl(EC[:, D_]), fl(g2), AF.Exp, ["g2"], [("EC", D_)])
    wA = K.sb("wA", [128, 8, 1024], BF16)
    qT = K.sb("mqT", [128, NTOK], BF16)
    kT = K.sb("mkT", [128, NTOK], BF16)
    vaug = K.sb("mvaug", [128, NT, 257], BF16)
    Kp = [K.sb(f"Kp{i}", [128, NT, 128], BF16) for i in range(2)]
    hacc = K.sb("hacc", [128, NT, 256], F32)
    Cst = K.sb("Cst", [128, 257], F32)
    Cdb = [K.sb(f"Cdb{i}", [128, 257], BF16) for i in range(2)]
    aT = [K.sb(f"maT{i}", [128, 64], BF16) for i in range(2)]
    dm = K.sb("mdm", [128, 2], F32)
    tho = K.sb("tho", [128, 512], F32)
    w1 = K.sb("mw1", [128, 256], F32)
    w2 = K.sb("mw2", [128, 256], F32)
    y1 = K.sb("my1", [128, 256], F32)
    y2 = K.sb("my2", [128, 256], BF16)
    junk = K.sb("mjunk", [128, 256], F32)
    ssA = K.sb("ssA", [128, NT], F32)
    rsA = K.sb("rsA", [128, NT], F32)
    ybuf = [K.sb(f"mybuf{i}", [128, 2, 128], BF16) for i in range(2)]
    K.memset("pool", vaug[:, :, 256:257], 1.0, T("mvaug", range(NT)))
    TGS = [(t0, min(512, NTOK - t0)) for t0 in range(0, NTOK, 512)]
    mask = (cf["maskf"], cf["maskb"])
    for i in range(NA):
        K.dma("pool", "wA", wA, wslice(K, l, s * SUBW + i * 1024, 1024), (), ["wA"], max_dma_last_dim=4096)
        for gi, (t0, n) in enumerate(TGS):
            tl = tiles_of(t0, n)
            for k in range(8):
                K.mm(K.pf[0][:, 0:n], wA[:, k, 0:128], K.hT[:, k, t0:t0 + n], k == 0, k == 7, ["wA"] + hk(tl, k),
                     ["pf0"])
            K.cp("act", qT[:, t0:t0 + n], K.pf[0][:, 0:n], ["pf0"], T("mqT", tl))
            for k in range(8):
                K.mm(K.pf[1][:, 0:n], wA[:, k, 128:256], K.hT[:, k, t0:t0 + n], k == 0, k == 7, ["wA"] + hk(tl, k),
                     ["pf1"])
            K.ts("dve", kT[:, t0:t0 + n], K.pf[1][:, 0:n], 128 ** -0.5, None, ALU.mult, None, ["pf1"], T("mkT", tl))
        for j in range(NT):
            pkv = K.pf[2 + j % 2]
            for k in range(8):
                K.mm(pkv[:, 0:384], K.hT[:, k, j * 128:(j + 1) * 128], wA[:, k, 128:512], k == 0, k == 7,
                     ["wA"] + hk([j], k), [f"pf{2 + j % 2}"])
            for D_ in range(2):
                K.ts("dve", Kp[D_][:, j, :], pkv[:, 0:128], WK[:, D_, j, i:i + 1], 128 ** -0.5, ALU.mult, ALU.mult,
                     [f"pf{2 + j % 2}", ("WK", D_)], [("Kp", D_, j)])
            K.cp("dve", vaug[:, j, 0:256], pkv[:, 128:384], [f"pf{2 + j % 2}"], [("mvaug", j)])
        it = 0
        for D_ in range(2):
            K.memset("pool", Cst, 0.0, ["Cst"])
            for c in ORD[D_]:
                j, pi = divmod(c, 2)
                rows = slice(64 * pi, 64 * pi + 64)
                t0 = 64 * c
                b2 = it % 2
                dsc = decB[:, D_, i, c:c + 1]
                K.ts("pool", Cdb[b2], Cst, dsc, None, ALU.mult, None, ["Cst", ("decB", D_)], [f"Cdb{b2}"])
                pqk = K.pf[b2]
                K.mm(pqk[rows, 0:64], kT[:, t0:t0 + 64], qT[:, t0:t0 + 64], True, True, [("mkT", j), ("mqT", j)],
                     [f"pf{b2}"])
                K.stt(aT[b2][rows, :], pqk[rows, 0:64], WK[rows, D_, j, i:i + 1], mask[D_][rows, :], ALU.mult,
                      ALU.mult, [f"pf{b2}", ("WK", D_), "m_maskf", "m_maskb"], [f"maT{b2}"])
                pnum = K.pf[2 + b2]
                K.mm(pnum[rows, 0:257], aT[b2][rows, :], vaug[rows, j, :], True, False, [f"maT{b2}", ("mvaug", j)],
                     [f"pf{2 + b2}"])
                K.mm(pnum[rows, 0:257], qT[:, t0:t0 + 64], Cdb[b2], False, True, [("mqT", j), f"Cdb{b2}"],
                     [f"pf{2 + b2}"])
                pdc = K.pf[4 + b2]
                K.mm(pdc[:, 0:257], Kp[D_][rows, j, :], vaug[rows, j, :], True, True, [("Kp", D_, j), ("mvaug", j)],
                     [f"pf{4 + b2}"])
                K.stt(Cst, Cst, dsc, pdc[:, 0:257], ALU.mult, ALU.add, ["Cst", ("decB", D_), f"pf{4 + b2}"], ["Cst"])
                K.ts("dve", dm[rows, 0:1], pnum[rows, 256:257], EC[rows, D_, j, i:i + 1], None, ALU.max, None,
                     [f"pf{2 + b2}", ("EC", D_)], ["mdm"])
                K.stt(dm[rows, 0:1], pnum[rows, 256:257], -1.0, dm[rows, 0:1], ALU.mult, ALU.max,
                      [f"pf{2 + b2}", "mdm"], ["mdm"])
                K.S.op("dve", lambda h, rows=rows: h.reciprocal(dm[rows, 1:2], dm[rows, 0:1]), ["mdm"], ["mdm"])
                if D_ == 0:
                    K.act(hacc[rows, j, :], pnum[rows, 0:256], AF.Copy, [f"pf{2 + b2}", "mdm"], [("hacc", j)],
                          scale=dm[rows, 1:2])
                else:
                    K.stt(hacc[rows, j, :], pnum[rows, 0:256], dm[rows, 1:2], hacc[rows, j, :], ALU.mult, ALU.add,
                          [f"pf{2 + b2}", "mdm", ("hacc", j)], [("hacc", j)])
                it += 1
        for j in range(NT):
            K.act(junk, hacc[:, j, :], AF.Square, [("hacc", j)], ["mjunk"])
            K.S.op("dve", lambda h, j=j: h.tensor_reduce(ssA[:, j:j + 1], junk, AX.X, ALU.add), ["mjunk"], [("ssA", j)])
            K.rstd(rsA[:, j:j + 1], ssA[:, j:j + 1], 256, [("ssA", j)], [("rsA", j)])
            poz = K.pf[j % 2]
            for k in range(8):
                K.mm(poz[:, 0:512], K.hT[:, k, j * 128:(j + 1) * 128], wA[:, k, 512:1024], k == 0, k == 7,
                     ["wA"] + hk([j], k), [f"pf{j % 2}"])
            K.act(tho, poz[:, 0:512], AF.Tanh, [f"pf{j % 2}"], ["tho"], scale=0.5)
            K.stt(w1, tho[:, 256:512], 1.0, poz[:, 256:512], ALU.add, ALU.mult, ["tho", f"pf{j % 2}"], ["mw1"])
            K.stt(w2, tho[:, 0:256], 1.0, w1, ALU.add, ALU.mult, ["tho", "mw1"], ["mw2"])
            K.stt(y1, hacc[:, j, :], rsA[:, j:j + 1], anwb[:, i * 256:(i + 1) * 256], ALU.mult, ALU.mult,
                  [("hacc", j), ("rsA", j), "anwb"], ["my1"])
            K.tt("dve", y2, y1, w2, ALU.mult, ["my1", "mw2"], ["my2"])
            pt = K.pT[j % 2]
            for c2 in range(2):
                K.tr(pt[:, c2 * 128:(c2 + 1) * 128], y2[:, c2 * 128:(c2 + 1) * 128], K.ident, ["my2", "ident"],
                     [f"pT{j % 2}"])
            yb = ybuf[j % 2]
            K.cp("dve", yb.rearrange("p c t -> p (c t)"), pt[:, 0:256], [f"pT{j % 2}"], [f"mybuf{j % 2}"])
            K.dma("sp", f"mybuf{j % 2}", d["yT"][0, 2 * i:2 * i + 2, :, j * 128:(j + 1) * 128].rearrange("c p t -> p c t"),
                  yb, [f"mybuf{j % 2}"], [("yT", 0, i, j)])


def pair_exchange(K, tg):
    d = K.d
    groups = [[0, 1], [2, 3], [4, 5], [6, 7]]
    r0 = tg * 256
    rk = [("xp", 2 * tg), ("xp", 2 * tg + 1)]
    wk = [("xd", 2 * tg), ("xd", 2 * tg + 1)]
    K.S.dma("pool", "cc", lambda h: h.collective_compute("AllReduce", ALU.add, replica_groups=groups,
                                                         ins=[d["xpart"][r0:r0 + 256, :].opt()],
                                                         outs=[d["xcur"][r0:r0 + 256, :].opt()]),
            rk, wk, inc=1)
```

```python
import math
from contextlib import ExitStack

import numpy as np
import concourse.bass as bass
import concourse.mybir as mybir
from concourse.bass_utils import run_bass_kernel_spmd

F32 = mybir.dt.float32
BF16 = mybir.dt.bfloat16
AF = mybir.ActivationFunctionType
ALU = mybir.AluOpType
AX = mybir.AxisListType

D = 1024
NCTX = 256
NLAT = 4096
NTOK = NCTX + NLAT
NT = NTOK // 128
NCH = NTOK // 64
DEPTH = 4
EPS = 1e-6
NSUB = 2
NA = 2
NB = 4
NG = 2
SUBW = 5128
SB_BASE = 16640
SB_END = 229376


class Slot:
    def __init__(self, S, name):
        self.sem = S.new_sem(name)
        self.total = 0


class Sched:
    ENG = ("pe", "act", "dve", "pool", "sp")

    def __init__(self, nc, stack):
        self.nc = nc
        self.stack = stack
        self.ops = {e: [] for e in self.ENG}
        self.sem = {e: self.new_sem("s_" + e) for e in self.ENG}
        self.cnt = {e: 0 for e in self.ENG}
        self.waited = {}
        self.lastw = {}
        self.reads = {}
        self.slots = {}
        self.same_engine_sync = True

    def new_sem(self, name):
        return self.stack.enter_context(self.nc.semaphore(name))

    def slot(self, name):
        if name not in self.slots:
            self.slots[name] = Slot(self, "d_" + name)
        return self.slots[name]

    def _need(self, eng, toks, tok):
        if tok is None:
            return
        sem, val, src = tok
        if src == eng and (eng == "pe" or not self.same_engine_sync):
            return
        if self.waited.get((eng, id(sem)), 0) >= val:
            return
        cur = toks.get(id(sem))
        if cur is None or cur[1] < val:
            toks[id(sem)] = (sem, val)

    def _deps(self, eng, reads, writes):
        toks = {}
        for k in reads:
            self._need(eng, toks, self.lastw.get(k))
        for k in writes:
            self._need(eng, toks, self.lastw.get(k))
            for t in self.reads.get(k, ()):
                self._need(eng, toks, t)
        waits = list(toks.values())
        for sem, val in waits:
            self.waited[(eng, id(sem))] = val
        return waits

    def _commit(self, tok, reads, writes):
        for k in reads:
            self.reads.setdefault(k, []).append(tok)
        for k in writes:
            self.lastw[k] = tok
            self.reads[k] = []

    def op(self, eng, fn, reads=(), writes=()):
        waits = self._deps(eng, reads, writes)
        self.cnt[eng] += 1
        tok = (self.sem[eng], self.cnt[eng], eng)
        self.ops[eng].append((waits, fn, (self.sem[eng], 1)))
        self._commit(tok, reads, writes)
        return tok

    def dma(self, eng, slotname, fn, reads=(), writes=(), inc=16):
        slot = self.slot(slotname)
        waits = self._deps(eng, reads, writes)
        slot.total += inc
        tok = (slot.sem, slot.total, "dma")
        self.ops[eng].append((waits, fn, (slot.sem, inc)))
        self._commit(tok, reads, writes)
        return tok

    def barrier(self):
        toks = [(self.sem[e], self.cnt[e], "x") for e in self.ENG if self.cnt[e] > 0]
        toks += [(s.sem, s.total, "dma") for s in self.slots.values() if s.total > 0]
        for e in self.ENG:
            need = {}
            for t in toks:
                if t[0] is self.sem[e]:
                    continue
                self._need(e, need, t)
            waits = list(need.values())
            for sem, val in waits:
                self.waited[(e, id(sem))] = val
            if waits:
                self.ops[e].append((waits, None, None))

    def emit(self):
        nc = self.nc
        with nc.Block() as block:
            def mk(e):
                def body(h):
                    for waits, fn, inc in self.ops[e]:
                        for sem, val in waits:
                            h.wait_ge(sem, val)
                        if fn is not None:
                            fn(h).then_inc(inc[0], inc[1])
                return body
            block.tensor(mk("pe"))
            block.scalar(mk("act"))
            block.vector(mk("dve"))
            block.gpsimd(mk("pool"))
            block.sync(mk("sp"))


def T(name, idxs):
    return [(name, i) for i in idxs]


def hk(js, k):
    return [("hT", j, k) for j in js]


def tiles_of(t0, n):
    return range(t0 // 128, (t0 + n + 127) // 128)


def make_consts():
    c = {}
    p = np.arange(128)
    c["ident"] = np.eye(128, dtype=np.float32)
    tok = (np.arange(32)[None, :] * 128 + p[:, None]).astype(np.float32)
    row = np.floor(tok / 64.0).astype(np.float32)
    col = (tok - row * 64.0).astype(np.float32)
    inv = (10000.0 ** (-np.arange(0, 32, 2, dtype=np.float32) / 32.0)).astype(np.float32)
    ar = (row[..., None] * inv).astype(np.float32)
    ac = (col[..., None] * inv).astype(np.float32)
    cosT = np.concatenate([np.cos(ar), np.cos(ar), np.cos(ac), np.cos(ac)], axis=-1)
    sinT = np.concatenate([-np.sin(ar), np.sin(ar), -np.sin(ac), np.sin(ac)], axis=-1)
    c["ropec"] = cosT.astype(np.float32)
    c["ropes"] = sinT.astype(np.float32)
    s = p % 64
    t = np.arange(64)
    c["maskf"] = (s[:, None] <= t[None, :]).astype(np.float32)
    c["maskb"] = (s[:, None] >= t[None, :]).astype(np.float32)
    same = (p[:, None] // 64) == (p[None, :] // 64)
    c["trif"] = (same & (p[:, None] <= p[None, :])).astype(np.float32)
    c["trib"] = (same & (p[:, None] >= p[None, :])).astype(np.float32)
    c["chsel"] = ((p[:, None] // 64) == np.arange(2)[None, :]).astype(np.float32)
    sel = np.zeros((NA, NA, 128), np.float32)
    for h in range(NA):
        sel[h, h, :] = 1.0
    c["sel"] = sel
    dl = np.zeros((NA, NT, NA), np.float32)
    for h in range(NA):
        dl[h, :, h] = 1.0
    c["dlt"] = dl
    c["ones2"] = np.ones((NA, 64), np.float32)
    d = np.arange(256)
    ang = 2 * np.pi * np.outer(d, d) / 256.0
    f256 = np.concatenate([np.cos(ang), -np.sin(ang)], axis=1) / 16.0
    c["f256"] = f256.reshape(2, 128, 512).transpose(1, 0, 2).astype(np.float32)
    n = np.arange(64)
    a64 = 2 * np.pi * np.outer(n, n) / 64.0
    C64, S64 = np.cos(a64) / 8.0, np.sin(a64) / 8.0
    c["rb1"] = np.concatenate([C64, -S64], axis=1).astype(np.float32)
    c["rb2"] = np.concatenate([S64, C64], axis=1).astype(np.float32)
    n2 = (p % 64)[:, None, None]
    k1 = np.arange(64)[None, :, None]
    k2 = np.arange(64)[None, None, :]
    ag = 2 * np.pi * (n2 * (k1 + 64 * k2) % 4096) / 4096.0
    c["gc2"] = (np.cos(ag) / 16.0).astype(np.float32)
    c["gs2"] = (np.sin(ag) / 16.0).astype(np.float32)
    nn = np.arange(256)
    a256 = 2 * np.pi * (np.outer(nn, nn) % 256) / 256.0
    c["cc"] = (np.cos(a256) / 32.0).reshape(2, 128, 256).transpose(1, 0, 2).astype(np.float32)
    c["sc"] = (np.sin(a256) / 32.0).reshape(2, 128, 256).transpose(1, 0, 2).astype(np.float32)
    return c


def win_cols(subsets):
    cols = []
    for s in subsets:
        for i in range(NA):
            h = NA * s + i
            cols += list(range(0 + 128 * h, 128 * h + 128))
            cols += list(range(512 + 128 * h, 512 + 128 * h + 128))
            cols += list(range(1024 + 256 * h, 1024 + 256 * h + 256))
            cols += list(range(2064 + 256 * h, 2064 + 256 * h + 256))
            cols += list(range(3088 + 256 * h, 3088 + 256 * h + 256))
        for kind in range(4):
            for i in range(NA):
                cols.append(2048 + kind * 4 + NA * s + i)
        for i in range(NB):
            h = NB * s + i
            cols += list(range(4112 + 128 * h, 4112 + 128 * h + 128))
            cols += list(range(5136 + 128 * h, 5136 + 128 * h + 128))
            cols += list(range(6160 + 128 * h, 6160 + 128 * h + 128))
            cols += list(range(7184 + 128 * h, 7184 + 128 * h + 128))
        for i in range(NG):
            g = NG * s + i
            cols += list(range(8208 + 256 * g, 8208 + 256 * g + 256))
            cols += list(range(9232 + 256 * g, 9232 + 256 * g + 256))
    cols += list(range(10256, 13328))
    assert len(cols) == len(subsets) * SUBW + 3 * D
    return np.asarray(cols)


def gate_bias_idx(subsets):
    idx = []
    for s in subsets:
        for kind in range(4):
            for i in range(NA):
                idx.append(kind * 4 + NA * s + i)
    return np.asarray(idx)


def feat_rows(subsets):
    return np.concatenate([np.arange(512 * s, 512 * s + 512) for s in subsets])


class StopBuild(Exception):
    pass


class Ctx:
    def cut(self, name):
        if self.cfg.get("cut") == name:
            raise StopBuild()

    def __init__(self, nc, st, cfg):
        self.nc = nc
        self.st = st
        self.cfg = cfg
        self.S = Sched(nc, st)
        self.pair = bool(cfg.get("pair", False))
        self.nsub = 1 if self.pair else NSUB
        self.mg0 = self.nsub * SUBW
        self.pers = SB_BASE
        self.scr0 = None
        self.scr = None
        self.uid = 0
        self.d = {}
        self.ps = {}

    def _alloc(self, name, shape, dt, off):
        self.uid += 1
        t = self.nc.alloc_sbuf_tensor_at(f"{name}_{self.uid}", list(shape), dt, offset=off)
        return t.ap()

    @staticmethod
    def _bytes(shape, dt):
        n = 1
        for s in shape[1:]:
            n *= s
        b = n * (2 if dt == BF16 else 4)
        return (b + 63) // 64 * 64

    def pb(self, name, shape, dt):
        assert self.scr0 is None
        off = self.pers
        self.pers += self._bytes(shape, dt)
        assert self.pers <= SB_END, "persistent SBUF overflow"
        return self._alloc(name, shape, dt, off)

    def freeze(self):
        self.scr0 = self.pers
        self.scr = self.scr0

    def reset(self):
        self.S.barrier()
        self.scr = self.scr0

    def sb(self, name, shape, dt):
        off = self.scr
        self.scr += self._bytes(shape, dt)
        assert self.scr <= SB_END, f"scratch SBUF overflow at {name}: {self.scr - SB_END}"
        return self._alloc(name, shape, dt, off)

    def mm(self, out, lhsT, rhs, start, stop, r, w, **kw):
        return self.S.op("pe", lambda h: h.matmul(out, lhsT, rhs, start=start, stop=stop, **kw), r, w)

    def tr(self, out, in_, ident, r, w):
        return self.S.op("pe", lambda h: h.transpose(out, in_, ident), r, w)

    def act(self, out, in_, func, r, w, scale=1.0, bias=0.0, accum_out=None):
        if accum_out is None:
            return self.S.op("act", lambda h: h.activation(out=out, in_=in_, func=func, bias=bias, scale=scale), r, w)
        return self.S.op("act", lambda h: h.activation(out=out, in_=in_, func=func, bias=bias, scale=scale,
                                                       accum_out=accum_out), r, w)

    def tt(self, eng, out, in0, in1, op, r, w):
        return self.S.op(eng, lambda h: h.tensor_tensor(out, in0, in1, op), r, w)

    def ts(self, eng, out, in0, s1, s2, op0, op1, r, w):
        if s2 is None:
            return self.S.op(eng, lambda h: h.tensor_scalar(out, in0, s1, None, op0), r, w)
        return self.S.op(eng, lambda h: h.tensor_scalar(out, in0, s1, s2, op0, op1), r, w)

    def stt(self, out, in0, scalar, in1, op0, op1, r, w, accum_out=None):
        if accum_out is None:
            return self.S.op("dve", lambda h: h.scalar_tensor_tensor(out, in0, scalar, in1, op0, op1), r, w)
        return self.S.op("dve", lambda h: h.scalar_tensor_tensor(out, in0, scalar, in1, op0, op1,
                                                                 accum_out=accum_out), r, w)

    def cp(self, eng, out, in_, r, w):
        if eng == "act":
            return self.S.op("act", lambda h: h.copy(out, in_), r, w)
        return self.S.op(eng, lambda h: h.tensor_copy(out, in_), r, w)

    def memset(self, eng, ap, val, w):
        return self.S.op(eng, lambda h: h.memset(ap, val), (), w)

    def dma(self, eng, slot, out, in_, r, w, **kw):
        return self.S.dma(eng, slot, lambda h: h.dma_start(out=out, in_=in_, **kw), r, w)

    def rstd(self, out, ss, n, r, w):
        self.ts("dve", out, ss, 1.0 / n, EPS, ALU.mult, ALU.add, r, w)
        self.act(out, out, AF.Sqrt, w, w)
        return self.S.op("dve", lambda h: h.reciprocal(out, out), w, w)


def wslice(K, l, c0, n):
    return K.d["win"][l].rearrange("(k p) n -> p k n", p=128)[:, :, c0:c0 + n]


CONST_SHAPES = dict(ident=[128, 128], ropec=[128, 32, 64], ropes=[128, 32, 64], maskf=[128, 64], maskb=[128, 64],
                    trif=[128, 128], trib=[128, 128], chsel=[128, 2], sel=[NA, NA, 128], dlt=[NA, NT, NA],
                    ones2=[NA, 64], f256=[128, 2, 512], rb1=[64, 128], rb2=[64, 128], gc2=[128, 64, 64],
                    gs2=[128, 64, 64], cc=[128, 2, 256], sc=[128, 2, 256])


def declare_dram(K):
    nc, d = K.nc, K.d
    DEPTH = K.cfg.get("layers", 4)
    inp = lambda n, s: nc.dram_tensor(n, list(s), F32, kind="ExternalInput").ap()
    d["xin"] = inp("xin", [NTOK, D])
    d["cvec"] = inp("cvec", [128, 8, 2])
    d["normw"] = inp("normw", [DEPTH, 128, 8])
    d["wada"] = inp("wada", [DEPTH, D, 3 * D])
    d["bada"] = inp("bada", [DEPTH, 128, 24])
    ns = K.nsub
    d["win"] = inp("win", [DEPTH, D, ns * SUBW + 3 * D])
    d["bif"] = inp("bif", [DEPTH, ns * 4 * NA])
    d["anw"] = inp("anw", [DEPTH, ns * 512])
    d["qknw"] = inp("qknw", [DEPTH, 256])
    d["lam4"] = inp("lam4", [DEPTH, 256])
    d["subw"] = inp("subw", [DEPTH, 128])
    for n in ("wao", "wbo", "wco"):
        d[n] = inp(n, [DEPTH, ns * 512, D])
    d["wo"] = inp("wo", [DEPTH, D, D])
    for n, s in CONST_SHAPES.items():
        d["c_" + n] = inp("c_" + n, s)
    d["out"] = nc.dram_tensor("out", [NLAT, D], F32, kind="ExternalOutput").ap()
    d["xctx"] = nc.dram_tensor("xctx", [NCTX, D], F32, kind="Internal").ap()
    d["yT"] = nc.dram_tensor("yT", [3, 4, 128, NTOK], BF16, kind="Internal").ap()
    d["gsc"] = nc.dram_tensor("gsc", [2, D], F32, kind="Internal").ap()
    if K.pair:
        d["xpart"] = nc.dram_tensor("xpart", [NTOK, D], F32, kind="Internal").ap()
        d["xcur"] = nc.dram_tensor("xcur", [NTOK, D], F32, kind="Internal").ap()
    if K.cfg.get("dbg"):
        d["dbg_hT"] = nc.dram_tensor("dbg_hT", [128, 8, NTOK], BF16, kind="ExternalOutput").ap()
        d["dbg_yT"] = nc.dram_tensor("dbg_yT", [K.nsub, 3, 4, 128, NTOK], BF16, kind="ExternalOutput").ap()
        d["dbg_ctx"] = nc.dram_tensor("dbg_ctx", [NCTX, D], F32, kind="ExternalOutput").ap()


def xrows(K, l, j, write=False):
    if K.pair:
        if write:
            return K.d["xpart"][j * 128:(j + 1) * 128, :], ("xp", j)
        src = K.d["xin"] if l == 0 else K.d["xcur"]
        return src[j * 128:(j + 1) * 128, :], ("xd", j)
    if j < 2:
        src = K.d["xin"] if (l == 0 and not write) else K.d["xctx"]
        return src[j * 128:(j + 1) * 128, :], ("xd", j)
    jj = j - 2
    if l == 0 and not write:
        return K.d["xin"][NCTX + jj * 128:NCTX + (jj + 1) * 128, :], ("xd", j)
    return K.d["out"][jj * 128:(jj + 1) * 128, :], ("xd", j)


def setup(K):
    S, d = K.S, K.d
    K.hT = K.pb("hT", [128, 8, NTOK], BF16)
    K.ident = K.pb("ident", [128, 128], BF16)
    K.identf = K.pb("identf", [128, 128], F32)
    K.c_mhalf = K.pb("mhalf", [128, 64], F32)
    K.scT = K.pb("scT", [128, 8, 2], BF16)
    K.A1 = K.pb("A1", [128, 8, 2], F32)
    K.B1 = K.pb("B1", [128, 8, 2], F32)
    K.gateb = K.pb("gateb", [128, 2, D], F32)
    K.lam = K.pb("lam", [128, 1], F32)
    K.freeze()
    nc = K.nc
    K.pT = [K.st.enter_context(nc.psum_tensor(f"pT{i}", [128, 1024], BF16)) for i in range(2)]
    K.pf = [K.st.enter_context(nc.psum_tensor(f"pf{i}", [128, 512], F32)) for i in range(6)]
    K.dma("pool", "c0", K.ident, d["c_ident"], (), ["ident"])
    K.dma("sp", "c1", K.identf, d["c_ident"], (), ["identf"])
    K.memset("dve", K.c_mhalf, -0.5, ["mhalf"])
    cv = K.sb("cv", [128, 8, 2], F32)
    th = K.sb("cth", [128, 8, 2], F32)
    K.dma("sp", "c2", cv, d["cvec"], (), ["cv"])
    K.act(th, cv, AF.Tanh, ["cv"], ["cth"], scale=0.5)
    K.stt(th, th, 1.0, cv, ALU.add, ALU.mult, ["cth", "cv"], ["cth"])
    K.ts("dve", K.scT, th, 0.5, None, ALU.mult, None, ["cth"], ["scT"])


def phase_norm(K, l):
    S, d = K.S, K.d
    K.reset()
    normw = K.sb("normw", [128, 8], F32)
    bada = K.sb("bada", [128, 24], F32)
    modT = K.sb("modT", [128, 24, 2], F32)
    wad = K.sb("wad", [128, 8, D], BF16)
    K.dma("sp", "sv0", normw, d["normw"][l], (), ["normw"])
    K.dma("sp", "sv1", bada, d["bada"][l], (), ["bada"])
    pm = K.pf[0][:, 0:48]
    wv = d["wada"][l].rearrange("(k p) n -> p k n", p=128)
    for third in range(3):
        K.dma("pool", "wad", wad, wv[:, :, third * D:(third + 1) * D], (), ["wad"], max_dma_last_dim=4096)
        for m in range(8):
            g = third * 8 + m
            for k in range(8):
                K.mm(pm[:, 2 * g:2 * g + 2], wad[:, k, m * 128:(m + 1) * 128], K.scT[:, k, :], k == 0, k == 7,
                     ["wad", "scT"], ["pf0"])
    K.tt("dve", modT, K.pf[0][:, 0:48].rearrange("p (g j) -> p g j", j=2),
         bada.unsqueeze(2).to_broadcast([128, 24, 2]), ALU.add, ["pf0", "bada"], ["modT"])
    K.stt(K.A1, modT[:, 8:16, :], 1.0, normw.unsqueeze(2).to_broadcast([128, 8, 2]), ALU.add, ALU.mult,
          ["modT", "normw"], ["A1"])
    K.cp("dve", K.B1, modT[:, 0:8, :], ["modT"], ["B1"])
    gt = K.sb("gt", [128, 2, 8], F32)
    gtT = K.sb("gtT", [8, 2, 128], F32)
    for j in range(2):
        K.ts("dve", gt[:, j, :], modT[:, 16:24, j], 0.5, None, ALU.mult, None, ["modT"], ["gt"])
    for j in range(2):
        K.tr(K.pf[1][0:8, j * 128:(j + 1) * 128], gt[:, j, :], K.identf, ["gt", "identf"], ["pf1"])
    K.cp("dve", gtT, K.pf[1][0:8, 0:256].rearrange("p (j f) -> p j f", j=2), ["pf1"], ["gtT"])
    K.dma("sp", "gsc", d["gsc"].rearrange("j (k p) -> k j p", p=128), gtT, ["gtT"], ["gsc"])
    for j in range(2):
        K.dma("sp", "gsc", K.gateb[:, j, :], d["gsc"][j].partition_broadcast(128), ["gsc"], ["gateb"])
    K.cut("p0")
    NXB = 3
    xt = [K.sb(f"xt{i}", [128, D], F32) for i in range(NXB)]
    xn = [K.sb(f"xn{i}", [128, D], BF16) for i in range(2)]
    junk = K.sb("junk", [128, D], F32)
    ss = K.sb("ss", [128, NT], F32)
    rs = K.sb("rs", [128, NT], F32)
    for j in range(NT):
        b = j % NXB
        src, xkey = xrows(K, l, j)
        K.dma("sp", f"xt{b}", xt[b], src, [xkey], [f"xt{b}"])
        K.act(junk, xt[b], AF.Square, [f"xt{b}"], ["junk"])
        K.S.op("dve", lambda h, j=j: h.tensor_reduce(ss[:, j:j + 1], junk, AX.X, ALU.add), ["junk"], [("ss", j)])
        K.cut("p1a")
        K.rstd(rs[:, j:j + 1], ss[:, j:j + 1], D, [("ss", j)], [("rs", j)])
        K.cut("p1b")
        nb = j % 2
        K.ts("dve", xn[nb], xt[b], rs[:, j:j + 1], None, ALU.mult, None, [f"xt{b}", ("rs", j)], [f"xn{nb}"])
        K.cut("p1c")
        for k in range(8):
            K.tr(K.pT[k // 4][:, (k % 4) * 128:(k % 4 + 1) * 128], xn[nb][:, k * 128:(k + 1) * 128], K.ident,
                 [f"xn{nb}", "ident"], [f"pT{k // 4}"])
        K.cut("p1d")
        jc = 1 if j < 2 else 0
        for k in range(8):
            o = K.hT[:, k, j * 128:(j + 1) * 128]
            i_ = K.pT[k // 4][:, (k % 4) * 128:(k % 4 + 1) * 128]
            if k < 4:
                K.act(o, i_, AF.Identity, [f"pT{k // 4}", "A1", "B1"], [("hT", j, k)], scale=K.A1[:, k, jc:jc + 1],
                      bias=K.B1[:, k, jc:jc + 1])
            else:
                K.ts("dve", o, i_, K.A1[:, k, jc:jc + 1], K.B1[:, k, jc:jc + 1], ALU.mult, ALU.add,
                     [f"pT{k // 4}", "A1", "B1"], [("hT", j, k)])
            if k == 0:
                K.cut("p1k0")
            if k == 1:
                K.cut("p1k1")
        K.cut(f"p1e{j}")
    K.cut("p1f")


def finish(K):
    K.S.barrier()


def build(cfg):
    nc = bass.Bass("TRN2", target_bir_lowering=False)
    with ExitStack() as st:
        K = Ctx(nc, st, cfg)
        declare_dram(K)
        try:
            build_body(K, cfg)
        except StopBuild:
            pass
        finish(K)
        K.S.emit()
    return nc


def build_body(K, cfg):
    if True:
        setup(K)
        K.cut("setup")
        for l in range(cfg.get("layers", DEPTH)):
            phase_norm(K, l)
            if cfg.get("dbg") == "norm":
                K.dma("sp", "dbg", K.d["dbg_hT"], K.hT, [("hT", j, k) for j in range(NT) for k in range(8)], ["dbg"])
                break
            for s in range(K.nsub):
                if "mlstm" not in cfg.get("skip", ()):
                    phase_mlstm(K, l, s)
                if "attn" not in cfg.get("skip", ()):
                    phase_attn(K, l, s)
                if "fft" not in cfg.get("skip", ()):
                    phase_fft(K, l, s)
                if cfg.get("dbg"):
                    K.S.barrier()
                    K.dma("sp", "dbg", K.d["dbg_yT"][s], K.d["yT"], ["yT"], ["dbg"])
                if "merge" not in cfg.get("skip", ()):
                    phase_merge(K, l, s)
            if K.pair and l == cfg.get("layers", DEPTH) - 1:
                K.dma("sp", "fin", K.d["out"], K.d["xcur"][NCTX:NTOK, :], [("xd", j) for j in range(2, NT)], ["outfin"])
        if cfg.get("dbg"):
            K.S.barrier()
            K.dma("sp", "dbg", K.d["dbg_ctx"], K.d["xctx"], [("xd", 0), ("xd", 1)], ["dbg"])


def prep_shared(inp, DEPTH=DEPTH, subsets=(0, 1)):
    f = lambda a: np.ascontiguousarray(np.asarray(a, dtype=np.float32))
    inp = {k: (np.asarray(v)[:DEPTH] if k not in ("x", "c", "ctx", "c_ctx") else v) for k, v in inp.items()}
    subsets = list(subsets)
    sh = {}
    sh["normw"] = f(inp["norm_w"].reshape(DEPTH, 8, 128).transpose(0, 2, 1))
    sh["wada"] = f(inp["w_ada"])
    sh["bada"] = f(inp["b_ada"].reshape(DEPTH, 24, 128).transpose(0, 2, 1))
    sh["win"] = f(inp["w_in"][:, :, win_cols(subsets)])
    sh["bif"] = f(inp["b_if"][:, gate_bias_idx(subsets)])
    fr = feat_rows(subsets)
    sh["anw"] = f(inp["a_norm_w"][:, fr])
    sh["qknw"] = f(np.concatenate([inp["q_norm_w"], inp["q_norm_w"], inp["k_norm_w"], inp["k_norm_w"]], axis=1))
    sh["lam4"] = f(np.concatenate([inp["lambda_q1"], inp["lambda_k1"], inp["lambda_q2"], inp["lambda_k2"]], axis=1))
    sh["subw"] = f(inp["subln_w"])
    sh["wao"] = f(inp["w_a_out"][:, fr]); sh["wbo"] = f(inp["w_b_out"][:, fr]); sh["wco"] = f(inp["w_c_out"][:, fr])
    sh["wo"] = f(inp["w_out"])
    for n, v in make_consts().items():
        assert list(v.shape) == CONST_SHAPES[n], (n, v.shape)
        sh["c_" + n] = f(v)
    return sh


def prep_core(inp, b):
    m = {}
    m["xin"] = np.ascontiguousarray(np.concatenate([inp["ctx"][b], inp["x"][b]], axis=0).astype(np.float32))
    cv = np.stack([np.asarray(inp["c"][b]), np.asarray(inp["c_ctx"])], axis=-1)
    m["cvec"] = np.ascontiguousarray(cv.reshape(8, 128, 2).transpose(1, 0, 2).astype(np.float32))
    return m


_NC_CACHE = {}


PAIR = True


def kernel(**inputs):
    cfg = {"pair": PAIR}
    if "full" not in _NC_CACHE:
        _NC_CACHE["full"] = build(cfg)
    nc = _NC_CACHE["full"]
    in_maps = []
    if PAIR:
        shs = [prep_shared(inputs, DEPTH, (s,)) for s in range(NSUB)]
        cores = [prep_core(inputs, b) for b in range(4)]
        for core in range(8):
            m = dict(shs[core % 2])
            m.update(cores[core // 2])
            in_maps.append(m)
        res = run_bass_kernel_spmd(nc, in_maps, core_ids=list(range(8)))
        out = np.stack([np.asarray(res.results[2 * b]["out"]) for b in range(4)], axis=0)
    else:
        sh = prep_shared(inputs)
        for core in range(8):
            m = dict(sh)
            m.update(prep_core(inputs, core % 4))
            in_maps.append(m)
        res = run_bass_kernel_spmd(nc, in_maps, core_ids=list(range(8)))
        out = np.stack([np.asarray(res.results[b]["out"]) for b in range(4)], axis=0)
    return out.astype(np.float32)


def phase_merge(K, l, s):
    S, d = K.S, K.d
    last = (l == K.cfg.get("nlayers_total", DEPTH) - 1)
    K.reset()
    TG = 256
    wmg = K.sb("wmg", [128, 8, 3 * D], BF16)
    wbr = [K.sb(f"wbr{b}", [128, 4, D], BF16) for b in range(3)]
    wo = K.sb("wo", [128, 8, D], BF16)
    for b in range(3):
        K.dma("pool", f"wmg{b}", wmg[:, :, b * D:(b + 1) * D], wslice(K, l, K.mg0 + b * D, D), (), [("wmg", b)],
              max_dma_last_dim=4096)
        srcw = d[("wao", "wbo", "wco")[b]][l][512 * s:512 * s + 512, :].rearrange("(k p) n -> p k n", p=128)
        K.dma("pool", f"wbr{b}", wbr[b], srcw, (), [("wbr", b)], max_dma_last_dim=4096)
    K.dma("pool", "wo", wo, d["wo"][l].rearrange("(k p) n -> p k n", p=128), (), ["wo"], max_dma_last_dim=4096)
    ybr = [[K.sb(f"ybr{b}_{i}", [128, 4, TG], BF16) for b in range(3)] for i in range(2)]
    th = [K.sb(f"mth{i}", [128, TG], F32) for i in range(2)]
    tmp = [K.sb(f"mtmp{i}", [128, TG], F32) for i in range(2)]
    yacc = K.sb("yacc", [128, 8, TG], F32)
    yTb = K.sb("yTb", [128, 8, TG], BF16)
    xt = [K.sb(f"mxt{i}", [128, D], F32) for i in range(2)]
    tmp2 = K.sb("mtmp2", [128, 512], F32)
    it = 0
    xi = 0
    pend = []
    for tg in range(NTOK // TG):
        if tg == 0 and last:
            continue
        t0 = tg * TG
        jc = 1 if tg == 0 else 0
        yb = ybr[tg % 2]
        for b in range(3):
            K.dma("sp", f"ybr{b}_{tg % 2}", yb[b], d["yT"][b, :, :, t0:t0 + TG].rearrange("c p t -> p c t"),
                  ["yT"], [f"ybr{b}_{tg % 2}"])
        tls = tiles_of(t0, TG)
        for m in range(8):
            for b in range(3):
                pg = K.pf[it % 2]
                pp = K.pf[2 + it % 2]
                for k in range(8):
                    K.mm(pg[:, 0:TG], wmg[:, k, b * D + m * 128:b * D + (m + 1) * 128], K.hT[:, k, t0:t0 + TG],
                         k == 0, k == 7, [("wmg", b)] + hk(tls, k), [f"pf{it % 2}"])
                K.act(th[it % 2], pg[:, 0:TG], AF.Tanh, [f"pf{it % 2}"], [f"mth{it % 2}"], scale=0.5)
                for kc in range(4):
                    K.mm(pp[:, 0:TG], wbr[b][:, kc, m * 128:(m + 1) * 128], yb[b][:, kc, :], kc == 0, kc == 3,
                         [("wbr", b), f"ybr{b}_{tg % 2}"], [f"pf{2 + it % 2}"])
                if b == 0:
                    K.stt(yacc[:, m, :], th[it % 2], 1.0, pp[:, 0:TG], ALU.add, ALU.mult,
                          [f"mth{it % 2}", f"pf{2 + it % 2}"], [("yacc", m)])
                else:
                    K.stt(tmp[it % 2], th[it % 2], 1.0, pp[:, 0:TG], ALU.add, ALU.mult,
                          [f"mth{it % 2}", f"pf{2 + it % 2}"], [f"mtmp{it % 2}"])
                    if b == 1:
                        K.tt("pool", yacc[:, m, :], yacc[:, m, :], tmp[it % 2], ALU.add,
                             [f"mtmp{it % 2}", ("yacc", m)], [("yacc", m)])
                    else:
                        K.tt("pool", yTb[:, m, :], yacc[:, m, :], tmp[it % 2], ALU.add,
                             [f"mtmp{it % 2}", ("yacc", m)], [("yTb", m)])
                it += 1
        for tsub in range(TG // 128):
            j = t0 // 128 + tsub
            xb = xt[xi % 2]
            src, xkey = xrows(K, l if s == 0 else 99, j)
            K.dma("sp", f"mxt{xi % 2}", xb, src, [xkey], [f"mxt{xi % 2}"])
            if K.pair:
                K.ts("pool", xb, xb, 0.5, 1.0, ALU.mult, ALU.mult, [f"mxt{xi % 2}"], [f"mxt{xi % 2}"])
            for half in range(2):
                po = K.pf[4 + half]
                for m in range(8):
                    K.mm(po[:, 0:512], yTb[:, m, tsub * 128:(tsub + 1) * 128], wo[:, m, half * 512:(half + 1) * 512],
                         m == 0, m == 7, [("yTb", m), "wo"], [f"pf{4 + half}"])
                K.tt("dve", tmp2, po[:, 0:512], K.gateb[:, jc, half * 512:(half + 1) * 512], ALU.mult,
                     [f"pf{4 + half}", "gateb"], ["mtmp2"])
                K.tt("pool", xb[:, half * 512:(half + 1) * 512], xb[:, half * 512:(half + 1) * 512], tmp2, ALU.add,
                     ["mtmp2", f"mxt{xi % 2}"], [f"mxt{xi % 2}"])
            dst, xkey = xrows(K, l, j, write=True)
            K.dma("sp", f"mxt{xi % 2}", dst, xb, [f"mxt{xi % 2}"], [xkey])
            xi += 1
        if K.pair:
            if pend:
                pair_exchange(K, pend.pop())
            pend.append(tg)
    if K.pair and pend:
        pair_exchange(K, pend.pop())


def phase_attn(K, l, s):
    S, d = K.S, K.d
    last = (l == K.cfg.get("nlayers_total", DEPTH) - 1)
    lam_init = 0.8 - 0.6 * math.exp(-0.3 * l)
    K.reset()
    ropec = K.sb("ropec", [128, 32, 64], F32)
    ropes = K.sb("ropes", [128, 32, 64], F32)
    nwb = K.sb("nwb", [128, 256], F32)
    subwb = K.sb("subwb", [128, 128], F32)
    lam4 = K.sb("lam4", [128, 256], F32)
    lsum = K.sb("lsum", [128, 2], F32)
    neglam = K.sb("neglam", [128, 1], F32)
    K.dma("sp", "a0", ropec, d["c_ropec"], (), ["ropec"])
    K.dma("sp", "a1", ropes, d["c_ropes"], (), ["ropes"])
    K.dma("sp", "a2", nwb, d["qknw"][l].partition_broadcast(128), (), ["nwb"])
    K.dma("sp", "a3", subwb, d["subw"][l].partition_broadcast(128), (), ["subwb"])
    K.dma("sp", "a4", lam4, d["lam4"][l].partition_broadcast(128), (), ["lam4"])
    K.ts("dve", nwb[:, 0:128], nwb[:, 0:128], 0.125, None, ALU.mult, None, ["nwb"], ["nwb"])
    K.ts("dve", subwb, subwb, (1.0 - lam_init) * 0.5, None, ALU.mult, None, ["subwb"], ["subwb"])
    lp = K.sb("lp", [128, 2, 64], F32)
    l4 = lam4.rearrange("p (a b c) -> p a b c", a=2, b=2)
    K.tt("dve", lp, l4[:, :, 0, :], l4[:, :, 1, :], ALU.mult, ["lam4"], ["lp"])
    K.S.op("dve", lambda h: h.tensor_reduce(lsum, lp, AX.X, ALU.add), ["lp"], ["lsum"])
    K.act(lsum, lsum, AF.Exp, ["lsum"], ["lsum"])
    K.stt(neglam, lsum[:, 1:2], -lam_init, lsum[:, 0:1], ALU.add, ALU.subtract, ["lsum"], ["neglam"])
    wB = K.sb("wB", [128, 8, 512], BF16)
    qT = K.sb("aqT", [128, NTOK], BF16)
    kT = K.sb("akT", [128, NTOK], BF16)
    vaug = K.sb("avaug", [128, NT, 129], BF16)
    K.memset("pool", vaug[:, :, 128:129], 1.0, T("avaug", range(NT)))
    qk32 = K.sb("qk32", [128, 256], F32)
    sq = K.sb("asq", [128, 256], F32)
    ss4 = K.sb("ss4", [128, 4], F32)
    rs4 = K.sb("rs4", [128, 4], F32)
    xn = K.sb("axn", [128, 256], F32)
    t1 = K.sb("at1", [128, 256], F32)
    t2 = K.sb("at2", [128, 256], F32)
    qkr = [K.sb(f"qkr{i}", [128, 256], BF16) for i in range(2)]
    NE = 4
    ET = [K.sb(f"ET{i}", [128, 512], BF16) for i in range(NE)]
    rr = K.sb("arr", [128, 2], F32)
    tt_ = K.sb("att", [128, 128], F32)
    o32 = K.sb("ao32", [128, 128], F32)
    junk = K.sb("ajunk", [128, 128], F32)
    ss1 = K.sb("ass1", [128, 1], F32)
    rs1 = K.sb("ars1", [128, 1], F32)
    thz = K.sb("athz", [128, 128], F32)
    wz = K.sb("awz", [128, 128], F32)
    y1 = K.sb("ay1", [128, 128], F32)
    y2 = K.sb("ay2", [128, 128], BF16)
    ybT = [K.sb(f"aybT{i}", [128, 512], BF16) for i in range(2)]
    ei = 0
    yi = 0
    for i in range(NB):
        c0 = s * SUBW + 2056 + i * 512
        K.dma("pool", "wB", wB, wslice(K, l, c0, 512), (), ["wB"], max_dma_last_dim=2048)
        for j in range(NT):
            pqk = K.pf[5]
            for k in range(8):
                K.mm(pqk[:, 0:256], K.hT[:, k, j * 128:(j + 1) * 128], wB[:, k, 0:256], k == 0, k == 7,
                     ["wB"] + hk([j], k), ["pf5"])
            for k in range(8):
                K.mm(pqk[:, 256:384], K.hT[:, k, j * 128:(j + 1) * 128], wB[:, k, 256:384], k == 0, k == 7,
                     ["wB"] + hk([j], k), ["pf5"])
            K.cp("dve", vaug[:, j, 0:128], pqk[:, 256:384], ["pf5"], [("avaug", j)])
            K.cp("dve", qk32, pqk[:, 0:256], ["pf5"], ["qk32"])
            K.act(sq, qk32, AF.Square, ["qk32"], ["asq"])
            K.S.op("dve", lambda h: h.tensor_reduce(ss4, sq.rearrange("p (a b) -> p a b", a=4), AX.X, ALU.add),
                   ["asq"], ["ss4"])
            K.rstd(rs4, ss4, 64, ["ss4"], ["rs4"])
            K.tt("dve", xn.rearrange("p (a b) -> p a b", a=4), qk32.rearrange("p (a b) -> p a b", a=4),
                 rs4.unsqueeze(2).to_broadcast([128, 4, 64]), ALU.mult, ["qk32", "rs4"], ["axn"])
            qb = qkr[j % 2]
            if j >= 2:
                jl = j - 2
                K.tt("pool", xn, xn, nwb, ALU.mult, ["axn", "nwb"], ["axn"])
                K.tt("dve", t1.rearrange("p (a b) -> p a b", a=4), xn.rearrange("p (a b) -> p a b", a=4),
                     ropec[:, jl, :].unsqueeze(1).to_broadcast([128, 4, 64]), ALU.mult, ["axn", "ropec"], ["at1"])
                x5 = xn.rearrange("p (a h i) -> p a h i", h=2, i=16)
                t5 = t2.rearrange("p (a h i) -> p a h i", h=2, i=16)
                s5 = ropes[:, jl, :].rearrange("p (r h i) -> p r h i", h=2, i=16)
                for hh in range(2):
                    sin_b = s5[:, :, hh, :].unsqueeze(1).to_broadcast([128, 4, 2, 16])
                    K.tt("pool", t5[:, :, hh, :].rearrange("p (s r) i -> p s r i", r=2),
                         x5[:, :, 1 - hh, :].rearrange("p (s r) i -> p s r i", r=2), sin_b, ALU.mult,
                         ["axn", "ropes"], ["at2"])
                K.tt("dve", qb, t1, t2, ALU.add, ["at1", "at2"], [f"qkr{j % 2}"])
            else:
                K.tt("dve", qb, xn, nwb, ALU.mult, ["axn", "nwb"], [f"qkr{j % 2}"])
            pt = K.pT[j % 2]
            for hf in range(2):
                K.tr(pt[:, hf * 128:(hf + 1) * 128], qb[:, hf * 128:(hf + 1) * 128], K.ident,
                     [f"qkr{j % 2}", "ident"], [f"pT{j % 2}"])
            if j % 2 == 0:
                K.cp("dve", qT[:, j * 128:(j + 1) * 128], pt[:, 0:128], [f"pT{j % 2}"], [("aqT", j)])
                K.cp("dve", kT[:, j * 128:(j + 1) * 128], pt[:, 128:256], [f"pT{j % 2}"], [("akT", j)])
            else:
                K.cp("act", qT[:, j * 128:(j + 1) * 128], pt[:, 0:128], [f"pT{j % 2}"], [("aqT", j)])
                K.cp("act", kT[:, j * 128:(j + 1) * 128], pt[:, 128:256], [f"pT{j % 2}"], [("akT", j)])
        groups = [(256 + 512 * g, 4, list(range(NT))) for g in range(8)]
        if not last:
            groups.append((0, 2, [0, 1]))
        for (q0, nq, kcs) in groups:
            nqc = nq * 128
            qtl = list(tiles_of(q0, nqc))

            started = set()

            def acc(m, qs):
                a = m * 4 + qs
                first = (a // 3) not in started
                return K.pf[2 + a // 3][:, (a % 3) * 129:(a % 3) * 129 + 129], f"pf{2 + a // 3}", first
            iters = [(ki, kc, m) for ki, kc in enumerate(kcs) for m in range(2)]

            def emit_qk(it_):
                _, kc_, m_ = iters[it_]
                K.mm(K.pf[m_][:, 0:nqc], kT[64 * m_:64 * m_ + 64, kc_ * 128:(kc_ + 1) * 128],
                     qT[64 * m_:64 * m_ + 64, q0:q0 + nqc], True, True,
                     [("akT", kc_)] + T("aqT", qtl), [f"pf{m_}"])
            emit_qk(0)
            for it_ in range(len(iters)):
                if it_ + 1 < len(iters):
                    emit_qk(it_ + 1)
                ki, kc, m = iters[it_]
                pS = K.pf[m]
                e = ET[ei % NE]
                K.act(e[:, 0:nqc], pS[:, 0:nqc], AF.Exp, [f"pf{m}"], [f"ET{ei % NE}"])
                for qs in range(nq):
                    ap_, key, first = acc(m, qs)
                    if ki == 0:
                        started.add((m * 4 + qs) // 3)
                    K.mm(ap_, e[:, qs * 128:(qs + 1) * 128], vaug[:, kc, :], (ki == 0 and first),
                         ki == len(kcs) - 1, [f"ET{ei % NE}", ("avaug", kc)], [(key, m, qs)],
                         skip_group_check=True)
                ei += 1
            yb = ybT[yi % 2]
            allacc = [(acc(m_, q_)[1], m_, q_) for m_ in range(2) for q_ in range(nq)]
            for qs in range(nq):
                a0, k0, _ = acc(0, qs)
                a1, k1, _ = acc(1, qs)
                jq = q0 // 128 + qs
                K.S.op("dve", lambda h, a0=a0: h.reciprocal(rr[:, 0:1], a0[:, 128:129]), allacc, ["arr"])
                K.S.op("dve", lambda h, a1=a1: h.reciprocal(rr[:, 1:2], a1[:, 128:129]), [(k1, 1, qs)], ["arr"])
                K.tt("dve", rr[:, 1:2], rr[:, 1:2], neglam, ALU.mult, ["arr", "neglam"], ["arr"])
                K.ts("dve", tt_, a1[:, 0:128], rr[:, 1:2], None, ALU.mult, None, [(k1, 1, qs), "arr"], ["att"])
                K.stt(o32, a0[:, 0:128], rr[:, 0:1], tt_, ALU.mult, ALU.add, [(k0, 0, qs), "arr", "att"], ["ao32"])
                K.act(junk, o32, AF.Square, ["ao32"], ["ajunk"])
                K.S.op("dve", lambda h: h.tensor_reduce(ss1, junk, AX.X, ALU.add), ["ajunk"], ["ass1"])
                K.rstd(rs1, ss1, 128, ["ass1"], ["ars1"])
                pz = K.pf[5]
                for k in range(8):
                    K.mm(pz[:, 0:128], K.hT[:, k, jq * 128:(jq + 1) * 128], wB[:, k, 384:512], k == 0, k == 7,
                         ["wB"] + hk([jq], k), ["pf5"])
                K.act(thz, pz[:, 0:128], AF.Tanh, ["pf5"], ["athz"], scale=0.5)
                K.stt(wz, thz, 1.0, pz[:, 0:128], ALU.add, ALU.mult, ["athz", "pf5"], ["awz"])
                K.stt(y1, o32, rs1, subwb, ALU.mult, ALU.mult, ["ao32", "ars1", "subwb"], ["ay1"])
                K.tt("dve", y2, y1, wz, ALU.mult, ["ay1", "awz"], ["ay2"])
                pt = K.pT[qs % 2]
                K.tr(pt[:, 0:128], y2, K.ident, ["ay2", "ident"], [f"pT{qs % 2}"])
                K.cp("dve", yb[:, qs * 128:(qs + 1) * 128], pt[:, 0:128], [f"pT{qs % 2}"], [f"aybT{yi % 2}"])
            K.dma("sp", f"aybT{yi % 2}", d["yT"][1, i, :, q0:q0 + nqc], yb[:, 0:nqc], [f"aybT{yi % 2}"],
                  [("yT", 1, i, q0)])
            yi += 1


def phase_fft(K, l, s):
    S, d = K.S, K.d
    K.reset()
    f256 = K.sb("f256", [128, 2, 512], BF16)
    rb1 = K.sb("rb1", [64, 128], BF16)
    rb2 = K.sb("rb2", [64, 128], BF16)
    gc2 = K.sb("gc2", [128, 64, 64], BF16)
    gs2 = K.sb("gs2", [128, 64, 64], BF16)
    cc = K.sb("fcc", [128, 2, 256], BF16)
    sc = K.sb("fsc", [128, 2, 256], BF16)
    for nm, t_ in (("f256", f256), ("rb1", rb1), ("rb2", rb2), ("gc2", gc2), ("gs2", gs2), ("cc", cc), ("sc", sc)):
        K.dma("pool", "fc_" + nm, t_, d["c_" + nm], (), ["fc_" + nm], max_dma_last_dim=4096)
    fck = ["fc_f256", "fc_rb1", "fc_rb2", "fc_gc2", "fc_gs2", "fc_cc", "fc_sc"]
    wC = K.sb("wC", [128, 8, 512], BF16)
    cuT = K.sb("cuT", [128, 2, NTOK], BF16)
    gz = K.sb("fgz", [128, NTOK], BF16)
    thz = K.sb("fthz", [128, 512], F32)
    Zt = K.sb("Zt", [64, 2, 64, 2, 64], BF16)
    AT = K.sb("AT", [128, 64, 2, 64], BF16)
    Zc = K.sb("Zc", [128, 2, 2, 128], BF16)
    ycT = K.sb("ycT", [128, NTOK], BF16)
    TGS = [(t0, min(512, NTOK - t0)) for t0 in range(0, NTOK, 512)]
    for i in range(NG):
        c0 = s * SUBW + 4104 + i * 512
        K.dma("pool", "wC", wC, wslice(K, l, c0, 512), (), ["wC"], max_dma_last_dim=2048)
        for dc in range(2):
            for gi, (t0, n) in enumerate(TGS):
                pc = K.pf[gi % 2]
                for k in range(8):
                    K.mm(pc[:, 0:n], wC[:, k, dc * 128:(dc + 1) * 128], K.hT[:, k, t0:t0 + n], k == 0, k == 7,
                         ["wC"] + hk(tiles_of(t0, n), k), [f"pf{gi % 2}"])
                if gi % 2 == 0:
                    K.cp("act", cuT[:, dc, t0:t0 + n], pc[:, 0:n], [f"pf{gi % 2}"], [("cuT", dc)])
                else:
                    K.cp("dve", cuT[:, dc, t0:t0 + n], pc[:, 0:n], [f"pf{gi % 2}"], [("cuT", dc)])
        for e in range(2):
            for gi, (t0, n) in enumerate(TGS):
                pc = K.pf[gi % 2]
                for k in range(8):
                    K.mm(pc[:, 0:n], wC[:, k, 256 + e * 128:256 + (e + 1) * 128], K.hT[:, k, t0:t0 + n], k == 0,
                         k == 7, ["wC"] + hk(tiles_of(t0, n), k), [f"pf{gi % 2}"])
                K.act(thz[:, 0:n], pc[:, 0:n], AF.Tanh, [f"pf{gi % 2}"], ["fthz"], scale=0.5)
                K.stt(gz[:, t0:t0 + n], thz[:, 0:n], 1.0, pc[:, 0:n], ALU.add, ALU.mult, ["fthz", f"pf{gi % 2}"],
                      ["fgz"])
            rcols = f256[:, :, :].rearrange("p c (r x) -> p c r x", r=2)[:, :, :, e * 128:(e + 1) * 128]
            for jc in range(2):
                pa = K.pf[2]
                for dc in range(2):
                    K.mm(pa[:, 0:256].rearrange("p (r x) -> p r x", r=2), cuT[:, dc, jc * 128:(jc + 1) * 128],
                         rcols[:, dc], dc == 0, dc == 1, [("cuT", 0), ("cuT", 1), "fc_f256"], ["pf2"])
                K.cp("dve", Zc[:, jc].rearrange("p r x -> p (r x)"), pa[:, 0:256], ["pf2"], ["Zc"])
            pcx = K.pf[3]
            n_mm = 0
            for jc in range(2):
                for r, tab in ((0, cc), (1, sc)):
                    K.mm(pcx[:, 0:256], Zc[:, jc, r, :], tab[:, jc, :], n_mm == 0, n_mm == 3,
                         ["Zc", "fc_cc", "fc_sc"], ["pf3"])
                    n_mm += 1
            K.tt("dve", ycT[:, 0:256], pcx[:, 0:256], gz[:, 0:256], ALU.mult, ["pf3", "fgz"], ["ycT"])
            for np_ in range(32):
                pa = K.pf[2 + np_ % 2]
                for q in range(2):
                    n2 = 2 * np_ + q
                    for dc in range(2):
                        lat = cuT[:, dc, NCTX:NTOK].rearrange("p (a b) -> p a b", b=64)[:, :, n2]
                        K.mm(pa[0:64, q * 256:(q + 1) * 256].rearrange("p (r x) -> p r x", r=2), lat, rcols[:, dc],
                             dc == 0, dc == 1, [("cuT", 0), ("cuT", 1), "fc_f256"], [f"pf{2 + np_ % 2}"])
                src_ = pa[0:64, 0:512].rearrange("p (q r m x) -> p q r m x", q=2, r=2, m=2)
                for m_ in range(2):
                    dst_ = Zt[:, :, :, m_, 2 * np_:2 * np_ + 2].rearrange("p r x q -> p q r x")
                    if np_ % 2 == 0:
                        K.cp("dve", dst_, src_[:, :, :, m_, :], [f"pf{2 + np_ % 2}"], ["Zt"])
                    else:
                        K.cp("act", dst_, src_[:, :, :, m_, :], [f"pf{2 + np_ % 2}"], ["Zt"])
            for ib in range(16):
                pb_ = K.pf[4 + ib % 2]
                for q in range(4):
                    ii = 4 * ib + q
                    for r, rb in ((0, rb1), (1, rb2)):
                        lhs = Zt[:, r, ii, :, :].rearrange("p m n -> p (m n)")
                        K.mm(pb_[:, q * 128:(q + 1) * 128], lhs, rb, r == 0, r == 1, ["Zt", "fc_rb1", "fc_rb2"],
                             [f"pf{4 + ib % 2}"])
                src_ = pb_[:, 0:512].rearrange("p (q r k) -> p q r k", q=4, r=2)
                dst_ = AT[:, :, :, 4 * ib:4 * ib + 4].rearrange("p k r q -> p q r k")
                if ib % 2 == 0:
                    K.cp("dve", dst_, src_, [f"pf{4 + ib % 2}"], ["AT"])
                else:
                    K.cp("act", dst_, src_, [f"pf{4 + ib % 2}"], ["AT"])
            for kb in range(8):
                pc = K.pf[kb % 2]
                for q in range(8):
                    k1 = 8 * kb + q
                    for m in range(2):
                        rows = slice(64 * m, 64 * m + 64)
                        for r, tab in ((0, gc2), (1, gs2)):
                            K.mm(pc[rows, q * 64:(q + 1) * 64], AT[rows, k1, r, :], tab[rows, k1, :], r == 0, r == 1,
                                 ["AT", "fc_gc2", "fc_gs2"], [f"pf{kb % 2}"], skip_group_check=True)
                lat_y = ycT[:, NCTX:NTOK].rearrange("p (k2 k1) -> p k1 k2", k1=64)[:, 8 * kb:8 * kb + 8, :]
                lat_g = gz[:, NCTX:NTOK].rearrange("p (k2 k1) -> p k1 k2", k1=64)[:, 8 * kb:8 * kb + 8, :]
                K.tt("dve", lat_y, pc[:, 0:512].rearrange("p (q k) -> p q k", q=8), lat_g, ALU.mult,
                     [f"pf{kb % 2}", "fgz"], ["ycT"])
            K.dma("sp", "ycT", d["yT"][2, 2 * i + e, :, :], ycT, ["ycT"], [("yT", 2, i, e)])


ORD = (list(range(NCH)), [3, 2, 1, 0] + list(range(NCH - 1, 3, -1)))


def phase_mlstm(K, l, s):
    S, d = K.S, K.d
    K.reset()
    cf = {}
    for nm, shp in (("maskf", [128, 64]), ("maskb", [128, 64]), ("trif", [128, 128]), ("trib", [128, 128]),
                    ("chsel", [128, 2]), ("sel", [NA, NA, 128]), ("dlt", [NA, NT, NA]), ("ones2", [NA, 64])):
        cf[nm] = K.sb("m_" + nm, shp, F32)
        K.dma("sp", "mc_" + nm, cf[nm], d["c_" + nm], (), ["m_" + nm])
    wg = K.sb("wg", [128, 8, 4 * NA], BF16)
    bifb = K.sb("bifb", [128, 4 * NA], F32)
    anwb = K.sb("anwb", [128, 512], F32)
    K.dma("pool", "wg", wg, wslice(K, l, s * SUBW + 2048, 4 * NA), (), ["wg"])
    K.dma("sp", "bifb", bifb, d["bif"][l, s * 4 * NA:(s + 1) * 4 * NA].partition_broadcast(128), (), ["bifb"])
    K.dma("sp", "anwb", anwb, d["anw"][l, 512 * s:512 * s + 512].partition_broadcast(128), (), ["anwb"])
    K.ts("dve", anwb, anwb, 0.25, None, ALU.mult, None, ["anwb"], ["anwb"])
    G = 4 * NA
    Gt = K.sb("Gt", [128, NT, G], F32)
    pgt = K.pf[0][:, 0:NT * G]
    for j in range(NT):
        for k in range(8):
            K.mm(pgt[:, j * G:(j + 1) * G], K.hT[:, k, j * 128:(j + 1) * 128], wg[:, k, :], k == 0, k == 7,
                 ["wg"] + hk([j], k), ["pf0"])
    K.tt("dve", Gt, pgt.rearrange("p (j g) -> p j g", g=G), bifb.unsqueeze(1).to_broadcast([128, NT, G]), ALU.add,
         ["pf0", "bifb"], ["Gt"])
    WK = K.sb("WK", [128, 2, NT, NA], F32)
    EC = K.sb("EC", [128, 2, NT, NA], F32)
    decB = K.sb("decB", [128, 2, NA, NCH], F32)
    sh3 = [128, NT, NA]
    gA = K.sb("gA", sh3, F32); gE = K.sb("gE", sh3, F32); gL = K.sb("gL", sh3, F32); Fg = K.sb("Fg", sh3, F32)
    Bs = K.sb("Bs", sh3, F32); U = K.sb("U", sh3, F32); g1 = K.sb("g1", sh3, F32); g2 = K.sb("g2", sh3, F32)
    blastF = K.sb("blastF", [NA, NCH], F32); umaxF = K.sb("umaxF", [NA, NCH], F32); mst = K.sb("mst", [NA, NCH], F32)
    Mc = K.sb("Mc", [NA, NCH], F32); dd = K.sb("dd", [NA, NCH], F32); dec = K.sb("dec", [NA, NCH], F32)
    Rp = [K.sb(f"Rp{i}", [NA, NT, NA], F32) for i in range(2)]
    fl = lambda t_: t_.rearrange("p j h -> p (j h)")
    for D_ in range(2):
        PI = Gt[:, :, 2 * D_ * NA:2 * D_ * NA + NA]
        PF = Gt[:, :, (2 * D_ + 1) * NA:(2 * D_ + 2) * NA]
        tri = cf["trif" if D_ == 0 else "trib"]
        K.stt(gA, PF, -1.0, PF, ALU.mult, ALU.max, ["Gt"], ["gA"])
        K.act(gE, gA, AF.Exp, ["gA"], ["gE"], scale=-1.0)
        K.act(gL, gE, AF.Ln, ["gE"], ["gL"], bias=1.0)
        K.stt(Fg, PF, 0.0, gL, ALU.min, ALU.subtract, ["Gt", "gL"], ["Fg"])
        K.mm(K.pf[1][:, 0:NT * NA], tri, fl(Fg), True, True, ["Fg", "m_trif", "m_trib"], ["pf1"])
        K.cp("dve", fl(Bs), K.pf[1][:, 0:NT * NA], ["pf1"], ["Bs"])
        K.tt("dve", U, PI, Bs, ALU.subtract, ["Gt", "Bs"], ["U"])
        for j in range(NT):
            K.mm(K.pf[2][0:NA, 2 * j:2 * j + 2], Fg[:, j, :], cf["chsel"], True, True, ["Fg", "m_chsel"], ["pf2"])
        K.cp("dve", blastF, K.pf[2][0:NA, 0:NCH], ["pf2"], ["blastF"])
        for jb in range(0, NT, 4):
            nb_ = min(4, NT - jb)
            pu = K.pf[3 + (jb // 4) % 2]
            for q in range(nb_):
                K.tr(pu[0:NA, q * 128:(q + 1) * 128], U[:, jb + q, :], K.identf, ["U", "identf"],
                     [f"pf{3 + (jb // 4) % 2}"])
            K.S.op("dve", lambda h, pu=pu, jb=jb, nb_=nb_: h.tensor_reduce(
                umaxF[:, 2 * jb:2 * jb + 2 * nb_], pu[0:NA, 0:nb_ * 128].rearrange("p (c t) -> p c t", t=64),
                AX.X, ALU.max), [f"pf{3 + (jb // 4) % 2}"], ["umaxF"])
        od = ORD[D_]
        K.memset("dve", mst[:, od[0]:od[0] + 1], 0.0, ["mst"])
        for ix in range(NCH - 1):
            c, nx = od[ix], od[ix + 1]
            K.ts("dve", mst[:, nx:nx + 1], mst[:, c:c + 1], umaxF[:, c:c + 1], blastF[:, c:c + 1], ALU.max, ALU.add,
                 ["mst", "umaxF", "blastF"], ["mst"])
        K.tt("dve", Mc, mst, umaxF, ALU.max, ["mst", "umaxF"], ["Mc"])
        K.tt("dve", dd, mst, Mc, ALU.subtract, ["mst", "Mc"], ["dd"])
        K.act(dec, dd, AF.Exp, ["dd"], ["dec"])
        for h_ in range(NA):
            K.mm(K.pf[5][:, h_ * NCH:(h_ + 1) * NCH], cf["sel"][:, h_, :], dec, True, True, ["dec", "m_sel"], ["pf5"])
        K.cp("dve", decB[:, D_].rearrange("p h c -> p (h c)"), K.pf[5][:, 0:NA * NCH], ["pf5"], [("decB", D_)])
        for pi in range(2):
            K.tt("dve", Rp[pi], cf["dlt"], Mc[:, pi::2].unsqueeze(2).to_broadcast([NA, NT, NA]), ALU.mult,
                 ["Mc", "m_dlt"], [f"Rp{pi}"])
            K.mm(K.pf[1][64 * pi:64 * pi + 64, 0:NT * NA], cf["ones2"], fl(Rp[pi]), True, True,
                 [f"Rp{pi}", "m_ones2"], ["pf1"])
        mt = K.pf[1][:, 0:NT * NA]
        K.tt("dve", fl(g1), fl(U), mt, ALU.subtract, ["U", "pf1"], ["g1"])
        K.act(fl(WK[:, D_]), fl(g1), AF.Exp, ["g1"], [("WK", D_)])
        K.stt(fl(g2), fl(Bs), -1.0, mt, ALU.mult, ALU.subtract, ["Bs", "pf1"], ["g2"])
        K.act(fl(EC[:, D_]), fl(g2), AF.Exp, ["g2"], [("EC", D_)])
    wA = K.sb("wA", [128, 8, 1024], BF16)
    qT = K.sb("mqT", [128, NTOK], BF16)
    kT = K.sb("mkT", [128, NTOK], BF16)
    vaug = K.sb("mvaug", [128, NT, 257], BF16)
    Kp = [K.sb(f"Kp{i}", [128, NT, 128], BF16) for i in range(2)]
    hacc = K.sb("hacc", [128, NT, 256], F32)
    Cst = K.sb("Cst", [128, 257], F32)
    Cdb = [K.sb(f"Cdb{i}", [128, 257], BF16) for i in range(2)]
    aT = [K.sb(f"maT{i}", [128, 64], BF16) for i in range(2)]
    dm = K.sb("mdm", [128, 2], F32)
    tho = K.sb("tho", [128, 512], F32)
    w1 = K.sb("mw1", [128, 256], F32)
    w2 = K.sb("mw2", [128, 256], F32)
    y1 = K.sb("my1", [128, 256], F32)
    y2 = K.sb("my2", [128, 256], BF16)
    junk = K.sb("mjunk", [128, 256], F32)
    ssA = K.sb("ssA", [128, NT], F32)
    rsA = K.sb("rsA", [128, NT], F32)
    ybuf = [K.sb(f"mybuf{i}", [128, 2, 128], BF16) for i in range(2)]
    K.memset("pool", vaug[:, :, 256:257], 1.0, T("mvaug", range(NT)))
    TGS = [(t0, min(512, NTOK - t0)) for t0 in range(0, NTOK, 512)]
    mask = (cf["maskf"], cf["maskb"])
    for i in range(NA):
        K.dma("pool", "wA", wA, wslice(K, l, s * SUBW + i * 1024, 1024), (), ["wA"], max_dma_last_dim=4096)
        for gi, (t0, n) in enumerate(TGS):
            tl = tiles_of(t0, n)
            for k in range(8):
                K.mm(K.pf[0][:, 0:n], wA[:, k, 0:128], K.hT[:, k, t0:t0 + n], k == 0, k == 7, ["wA"] + hk(tl, k),
                     ["pf0"])
            K.cp("act", qT[:, t0:t0 + n], K.pf[0][:, 0:n], ["pf0"], T("mqT", tl))
            for k in range(8):
                K.mm(K.pf[1][:, 0:n], wA[:, k, 128:256], K.hT[:, k, t0:t0 + n], k == 0, k == 7, ["wA"] + hk(tl, k),
                     ["pf1"])
            K.ts("dve", kT[:, t0:t0 + n], K.pf[1][:, 0:n], 128 ** -0.5, None, ALU.mult, None, ["pf1"], T("mkT", tl))
        for j in range(NT):
            pkv = K.pf[2 + j % 2]
            for k in range(8):
                K.mm(pkv[:, 0:384], K.hT[:, k, j * 128:(j + 1) * 128], wA[:, k, 128:512], k == 0, k == 7,
                     ["wA"] + hk([j], k), [f"pf{2 + j % 2}"])
            for D_ in range(2):
                K.ts("dve", Kp[D_][:, j, :], pkv[:, 0:128], WK[:, D_, j, i:i + 1], 128 ** -0.5, ALU.mult, ALU.mult,
                     [f"pf{2 + j % 2}", ("WK", D_)], [("Kp", D_, j)])
            K.cp("dve", vaug[:, j, 0:256], pkv[:, 128:384], [f"pf{2 + j % 2}"], [("mvaug", j)])
        it = 0
        for D_ in range(2):
            K.memset("pool", Cst, 0.0, ["Cst"])
            for c in ORD[D_]:
                j, pi = divmod(c, 2)
                rows = slice(64 * pi, 64 * pi + 64)
                t0 = 64 * c
                b2 = it % 2
                dsc = decB[:, D_, i, c:c + 1]
                K.ts("pool", Cdb[b2], Cst, dsc, None, ALU.mult, None, ["Cst", ("decB", D_)], [f"Cdb{b2}"])
                pqk = K.pf[b2]
                K.mm(pqk[rows, 0:64], kT[:, t0:t0 + 64], qT[:, t0:t0 + 64], True, True, [("mkT", j), ("mqT", j)],
                     [f"pf{b2}"])
                K.stt(aT[b2][rows, :], pqk[rows, 0:64], WK[rows, D_, j, i:i + 1], mask[D_][rows, :], ALU.mult,
                      ALU.mult, [f"pf{b2}", ("WK", D_), "m_maskf", "m_maskb"], [f"maT{b2}"])
                pnum = K.pf[2 + b2]
                K.mm(pnum[rows, 0:257], aT[b2][rows, :], vaug[rows, j, :], True, False, [f"maT{b2}", ("mvaug", j)],
                     [f"pf{2 + b2}"])
                K.mm(pnum[rows, 0:257], qT[:, t0:t0 + 64], Cdb[b2], False, True, [("mqT", j), f"Cdb{b2}"],
                     [f"pf{2 + b2}"])
                pdc = K.pf[4 + b2]
                K.mm(pdc[:, 0:257], Kp[D_][rows, j, :], vaug[rows, j, :], True, True, [("Kp", D_, j), ("mvaug", j)],
                     [f"pf{4 + b2}"])
                K.stt(Cst, Cst, dsc, pdc[:, 0:257], ALU.mult, ALU.add, ["Cst", ("decB", D_), f"pf{4 + b2}"], ["Cst"])
                K.ts("dve", dm[rows, 0:1], pnum[rows, 256:257], EC[rows, D_, j, i:i + 1], None, ALU.max, None,
                     [f"pf{2 + b2}", ("EC", D_)], ["mdm"])
                K.stt(dm[rows, 0:1], pnum[rows, 256:257], -1.0, dm[rows, 0:1], ALU.mult, ALU.max,
                      [f"pf{2 + b2}", "mdm"], ["mdm"])
                K.S.op("dve", lambda h, rows=rows: h.reciprocal(dm[rows, 1:2], dm[rows, 0:1]), ["mdm"], ["mdm"])
                if D_ == 0:
                    K.act(hacc[rows, j, :], pnum[rows, 0:256], AF.Copy, [f"pf{2 + b2}", "mdm"], [("hacc", j)],
                          scale=dm[rows, 1:2])
                else:
                    K.stt(hacc[rows, j, :], pnum[rows, 0:256], dm[rows, 1:2], hacc[rows, j, :], ALU.mult, ALU.add,
                          [f"pf{2 + b2}", "mdm", ("hacc", j)], [("hacc", j)])
                it += 1
        for j in range(NT):
            K.act(junk, hacc[:, j, :], AF.Square, [("hacc", j)], ["mjunk"])
            K.S.op("dve", lambda h, j=j: h.tensor_reduce(ssA[:, j:j + 1], junk, AX.X, ALU.add), ["mjunk"], [("ssA", j)])
            K.rstd(rsA[:, j:j + 1], ssA[:, j:j + 1], 256, [("ssA", j)], [("rsA", j)])
            poz = K.pf[j % 2]
            for k in range(8):
                K.mm(poz[:, 0:512], K.hT[:, k, j * 128:(j + 1) * 128], wA[:, k, 512:1024], k == 0, k == 7,
                     ["wA"] + hk([j], k), [f"pf{j % 2}"])
            K.act(tho, poz[:, 0:512], AF.Tanh, [f"pf{j % 2}"], ["tho"], scale=0.5)
            K.stt(w1, tho[:, 256:512], 1.0, poz[:, 256:512], ALU.add, ALU.mult, ["tho", f"pf{j % 2}"], ["mw1"])
            K.stt(w2, tho[:, 0:256], 1.0, w1, ALU.add, ALU.mult, ["tho", "mw1"], ["mw2"])
            K.stt(y1, hacc[:, j, :], rsA[:, j:j + 1], anwb[:, i * 256:(i + 1) * 256], ALU.mult, ALU.mult,
                  [("hacc", j), ("rsA", j), "anwb"], ["my1"])
            K.tt("dve", y2, y1, w2, ALU.mult, ["my1", "mw2"], ["my2"])
            pt = K.pT[j % 2]
            for c2 in range(2):
                K.tr(pt[:, c2 * 128:(c2 + 1) * 128], y2[:, c2 * 128:(c2 + 1) * 128], K.ident, ["my2", "ident"],
                     [f"pT{j % 2}"])
            yb = ybuf[j % 2]
            K.cp("dve", yb.rearrange("p c t -> p (c t)"), pt[:, 0:256], [f"pT{j % 2}"], [f"mybuf{j % 2}"])
            K.dma("sp", f"mybuf{j % 2}", d["yT"][0, 2 * i:2 * i + 2, :, j * 128:(j + 1) * 128].rearrange("c p t -> p c t"),
                  yb, [f"mybuf{j % 2}"], [("yT", 0, i, j)])


def pair_exchange(K, tg):
    d = K.d
    groups = [[0, 1], [2, 3], [4, 5], [6, 7]]
    r0 = tg * 256
    rk = [("xp", 2 * tg), ("xp", 2 * tg + 1)]
    wk = [("xd", 2 * tg), ("xd", 2 * tg + 1)]
    K.S.dma("pool", "cc", lambda h: h.collective_compute("AllReduce", ALU.add, replica_groups=groups,
                                                         ins=[d["xpart"][r0:r0 + 256, :].opt()],
                                                         outs=[d["xcur"][r0:r0 + 256, :].opt()]),
            rk, wk, inc=1)
```

```python
import math
from contextlib import ExitStack

import numpy as np
import concourse.bass as bass
import concourse.mybir as mybir
from concourse.bass_utils import run_bass_kernel_spmd

F32 = mybir.dt.float32
BF16 = mybir.dt.bfloat16
AF = mybir.ActivationFunctionType
ALU = mybir.AluOpType
AX = mybir.AxisListType

D = 1024
NCTX = 256
NLAT = 4096
NTOK = NCTX + NLAT
NT = NTOK // 128
NCH = NTOK // 64
DEPTH = 4
EPS = 1e-6
NSUB = 2
NA = 2
NB = 4
NG = 2
SUBW = 5128
SB_BASE = 16640
SB_END = 229376


class Slot:
    def __init__(self, S, name):
        self.sem = S.new_sem(name)
        self.total = 0


class Sched:
    ENG = ("pe", "act", "dve", "pool", "sp")

    def __init__(self, nc, stack):
        self.nc = nc
        self.stack = stack
        self.ops = {e: [] for e in self.ENG}
        self.sem = {e: self.new_sem("s_" + e) for e in self.ENG}
        self.cnt = {e: 0 for e in self.ENG}
        self.waited = {}
        self.lastw = {}
        self.reads = {}
        self.slots = {}
        self.same_engine_sync = True

    def new_sem(self, name):
        return self.stack.enter_context(self.nc.semaphore(name))

    def slot(self, name):
        if name not in self.slots:
            self.slots[name] = Slot(self, "d_" + name)
        return self.slots[name]

    def _need(self, eng, toks, tok):
        if tok is None:
            return
        sem, val, src = tok
        if src == eng and (eng == "pe" or not self.same_engine_sync):
            return
        if self.waited.get((eng, id(sem)), 0) >= val:
            return
        cur = toks.get(id(sem))
        if cur is None or cur[1] < val:
            toks[id(sem)] = (sem, val)

    def _deps(self, eng, reads, writes):
        toks = {}
        for k in reads:
            self._need(eng, toks, self.lastw.get(k))
        for k in writes:
            self._need(eng, toks, self.lastw.get(k))
            for t in self.reads.get(k, ()):
                self._need(eng, toks, t)
        waits = list(toks.values())
        for sem, val in waits:
            self.waited[(eng, id(sem))] = val
        return waits

    def _commit(self, tok, reads, writes):
        for k in reads:
            self.reads.setdefault(k, []).append(tok)
        for k in writes:
            self.lastw[k] = tok
            self.reads[k] = []

    def op(self, eng, fn, reads=(), writes=()):
        waits = self._deps(eng, reads, writes)
        self.cnt[eng] += 1
        tok = (self.sem[eng], self.cnt[eng], eng)
        self.ops[eng].append((waits, fn, (self.sem[eng], 1)))
        self._commit(tok, reads, writes)
        return tok

    def dma(self, eng, slotname, fn, reads=(), writes=(), inc=16):
        slot = self.slot(slotname)
        waits = self._deps(eng, reads, writes)
        slot.total += inc
        tok = (slot.sem, slot.total, "dma")
        self.ops[eng].append((waits, fn, (slot.sem, inc)))
        self._commit(tok, reads, writes)
        return tok

    def barrier(self):
        toks = [(self.sem[e], self.cnt[e], "x") for e in self.ENG if self.cnt[e] > 0]
        toks += [(s.sem, s.total, "dma") for s in self.slots.values() if s.total > 0]
        for e in self.ENG:
            need = {}
            for t in toks:
                if t[0] is self.sem[e]:
                    continue
                self._need(e, need, t)
            waits = list(need.values())
            for sem, val in waits:
                self.waited[(e, id(sem))] = val
            if waits:
                self.ops[e].append((waits, None, None))

    def emit(self):
        nc = self.nc
        with nc.Block() as block:
            def mk(e):
                def body(h):
                    for waits, fn, inc in self.ops[e]:
                        for sem, val in waits:
                            h.wait_ge(sem, val)
                        if fn is not None:
                            fn(h).then_inc(inc[0], inc[1])
                return body
            block.tensor(mk("pe"))
            block.scalar(mk("act"))
            block.vector(mk("dve"))
            block.gpsimd(mk("pool"))
            block.sync(mk("sp"))


def T(name, idxs):
    return [(name, i) for i in idxs]


def hk(js, k):
    return [("hT", j, k) for j in js]


def tiles_of(t0, n):
    return range(t0 // 128, (t0 + n + 127) // 128)


def make_consts():
    c = {}
    p = np.arange(128)
    c["ident"] = np.eye(128, dtype=np.float32)
    tok = (np.arange(32)[None, :] * 128 + p[:, None]).astype(np.float32)
    row = np.floor(tok / 64.0).astype(np.float32)
    col = (tok - row * 64.0).astype(np.float32)
    inv = (10000.0 ** (-np.arange(0, 32, 2, dtype=np.float32) / 32.0)).astype(np.float32)
    ar = (row[..., None] * inv).astype(np.float32)
    ac = (col[..., None] * inv).astype(np.float32)
    cosT = np.concatenate([np.cos(ar), np.cos(ar), np.cos(ac), np.cos(ac)], axis=-1)
    sinT = np.concatenate([-np.sin(ar), np.sin(ar), -np.sin(ac), np.sin(ac)], axis=-1)
    c["ropec"] = cosT.astype(np.float32)
    c["ropes"] = sinT.astype(np.float32)
    s = p % 64
    t = np.arange(64)
    c["maskf"] = (s[:, None] <= t[None, :]).astype(np.float32)
    c["maskb"] = (s[:, None] >= t[None, :]).astype(np.float32)
    same = (p[:, None] // 64) == (p[None, :] // 64)
    c["trif"] = (same & (p[:, None] <= p[None, :])).astype(np.float32)
    c["trib"] = (same & (p[:, None] >= p[None, :])).astype(np.float32)
    c["chsel"] = ((p[:, None] // 64) == np.arange(2)[None, :]).astype(np.float32)
    sel = np.zeros((NA, NA, 128), np.float32)
    for h in range(NA):
        sel[h, h, :] = 1.0
    c["sel"] = sel
    dl = np.zeros((NA, NT, NA), np.float32)
    for h in range(NA):
        dl[h, :, h] = 1.0
    c["dlt"] = dl
    c["ones2"] = np.ones((NA, 64), np.float32)
    d = np.arange(256)
    ang = 2 * np.pi * np.outer(d, d) / 256.0
    f256 = np.concatenate([np.cos(ang), -np.sin(ang)], axis=1) / 16.0
    c["f256"] = f256.reshape(2, 128, 512).transpose(1, 0, 2).astype(np.float32)
    n = np.arange(64)
    a64 = 2 * np.pi * np.outer(n, n) / 64.0
    C64, S64 = np.cos(a64) / 8.0, np.sin(a64) / 8.0
    c["rb1"] = np.concatenate([C64, -S64], axis=1).astype(np.float32)
    c["rb2"] = np.concatenate([S64, C64], axis=1).astype(np.float32)
    n2 = (p % 64)[:, None, None]
    k1 = np.arange(64)[None, :, None]
    k2 = np.arange(64)[None, None, :]
    ag = 2 * np.pi * (n2 * (k1 + 64 * k2) % 4096) / 4096.0
    c["gc2"] = (np.cos(ag) / 16.0).astype(np.float32)
    c["gs2"] = (np.sin(ag) / 16.0).astype(np.float32)
    nn = np.arange(256)
    a256 = 2 * np.pi * (np.outer(nn, nn) % 256) / 256.0
    c["cc"] = (np.cos(a256) / 32.0).reshape(2, 128, 256).transpose(1, 0, 2).astype(np.float32)
    c["sc"] = (np.sin(a256) / 32.0).reshape(2, 128, 256).transpose(1, 0, 2).astype(np.float32)
    return c


def win_cols(subsets):
    cols = []
    for s in subsets:
        for i in range(NA):
            h = NA * s + i
            cols += list(range(0 + 128 * h, 128 * h + 128))
            cols += list(range(512 + 128 * h, 512 + 128 * h + 128))
            cols += list(range(1024 + 256 * h, 1024 + 256 * h + 256))
            cols += list(range(2064 + 256 * h, 2064 + 256 * h + 256))
            cols += list(range(3088 + 256 * h, 3088 + 256 * h + 256))
        for kind in range(4):
            for i in range(NA):
                cols.append(2048 + kind * 4 + NA * s + i)
        for i in range(NB):
            h = NB * s + i
            cols += list(range(4112 + 128 * h, 4112 + 128 * h + 128))
            cols += list(range(5136 + 128 * h, 5136 + 128 * h + 128))
            cols += list(range(6160 + 128 * h, 6160 + 128 * h + 128))
            cols += list(range(7184 + 128 * h, 7184 + 128 * h + 128))
        for i in range(NG):
            g = NG * s + i
            cols += list(range(8208 + 256 * g, 8208 + 256 * g + 256))
            cols += list(range(9232 + 256 * g, 9232 + 256 * g + 256))
    cols += list(range(10256, 13328))
    assert len(cols) == len(subsets) * SUBW + 3 * D
    return np.asarray(cols)


def gate_bias_idx(subsets):
    idx = []
    for s in subsets:
        for kind in range(4):
            for i in range(NA):
                idx.append(kind * 4 + NA * s + i)
    return np.asarray(idx)


def feat_rows(subsets):
    return np.concatenate([np.arange(512 * s, 512 * s + 512) for s in subsets])


class StopBuild(Exception):
    pass


class Ctx:
    def cut(self, name):
        if self.cfg.get("cut") == name:
            raise StopBuild()

    def __init__(self, nc, st, cfg):
        self.nc = nc
        self.st = st
        self.cfg = cfg
        self.S = Sched(nc, st)
        self.pair = bool(cfg.get("pair", False))
        self.nsub = 1 if self.pair else NSUB
        self.mg0 = self.nsub * SUBW
        self.pers = SB_BASE
        self.scr0 = None
        self.scr = None
        self.uid = 0
        self.d = {}
        self.ps = {}

    def _alloc(self, name, shape, dt, off):
        self.uid += 1
        t = self.nc.alloc_sbuf_tensor_at(f"{name}_{self.uid}", list(shape), dt, offset=off)
        return t.ap()

    @staticmethod
    def _bytes(shape, dt):
        n = 1
        for s in shape[1:]:
            n *= s
        b = n * (2 if dt == BF16 else 4)
        return (b + 63) // 64 * 64

    def pb(self, name, shape, dt):
        assert self.scr0 is None
        off = self.pers
        self.pers += self._bytes(shape, dt)
        assert self.pers <= SB_END, "persistent SBUF overflow"
        return self._alloc(name, shape, dt, off)

    def freeze(self):
        self.scr0 = self.pers
        self.scr = self.scr0

    def reset(self):
        self.S.barrier()
        self.scr = self.scr0

    def sb(self, name, shape, dt):
        off = self.scr
        self.scr += self._bytes(shape, dt)
        assert self.scr <= SB_END, f"scratch SBUF overflow at {name}: {self.scr - SB_END}"
        return self._alloc(name, shape, dt, off)

    def mm(self, out, lhsT, rhs, start, stop, r, w, **kw):
        return self.S.op("pe", lambda h: h.matmul(out, lhsT, rhs, start=start, stop=stop, **kw), r, w)

    def tr(self, out, in_, ident, r, w):
        return self.S.op("pe", lambda h: h.transpose(out, in_, ident), r, w)

    def act(self, out, in_, func, r, w, scale=1.0, bias=0.0, accum_out=None):
        if accum_out is None:
            return self.S.op("act", lambda h: h.activation(out=out, in_=in_, func=func, bias=bias, scale=scale), r, w)
        return self.S.op("act", lambda h: h.activation(out=out, in_=in_, func=func, bias=bias, scale=scale,
                                                       accum_out=accum_out), r, w)

    def tt(self, eng, out, in0, in1, op, r, w):
        return self.S.op(eng, lambda h: h.tensor_tensor(out, in0, in1, op), r, w)

    def ts(self, eng, out, in0, s1, s2, op0, op1, r, w):
        if s2 is None:
            return self.S.op(eng, lambda h: h.tensor_scalar(out, in0, s1, None, op0), r, w)
        return self.S.op(eng, lambda h: h.tensor_scalar(out, in0, s1, s2, op0, op1), r, w)

    def stt(self, out, in0, scalar, in1, op0, op1, r, w, accum_out=None):
        if accum_out is None:
            return self.S.op("dve", lambda h: h.scalar_tensor_tensor(out, in0, scalar, in1, op0, op1), r, w)
        return self.S.op("dve", lambda h: h.scalar_tensor_tensor(out, in0, scalar, in1, op0, op1,
                                                                 accum_out=accum_out), r, w)

    def cp(self, eng, out, in_, r, w):
        if eng == "act":
            return self.S.op("act", lambda h: h.copy(out, in_), r, w)
        return self.S.op(eng, lambda h: h.tensor_copy(out, in_), r, w)

    def memset(self, eng, ap, val, w):
        return self.S.op(eng, lambda h: h.memset(ap, val), (), w)

    def dma(self, eng, slot, out, in_, r, w, **kw):
        return self.S.dma(eng, slot, lambda h: h.dma_start(out=out, in_=in_, **kw), r, w)

    def rstd(self, out, ss, n, r, w):
        self.ts("dve", out, ss, 1.0 / n, EPS, ALU.mult, ALU.add, r, w)
        self.act(out, out, AF.Sqrt, w, w)
        return self.S.op("dve", lambda h: h.reciprocal(out, out), w, w)


def wslice(K, l, c0, n):
    return K.d["win"][l].rearrange("(k p) n -> p k n", p=128)[:, :, c0:c0 + n]


CONST_SHAPES = dict(ident=[128, 128], ropec=[128, 32, 64], ropes=[128, 32, 64], maskf=[128, 64], maskb=[128, 64],
                    trif=[128, 128], trib=[128, 128], chsel=[128, 2], sel=[NA, NA, 128], dlt=[NA, NT, NA],
                    ones2=[NA, 64], f256=[128, 2, 512], rb1=[64, 128], rb2=[64, 128], gc2=[128, 64, 64],
                    gs2=[128, 64, 64], cc=[128, 2, 256], sc=[128, 2, 256])


def declare_dram(K):
    nc, d = K.nc, K.d
    DEPTH = K.cfg.get("layers", 4)
    inp = lambda n, s: nc.dram_tensor(n, list(s), F32, kind="ExternalInput").ap()
    d["xin"] = inp("xin", [NTOK, D])
    d["cvec"] = inp("cvec", [128, 8, 2])
    d["normw"] = inp("normw", [DEPTH, 128, 8])
    d["wada"] = inp("wada", [DEPTH, D, 3 * D])
    d["bada"] = inp("bada", [DEPTH, 128, 24])
    ns = K.nsub
    d["win"] = inp("win", [DEPTH, D, ns * SUBW + 3 * D])
    d["bif"] = inp("bif", [DEPTH, ns * 4 * NA])
    d["anw"] = inp("anw", [DEPTH, ns * 512])
    d["qknw"] = inp("qknw", [DEPTH, 256])
    d["lam4"] = inp("lam4", [DEPTH, 256])
    d["subw"] = inp("subw", [DEPTH, 128])
    for n in ("wao", "wbo", "wco"):
        d[n] = inp(n, [DEPTH, ns * 512, D])
    d["wo"] = inp("wo", [DEPTH, D, D])
    for n, s in CONST_SHAPES.items():
        d["c_" + n] = inp("c_" + n, s)
    d["out"] = nc.dram_tensor("out", [NLAT, D], F32, kind="ExternalOutput").ap()
    d["xctx"] = nc.dram_tensor("xctx", [NCTX, D], F32, kind="Internal").ap()
    d["yT"] = nc.dram_tensor("yT", [3, 4, 128, NTOK], BF16, kind="Internal").ap()
    d["gsc"] = nc.dram_tensor("gsc", [2, D], F32, kind="Internal").ap()
    if K.pair:
        d["xpart"] = nc.dram_tensor("xpart", [NTOK, D], F32, kind="Internal").ap()
        d["xcur"] = nc.dram_tensor("xcur", [NTOK, D], F32, kind="Internal").ap()
    if K.cfg.get("dbg"):
        d["dbg_hT"] = nc.dram_tensor("dbg_hT", [128, 8, NTOK], BF16, kind="ExternalOutput").ap()
        d["dbg_yT"] = nc.dram_tensor("dbg_yT", [K.nsub, 3, 4, 128, NTOK], BF16, kind="ExternalOutput").ap()
        d["dbg_ctx"] = nc.dram_tensor("dbg_ctx", [NCTX, D], F32, kind="ExternalOutput").ap()


def xrows(K, l, j, write=False):
    if K.pair:
        if write:
            return K.d["xpart"][j * 128:(j + 1) * 128, :], ("xp", j)
        src = K.d["xin"] if l == 0 else K.d["xcur"]
        return src[j * 128:(j + 1) * 128, :], ("xd", j)
    if j < 2:
        src = K.d["xin"] if (l == 0 and not write) else K.d["xctx"]
        return src[j * 128:(j + 1) * 128, :], ("xd", j)
    jj = j - 2
    if l == 0 and not write:
        return K.d["xin"][NCTX + jj * 128:NCTX + (jj + 1) * 128, :], ("xd", j)
    return K.d["out"][jj * 128:(jj + 1) * 128, :], ("xd", j)


def setup(K):
    S, d = K.S, K.d
    K.hT = K.pb("hT", [128, 8, NTOK], BF16)
    K.ident = K.pb("ident", [128, 128], BF16)
    K.identf = K.pb("identf", [128, 128], F32)
    K.c_mhalf = K.pb("mhalf", [128, 64], F32)
    K.scT = K.pb("scT", [128, 8, 2], BF16)
    K.A1 = K.pb("A1", [128, 8, 2], F32)
    K.B1 = K.pb("B1", [128, 8, 2], F32)
    K.gateb = K.pb("gateb", [128, 2, D], F32)
    K.lam = K.pb("lam", [128, 1], F32)
    K.freeze()
    nc = K.nc
    K.pT = [K.st.enter_context(nc.psum_tensor(f"pT{i}", [128, 1024], BF16)) for i in range(2)]
    K.pf = [K.st.enter_context(nc.psum_tensor(f"pf{i}", [128, 512], F32)) for i in range(6)]
    K.dma("pool", "c0", K.ident, d["c_ident"], (), ["ident"])
    K.dma("sp", "c1", K.identf, d["c_ident"], (), ["identf"])
    K.memset("dve", K.c_mhalf, -0.5, ["mhalf"])
    cv = K.sb("cv", [128, 8, 2], F32)
    th = K.sb("cth", [128, 8, 2], F32)
    K.dma("sp", "c2", cv, d["cvec"], (), ["cv"])
    K.act(th, cv, AF.Tanh, ["cv"], ["cth"], scale=0.5)
    K.stt(th, th, 1.0, cv, ALU.add, ALU.mult, ["cth", "cv"], ["cth"])
    K.ts("dve", K.scT, th, 0.5, None, ALU.mult, None, ["cth"], ["scT"])


def phase_norm(K, l):
    S, d = K.S, K.d
    K.reset()
    normw = K.sb("normw", [128, 8], F32)
    bada = K.sb("bada", [128, 24], F32)
    modT = K.sb("modT", [128, 24, 2], F32)
    wad = K.sb("wad", [128, 8, D], BF16)
    K.dma("sp", "sv0", normw, d["normw"][l], (), ["normw"])
    K.dma("sp", "sv1", bada, d["bada"][l], (), ["bada"])
    pm = K.pf[0][:, 0:48]
    wv = d["wada"][l].rearrange("(k p) n -> p k n", p=128)
    for third in range(3):
        K.dma("pool", "wad", wad, wv[:, :, third * D:(third + 1) * D], (), ["wad"], max_dma_last_dim=4096)
        for m in range(8):
            g = third * 8 + m
            for k in range(8):
                K.mm(pm[:, 2 * g:2 * g + 2], wad[:, k, m * 128:(m + 1) * 128], K.scT[:, k, :], k == 0, k == 7,
                     ["wad", "scT"], ["pf0"])
    K.tt("dve", modT, K.pf[0][:, 0:48].rearrange("p (g j) -> p g j", j=2),
         bada.unsqueeze(2).to_broadcast([128, 24, 2]), ALU.add, ["pf0", "bada"], ["modT"])
    K.stt(K.A1, modT[:, 8:16, :], 1.0, normw.unsqueeze(2).to_broadcast([128, 8, 2]), ALU.add, ALU.mult,
          ["modT", "normw"], ["A1"])
    K.cp("dve", K.B1, modT[:, 0:8, :], ["modT"], ["B1"])
    gt = K.sb("gt", [128, 2, 8], F32)
    gtT = K.sb("gtT", [8, 2, 128], F32)
    for j in range(2):
        K.ts("dve", gt[:, j, :], modT[:, 16:24, j], 0.5, None, ALU.mult, None, ["modT"], ["gt"])
    for j in range(2):
        K.tr(K.pf[1][0:8, j * 128:(j + 1) * 128], gt[:, j, :], K.identf, ["gt", "identf"], ["pf1"])
    K.cp("dve", gtT, K.pf[1][0:8, 0:256].rearrange("p (j f) -> p j f", j=2), ["pf1"], ["gtT"])
    K.dma("sp", "gsc", d["gsc"].rearrange("j (k p) -> k j p", p=128), gtT, ["gtT"], ["gsc"])
    for j in range(2):
        K.dma("sp", "gsc", K.gateb[:, j, :], d["gsc"][j].partition_broadcast(128), ["gsc"], ["gateb"])
    K.cut("p0")
    NXB = 3
    xt = [K.sb(f"xt{i}", [128, D], F32) for i in range(NXB)]
    xn = [K.sb(f"xn{i}", [128, D], BF16) for i in range(2)]
    junk = K.sb("junk", [128, D], F32)
    ss = K.sb("ss", [128, NT], F32)
    rs = K.sb("rs", [128, NT], F32)
    for j in range(NT):
        b = j % NXB
        src, xkey = xrows(K, l, j)
        K.dma("sp", f"xt{b}", xt[b], src, [xkey], [f"xt{b}"])
        K.act(junk, xt[b], AF.Square, [f"xt{b}"], ["junk"])
        K.S.op("dve", lambda h, j=j: h.tensor_reduce(ss[:, j:j + 1], junk, AX.X, ALU.add), ["junk"], [("ss", j)])
        K.cut("p1a")
        K.rstd(rs[:, j:j + 1], ss[:, j:j + 1], D, [("ss", j)], [("rs", j)])
        K.cut("p1b")
        nb = j % 2
        K.ts("dve", xn[nb], xt[b], rs[:, j:j + 1], None, ALU.mult, None, [f"xt{b}", ("rs", j)], [f"xn{nb}"])
        K.cut("p1c")
        for k in range(8):
            K.tr(K.pT[k // 4][:, (k % 4) * 128:(k % 4 + 1) * 128], xn[nb][:, k * 128:(k + 1) * 128], K.ident,
                 [f"xn{nb}", "ident"], [f"pT{k // 4}"])
        K.cut("p1d")
        jc = 1 if j < 2 else 0
        for k in range(8):
            o = K.hT[:, k, j * 128:(j + 1) * 128]
            i_ = K.pT[k // 4][:, (k % 4) * 128:(k % 4 + 1) * 128]
            if k < 4:
                K.act(o, i_, AF.Identity, [f"pT{k // 4}", "A1", "B1"], [("hT", j, k)], scale=K.A1[:, k, jc:jc + 1],
                      bias=K.B1[:, k, jc:jc + 1])
            else:
                K.ts("dve", o, i_, K.A1[:, k, jc:jc + 1], K.B1[:, k, jc:jc + 1], ALU.mult, ALU.add,
                     [f"pT{k // 4}", "A1", "B1"], [("hT", j, k)])
            if k == 0:
                K.cut("p1k0")
            if k == 1:
                K.cut("p1k1")
        K.cut(f"p1e{j}")
    K.cut("p1f")


def finish(K):
    K.S.barrier()


def build(cfg):
    nc = bass.Bass("TRN2", target_bir_lowering=False)
    with ExitStack() as st:
        K = Ctx(nc, st, cfg)
        declare_dram(K)
        try:
            build_body(K, cfg)
        except StopBuild:
            pass
        finish(K)
        K.S.emit()
    return nc


def build_body(K, cfg):
    if True:
        setup(K)
        K.cut("setup")
        for l in range(cfg.get("layers", DEPTH)):
            phase_norm(K, l)
            if cfg.get("dbg") == "norm":
                K.dma("sp", "dbg", K.d["dbg_hT"], K.hT, [("hT", j, k) for j in range(NT) for k in range(8)], ["dbg"])
                break
            for s in range(K.nsub):
                if "mlstm" not in cfg.get("skip", ()):
                    phase_mlstm(K, l, s)
                if "attn" not in cfg.get("skip", ()):
                    phase_attn(K, l, s)
                if "fft" not in cfg.get("skip", ()):
                    phase_fft(K, l, s)
                if cfg.get("dbg"):
                    K.S.barrier()
                    K.dma("sp", "dbg", K.d["dbg_yT"][s], K.d["yT"], ["yT"], ["dbg"])
                if "merge" not in cfg.get("skip", ()):
                    phase_merge(K, l, s)
            if K.pair and l == cfg.get("layers", DEPTH) - 1:
                K.dma("sp", "fin", K.d["out"], K.d["xcur"][NCTX:NTOK, :], [("xd", j) for j in range(2, NT)], ["outfin"])
        if cfg.get("dbg"):
            K.S.barrier()
            K.dma("sp", "dbg", K.d["dbg_ctx"], K.d["xctx"], [("xd", 0), ("xd", 1)], ["dbg"])


def prep_shared(inp, DEPTH=DEPTH, subsets=(0, 1)):
    f = lambda a: np.ascontiguousarray(np.asarray(a, dtype=np.float32))
    inp = {k: (np.asarray(v)[:DEPTH] if k not in ("x", "c", "ctx", "c_ctx") else v) for k, v in inp.items()}
    subsets = list(subsets)
    sh = {}
    sh["normw"] = f(inp["norm_w"].reshape(DEPTH, 8, 128).transpose(0, 2, 1))
    sh["wada"] = f(inp["w_ada"])
    sh["bada"] = f(inp["b_ada"].reshape(DEPTH, 24, 128).transpose(0, 2, 1))
    sh["win"] = f(inp["w_in"][:, :, win_cols(subsets)])
    sh["bif"] = f(inp["b_if"][:, gate_bias_idx(subsets)])
    fr = feat_rows(subsets)
    sh["anw"] = f(inp["a_norm_w"][:, fr])
    sh["qknw"] = f(np.concatenate([inp["q_norm_w"], inp["q_norm_w"], inp["k_norm_w"], inp["k_norm_w"]], axis=1))
    sh["lam4"] = f(np.concatenate([inp["lambda_q1"], inp["lambda_k1"], inp["lambda_q2"], inp["lambda_k2"]], axis=1))
    sh["subw"] = f(inp["subln_w"])
    sh["wao"] = f(inp["w_a_out"][:, fr]); sh["wbo"] = f(inp["w_b_out"][:, fr]); sh["wco"] = f(inp["w_c_out"][:, fr])
    sh["wo"] = f(inp["w_out"])
    for n, v in make_consts().items():
        assert list(v.shape) == CONST_SHAPES[n], (n, v.shape)
        sh["c_" + n] = f(v)
    return sh


def prep_core(inp, b):
    m = {}
    m["xin"] = np.ascontiguousarray(np.concatenate([inp["ctx"][b], inp["x"][b]], axis=0).astype(np.float32))
    cv = np.stack([np.asarray(inp["c"][b]), np.asarray(inp["c_ctx"])], axis=-1)
    m["cvec"] = np.ascontiguousarray(cv.reshape(8, 128, 2).transpose(1, 0, 2).astype(np.float32))
    return m


_NC_CACHE = {}


PAIR = True


def kernel(**inputs):
    cfg = {"pair": PAIR}
    if "full" not in _NC_CACHE:
        _NC_CACHE["full"] = build(cfg)
    nc = _NC_CACHE["full"]
    in_maps = []
    if PAIR:
        shs = [prep_shared(inputs, DEPTH, (s,)) for s in range(NSUB)]
        cores = [prep_core(inputs, b) for b in range(4)]
        for core in range(8):
            m = dict(shs[core % 2])
            m.update(cores[core // 2])
            in_maps.append(m)
        res = run_bass_kernel_spmd(nc, in_maps, core_ids=list(range(8)))
        out = np.stack([np.asarray(res.results[2 * b]["out"]) for b in range(4)], axis=0)
    else:
        sh = prep_shared(inputs)
        for core in range(8):
            m = dict(sh)
            m.update(prep_core(inputs, core % 4))
            in_maps.append(m)
        res = run_bass_kernel_spmd(nc, in_maps, core_ids=list(range(8)))
        out = np.stack([np.asarray(res.results[b]["out"]) for b in range(4)], axis=0)
    return out.astype(np.float32)


def phase_merge(K, l, s):
    S, d = K.S, K.d
    last = (l == K.cfg.get("nlayers_total", DEPTH) - 1)
    K.reset()
    TG = 256
    wmg = K.sb("wmg", [128, 8, 3 * D], BF16)
    wbr = [K.sb(f"wbr{b}", [128, 4, D], BF16) for b in range(3)]
    wo = K.sb("wo", [128, 8, D], BF16)
    for b in range(3):
        K.dma("pool", f"wmg{b}", wmg[:, :, b * D:(b + 1) * D], wslice(K, l, K.mg0 + b * D, D), (), [("wmg", b)],
              max_dma_last_dim=4096)
        srcw = d[("wao", "wbo", "wco")[b]][l][512 * s:512 * s + 512, :].rearrange("(k p) n -> p k n", p=128)
        K.dma("pool", f"wbr{b}", wbr[b], srcw, (), [("wbr", b)], max_dma_last_dim=4096)
    K.dma("pool", "wo", wo, d["wo"][l].rearrange("(k p) n -> p k n", p=128), (), ["wo"], max_dma_last_dim=4096)
    ybr = [[K.sb(f"ybr{b}_{i}", [128, 4, TG], BF16) for b in range(3)] for i in range(2)]
    th = [K.sb(f"mth{i}", [128, TG], F32) for i in range(2)]
    tmp = [K.sb(f"mtmp{i}", [128, TG], F32) for i in range(2)]
    yacc = K.sb("yacc", [128, 8, TG], F32)
    yTb = K.sb("yTb", [128, 8, TG], BF16)
    xt = [K.sb(f"mxt{i}", [128, D], F32) for i in range(2)]
    tmp2 = K.sb("mtmp2", [128, 512], F32)
    it = 0
    xi = 0
    pend = []
    for tg in range(NTOK // TG):
        if tg == 0 and last:
            continue
        t0 = tg * TG
        jc = 1 if tg == 0 else 0
        yb = ybr[tg % 2]
        for b in range(3):
            K.dma("sp", f"ybr{b}_{tg % 2}", yb[b], d["yT"][b, :, :, t0:t0 + TG].rearrange("c p t -> p c t"),
                  ["yT"], [f"ybr{b}_{tg % 2}"])
        tls = tiles_of(t0, TG)
        for m in range(8):
            for b in range(3):
                pg = K.pf[it % 2]
                pp = K.pf[2 + it % 2]
                for k in range(8):
                    K.mm(pg[:, 0:TG], wmg[:, k, b * D + m * 128:b * D + (m + 1) * 128], K.hT[:, k, t0:t0 + TG],
                         k == 0, k == 7, [("wmg", b)] + hk(tls, k), [f"pf{it % 2}"])
                K.act(th[it % 2], pg[:, 0:TG], AF.Tanh, [f"pf{it % 2}"], [f"mth{it % 2}"], scale=0.5)
                for kc in range(4):
                    K.mm(pp[:, 0:TG], wbr[b][:, kc, m * 128:(m + 1) * 128], yb[b][:, kc, :], kc == 0, kc == 3,
                         [("wbr", b), f"ybr{b}_{tg % 2}"], [f"pf{2 + it % 2}"])
                if b == 0:
                    K.stt(yacc[:, m, :], th[it % 2], 1.0, pp[:, 0:TG], ALU.add, ALU.mult,
                          [f"mth{it % 2}", f"pf{2 + it % 2}"], [("yacc", m)])
                else:
                    K.stt(tmp[it % 2], th[it % 2], 1.0, pp[:, 0:TG], ALU.add, ALU.mult,
                          [f"mth{it % 2}", f"pf{2 + it % 2}"], [f"mtmp{it % 2}"])
                    if b == 1:
                        K.tt("pool", yacc[:, m, :], yacc[:, m, :], tmp[it % 2], ALU.add,
                             [f"mtmp{it % 2}", ("yacc", m)], [("yacc", m)])
                    else:
                        K.tt("pool", yTb[:, m, :], yacc[:, m, :], tmp[it % 2], ALU.add,
                             [f"mtmp{it % 2}", ("yacc", m)], [("yTb", m)])
                it += 1
        for tsub in range(TG // 128):
            j = t0 // 128 + tsub
            xb = xt[xi % 2]
            src, xkey = xrows(K, l if s == 0 else 99, j)
            K.dma("sp", f"mxt{xi % 2}", xb, src, [xkey], [f"mxt{xi % 2}"])
            if K.pair:
                K.ts("pool", xb, xb, 0.5, 1.0, ALU.mult, ALU.mult, [f"mxt{xi % 2}"], [f"mxt{xi % 2}"])
            for half in range(2):
                po = K.pf[4 + half]
                for m in range(8):
                    K.mm(po[:, 0:512], yTb[:, m, tsub * 128:(tsub + 1) * 128], wo[:, m, half * 512:(half + 1) * 512],
                         m == 0, m == 7, [("yTb", m), "wo"], [f"pf{4 + half}"])
                K.tt("dve", tmp2, po[:, 0:512], K.gateb[:, jc, half * 512:(half + 1) * 512], ALU.mult,
                     [f"pf{4 + half}", "gateb"], ["mtmp2"])
                K.tt("pool", xb[:, half * 512:(half + 1) * 512], xb[:, half * 512:(half + 1) * 512], tmp2, ALU.add,
                     ["mtmp2", f"mxt{xi % 2}"], [f"mxt{xi % 2}"])
            dst, xkey = xrows(K, l, j, write=True)
            K.dma("sp", f"mxt{xi % 2}", dst, xb, [f"mxt{xi % 2}"], [xkey])
            xi += 1
        if K.pair:
            if pend:
                pair_exchange(K, pend.pop())
            pend.append(tg)
    if K.pair and pend:
        pair_exchange(K, pend.pop())


def phase_attn(K, l, s):
    S, d = K.S, K.d
    last = (l == K.cfg.get("nlayers_total", DEPTH) - 1)
    lam_init = 0.8 - 0.6 * math.exp(-0.3 * l)
    K.reset()
    ropec = K.sb("ropec", [128, 32, 64], F32)
    ropes = K.sb("ropes", [128, 32, 64], F32)
    nwb = K.sb("nwb", [128, 256], F32)
    subwb = K.sb("subwb", [128, 128], F32)
    lam4 = K.sb("lam4", [128, 256], F32)
    lsum = K.sb("lsum", [128, 2], F32)
    neglam = K.sb("neglam", [128, 1], F32)
    K.dma("sp", "a0", ropec, d["c_ropec"], (), ["ropec"])
    K.dma("sp", "a1", ropes, d["c_ropes"], (), ["ropes"])
    K.dma("sp", "a2", nwb, d["qknw"][l].partition_broadcast(128), (), ["nwb"])
    K.dma("sp", "a3", subwb, d["subw"][l].partition_broadcast(128), (), ["subwb"])
    K.dma("sp", "a4", lam4, d["lam4"][l].partition_broadcast(128), (), ["lam4"])
    K.ts("dve", nwb[:, 0:128], nwb[:, 0:128], 0.125, None, ALU.mult, None, ["nwb"], ["nwb"])
    K.ts("dve", subwb, subwb, (1.0 - lam_init) * 0.5, None, ALU.mult, None, ["subwb"], ["subwb"])
    lp = K.sb("lp", [128, 2, 64], F32)
    l4 = lam4.rearrange("p (a b c) -> p a b c", a=2, b=2)
    K.tt("dve", lp, l4[:, :, 0, :], l4[:, :, 1, :], ALU.mult, ["lam4"], ["lp"])
    K.S.op("dve", lambda h: h.tensor_reduce(lsum, lp, AX.X, ALU.add), ["lp"], ["lsum"])
    K.act(lsum, lsum, AF.Exp, ["lsum"], ["lsum"])
    K.stt(neglam, lsum[:, 1:2], -lam_init, lsum[:, 0:1], ALU.add, ALU.subtract, ["lsum"], ["neglam"])
    wB = K.sb("wB", [128, 8, 512], BF16)
    qT = K.sb("aqT", [128, NTOK], BF16)
    kT = K.sb("akT", [128, NTOK], BF16)
    vaug = K.sb("avaug", [128, NT, 129], BF16)
    K.memset("pool", vaug[:, :, 128:129], 1.0, T("avaug", range(NT)))
    qk32 = K.sb("qk32", [128, 256], F32)
    sq = K.sb("asq", [128, 256], F32)
    ss4 = K.sb("ss4", [128, 4], F32)
    rs4 = K.sb("rs4", [128, 4], F32)
    xn = K.sb("axn", [128, 256], F32)
    t1 = K.sb("at1", [128, 256], F32)
    t2 = K.sb("at2", [128, 256], F32)
    qkr = [K.sb(f"qkr{i}", [128, 256], BF16) for i in range(2)]
    NE = 4
    ET = [K.sb(f"ET{i}", [128, 512], BF16) for i in range(NE)]
    rr = K.sb("arr", [128, 2], F32)
    tt_ = K.sb("att", [128, 128], F32)
    o32 = K.sb("ao32", [128, 128], F32)
    junk = K.sb("ajunk", [128, 128], F32)
    ss1 = K.sb("ass1", [128, 1], F32)
    rs1 = K.sb("ars1", [128, 1], F32)
    thz = K.sb("athz", [128, 128], F32)
    wz = K.sb("awz", [128, 128], F32)
    y1 = K.sb("ay1", [128, 128], F32)
    y2 = K.sb("ay2", [128, 128], BF16)
    ybT = [K.sb(f"aybT{i}", [128, 512], BF16) for i in range(2)]
    ei = 0
    yi = 0
    for i in range(NB):
        c0 = s * SUBW + 2056 + i * 512
        K.dma("pool", "wB", wB, wslice(K, l, c0, 512), (), ["wB"], max_dma_last_dim=2048)
        for j in range(NT):
            pqk = K.pf[5]
            for k in range(8):
                K.mm(pqk[:, 0:256], K.hT[:, k, j * 128:(j + 1) * 128], wB[:, k, 0:256], k == 0, k == 7,
                     ["wB"] + hk([j], k), ["pf5"])
            for k in range(8):
                K.mm(pqk[:, 256:384], K.hT[:, k, j * 128:(j + 1) * 128], wB[:, k, 256:384], k == 0, k == 7,
                     ["wB"] + hk([j], k), ["pf5"])
            K.cp("dve", vaug[:, j, 0:128], pqk[:, 256:384], ["pf5"], [("avaug", j)])
            K.cp("dve", qk32, pqk[:, 0:256], ["pf5"], ["qk32"])
            K.act(sq, qk32, AF.Square, ["qk32"], ["asq"])
            K.S.op("dve", lambda h: h.tensor_reduce(ss4, sq.rearrange("p (a b) -> p a b", a=4), AX.X, ALU.add),
                   ["asq"], ["ss4"])
            K.rstd(rs4, ss4, 64, ["ss4"], ["rs4"])
            K.tt("dve", xn.rearrange("p (a b) -> p a b", a=4), qk32.rearrange("p (a b) -> p a b", a=4),
                 rs4.unsqueeze(2).to_broadcast([128, 4, 64]), ALU.mult, ["qk32", "rs4"], ["axn"])
            qb = qkr[j % 2]
            if j >= 2:
                jl = j - 2
                K.tt("pool", xn, xn, nwb, ALU.mult, ["axn", "nwb"], ["axn"])
                K.tt("dve", t1.rearrange("p (a b) -> p a b", a=4), xn.rearrange("p (a b) -> p a b", a=4),
                     ropec[:, jl, :].unsqueeze(1).to_broadcast([128, 4, 64]), ALU.mult, ["axn", "ropec"], ["at1"])
                x5 = xn.rearrange("p (a h i) -> p a h i", h=2, i=16)
                t5 = t2.rearrange("p (a h i) -> p a h i", h=2, i=16)
                s5 = ropes[:, jl, :].rearrange("p (r h i) -> p r h i", h=2, i=16)
                for hh in range(2):
                    sin_b = s5[:, :, hh, :].unsqueeze(1).to_broadcast([128, 4, 2, 16])
                    K.tt("pool", t5[:, :, hh, :].rearrange("p (s r) i -> p s r i", r=2),
                         x5[:, :, 1 - hh, :].rearrange("p (s r) i -> p s r i", r=2), sin_b, ALU.mult,
                         ["axn", "ropes"], ["at2"])
                K.tt("dve", qb, t1, t2, ALU.add, ["at1", "at2"], [f"qkr{j % 2}"])
            else:
                K.tt("dve", qb, xn, nwb, ALU.mult, ["axn", "nwb"], [f"qkr{j % 2}"])
            pt = K.pT[j % 2]
            for hf in range(2):
                K.tr(pt[:, hf * 128:(hf + 1) * 128], qb[:, hf * 128:(hf + 1) * 128], K.ident,
                     [f"qkr{j % 2}", "ident"], [f"pT{j % 2}"])
            if j % 2 == 0:
                K.cp("dve", qT[:, j * 128:(j + 1) * 128], pt[:, 0:128], [f"pT{j % 2}"], [("aqT", j)])
                K.cp("dve", kT[:, j * 128:(j + 1) * 128], pt[:, 128:256], [f"pT{j % 2}"], [("akT", j)])
            else:
                K.cp("act", qT[:, j * 128:(j + 1) * 128], pt[:, 0:128], [f"pT{j % 2}"], [("aqT", j)])
                K.cp("act", kT[:, j * 128:(j + 1) * 128], pt[:, 128:256], [f"pT{j % 2}"], [("akT", j)])
        groups = [(256 + 512 * g, 4, list(range(NT))) for g in range(8)]
        if not last:
            groups.append((0, 2, [0, 1]))
        for (q0, nq, kcs) in groups:
            nqc = nq * 128
            qtl = list(tiles_of(q0, nqc))

            started = set()

            def acc(m, qs):
                a = m * 4 + qs
                first = (a // 3) not in started
                return K.pf[2 + a // 3][:, (a % 3) * 129:(a % 3) * 129 + 129], f"pf{2 + a // 3}", first
            iters = [(ki, kc, m) for ki, kc in enumerate(kcs) for m in range(2)]

            def emit_qk(it_):
                _, kc_, m_ = iters[it_]
                K.mm(K.pf[m_][:, 0:nqc], kT[64 * m_:64 * m_ + 64, kc_ * 128:(kc_ + 1) * 128],
                     qT[64 * m_:64 * m_ + 64, q0:q0 + nqc], True, True,
                     [("akT", kc_)] + T("aqT", qtl), [f"pf{m_}"])
            emit_qk(0)
            for it_ in range(len(iters)):
                if it_ + 1 < len(iters):
                    emit_qk(it_ + 1)
                ki, kc, m = iters[it_]
                pS = K.pf[m]
                e = ET[ei % NE]
                K.act(e[:, 0:nqc], pS[:, 0:nqc], AF.Exp, [f"pf{m}"], [f"ET{ei % NE}"])
                for qs in range(nq):
                    ap_, key, first = acc(m, qs)
                    if ki == 0:
                        started.add((m * 4 + qs) // 3)
                    K.mm(ap_, e[:, qs * 128:(qs + 1) * 128], vaug[:, kc, :], (ki == 0 and first),
                         ki == len(kcs) - 1, [f"ET{ei % NE}", ("avaug", kc)], [(key, m, qs)],
                         skip_group_check=True)
                ei += 1
            yb = ybT[yi % 2]
            allacc = [(acc(m_, q_)[1], m_, q_) for m_ in range(2) for q_ in range(nq)]
            for qs in range(nq):
                a0, k0, _ = acc(0, qs)
                a1, k1, _ = acc(1, qs)
                jq = q0 // 128 + qs
                K.S.op("dve", lambda h, a0=a0: h.reciprocal(rr[:, 0:1], a0[:, 128:129]), allacc, ["arr"])
                K.S.op("dve", lambda h, a1=a1: h.reciprocal(rr[:, 1:2], a1[:, 128:129]), [(k1, 1, qs)], ["arr"])
                K.tt("dve", rr[:, 1:2], rr[:, 1:2], neglam, ALU.mult, ["arr", "neglam"], ["arr"])
                K.ts("dve", tt_, a1[:, 0:128], rr[:, 1:2], None, ALU.mult, None, [(k1, 1, qs), "arr"], ["att"])
                K.stt(o32, a0[:, 0:128], rr[:, 0:1], tt_, ALU.mult, ALU.add, [(k0, 0, qs), "arr", "att"], ["ao32"])
                K.act(junk, o32, AF.Square, ["ao32"], ["ajunk"])
                K.S.op("dve", lambda h: h.tensor_reduce(ss1, junk, AX.X, ALU.add), ["ajunk"], ["ass1"])
                K.rstd(rs1, ss1, 128, ["ass1"], ["ars1"])
                pz = K.pf[5]
                for k in range(8):
                    K.mm(pz[:, 0:128], K.hT[:, k, jq * 128:(jq + 1) * 128], wB[:, k, 384:512], k == 0, k == 7,
                         ["wB"] + hk([jq], k), ["pf5"])
                K.act(thz, pz[:, 0:128], AF.Tanh, ["pf5"], ["athz"], scale=0.5)
                K.stt(wz, thz, 1.0, pz[:, 0:128], ALU.add, ALU.mult, ["athz", "pf5"], ["awz"])
                K.stt(y1, o32, rs1, subwb, ALU.mult, ALU.mult, ["ao32", "ars1", "subwb"], ["ay1"])
                K.tt("dve", y2, y1, wz, ALU.mult, ["ay1", "awz"], ["ay2"])
                pt = K.pT[qs % 2]
                K.tr(pt[:, 0:128], y2, K.ident, ["ay2", "ident"], [f"pT{qs % 2}"])
                K.cp("dve", yb[:, qs * 128:(qs + 1) * 128], pt[:, 0:128], [f"pT{qs % 2}"], [f"aybT{yi % 2}"])
            K.dma("sp", f"aybT{yi % 2}", d["yT"][1, i, :, q0:q0 + nqc], yb[:, 0:nqc], [f"aybT{yi % 2}"],
                  [("yT", 1, i, q0)])
            yi += 1


def phase_fft(K, l, s):
    S, d = K.S, K.d
    K.reset()
    f256 = K.sb("f256", [128, 2, 512], BF16)
    rb1 = K.sb("rb1", [64, 128], BF16)
    rb2 = K.sb("rb2", [64, 128], BF16)
    gc2 = K.sb("gc2", [128, 64, 64], BF16)
    gs2 = K.sb("gs2", [128, 64, 64], BF16)
    cc = K.sb("fcc", [128, 2, 256], BF16)
    sc = K.sb("fsc", [128, 2, 256], BF16)
    for nm, t_ in (("f256", f256), ("rb1", rb1), ("rb2", rb2), ("gc2", gc2), ("gs2", gs2), ("cc", cc), ("sc", sc)):
        K.dma("pool", "fc_" + nm, t_, d["c_" + nm], (), ["fc_" + nm], max_dma_last_dim=4096)
    fck = ["fc_f256", "fc_rb1", "fc_rb2", "fc_gc2", "fc_gs2", "fc_cc", "fc_sc"]
    wC = K.sb("wC", [128, 8, 512], BF16)
    cuT = K.sb("cuT", [128, 2, NTOK], BF16)
    gz = K.sb("fgz", [128, NTOK], BF16)
    thz = K.sb("fthz", [128, 512], F32)
    Zt = K.sb("Zt", [64, 2, 64, 2, 64], BF16)
    AT = K.sb("AT", [128, 64, 2, 64], BF16)
    Zc = K.sb("Zc", [128, 2, 2, 128], BF16)
    ycT = K.sb("ycT", [128, NTOK], BF16)
    TGS = [(t0, min(512, NTOK - t0)) for t0 in range(0, NTOK, 512)]
    for i in range(NG):
        c0 = s * SUBW + 4104 + i * 512
        K.dma("pool", "wC", wC, wslice(K, l, c0, 512), (), ["wC"], max_dma_last_dim=2048)
        for dc in range(2):
            for gi, (t0, n) in enumerate(TGS):
                pc = K.pf[gi % 2]
                for k in range(8):
                    K.mm(pc[:, 0:n], wC[:, k, dc * 128:(dc + 1) * 128], K.hT[:, k, t0:t0 + n], k == 0, k == 7,
                         ["wC"] + hk(tiles_of(t0, n), k), [f"pf{gi % 2}"])
                if gi % 2 == 0:
                    K.cp("act", cuT[:, dc, t0:t0 + n], pc[:, 0:n], [f"pf{gi % 2}"], [("cuT", dc)])
                else:
                    K.cp("dve", cuT[:, dc, t0:t0 + n], pc[:, 0:n], [f"pf{gi % 2}"], [("cuT", dc)])
        for e in range(2):
            for gi, (t0, n) in enumerate(TGS):
                pc = K.pf[gi % 2]
                for k in range(8):
                    K.mm(pc[:, 0:n], wC[:, k, 256 + e * 128:256 + (e + 1) * 128], K.hT[:, k, t0:t0 + n], k == 0,
                         k == 7, ["wC"] + hk(tiles_of(t0, n), k), [f"pf{gi % 2}"])
                K.act(thz[:, 0:n], pc[:, 0:n], AF.Tanh, [f"pf{gi % 2}"], ["fthz"], scale=0.5)
                K.stt(gz[:, t0:t0 + n], thz[:, 0:n], 1.0, pc[:, 0:n], ALU.add, ALU.mult, ["fthz", f"pf{gi % 2}"],
                      ["fgz"])
            rcols = f256[:, :, :].rearrange("p c (r x) -> p c r x", r=2)[:, :, :, e * 128:(e + 1) * 128]
            for jc in range(2):
                pa = K.pf[2]
                for dc in range(2):
                    K.mm(pa[:, 0:256].rearrange("p (r x) -> p r x", r=2), cuT[:, dc, jc * 128:(jc + 1) * 128],
                         rcols[:, dc], dc == 0, dc == 1, [("cuT", 0), ("cuT", 1), "fc_f256"], ["pf2"])
                K.cp("dve", Zc[:, jc].rearrange("p r x -> p (r x)"), pa[:, 0:256], ["pf2"], ["Zc"])
            pcx = K.pf[3]
            n_mm = 0
            for jc in range(2):
                for r, tab in ((0, cc), (1, sc)):
                    K.mm(pcx[:, 0:256], Zc[:, jc, r, :], tab[:, jc, :], n_mm == 0, n_mm == 3,
                         ["Zc", "fc_cc", "fc_sc"], ["pf3"])
                    n_mm += 1
            K.tt("dve", ycT[:, 0:256], pcx[:, 0:256], gz[:, 0:256], ALU.mult, ["pf3", "fgz"], ["ycT"])
            for np_ in range(32):
                pa = K.pf[2 + np_ % 2]
                for q in range(2):
                    n2 = 2 * np_ + q
                    for dc in range(2):
                        lat = cuT[:, dc, NCTX:NTOK].rearrange("p (a b) -> p a b", b=64)[:, :, n2]
                        K.mm(pa[0:64, q * 256:(q + 1) * 256].rearrange("p (r x) -> p r x", r=2), lat, rcols[:, dc],
                             dc == 0, dc == 1, [("cuT", 0), ("cuT", 1), "fc_f256"], [f"pf{2 + np_ % 2}"])
                src_ = pa[0:64, 0:512].rearrange("p (q r m x) -> p q r m x", q=2, r=2, m=2)
                for m_ in range(2):
                    dst_ = Zt[:, :, :, m_, 2 * np_:2 * np_ + 2].rearrange("p r x q -> p q r x")
                    if np_ % 2 == 0:
                        K.cp("dve", dst_, src_[:, :, :, m_, :], [f"pf{2 + np_ % 2}"], ["Zt"])
                    else:
                        K.cp("act", dst_, src_[:, :, :, m_, :], [f"pf{2 + np_ % 2}"], ["Zt"])
            for ib in range(16):
                pb_ = K.pf[4 + ib % 2]
                for q in range(4):
                    ii = 4 * ib + q
                    for r, rb in ((0, rb1), (1, rb2)):
                        lhs = Zt[:, r, ii, :, :].rearrange("p m n -> p (m n)")
                        K.mm(pb_[:, q * 128:(q + 1) * 128], lhs, rb, r == 0, r == 1, ["Zt", "fc_rb1", "fc_rb2"],
                             [f"pf{4 + ib % 2}"])
                src_ = pb_[:, 0:512].rearrange("p (q r k) -> p q r k", q=4, r=2)
                dst_ = AT[:, :, :, 4 * ib:4 * ib + 4].rearrange("p k r q -> p q r k")
                if ib % 2 == 0:
                    K.cp("dve", dst_, src_, [f"pf{4 + ib % 2}"], ["AT"])
                else:
                    K.cp("act", dst_, src_, [f"pf{4 + ib % 2}"], ["AT"])
            for kb in range(8):
                pc = K.pf[kb % 2]
                for q in range(8):
                    k1 = 8 * kb + q
                    for m in range(2):
                        rows = slice(64 * m, 64 * m + 64)
                        for r, tab in ((0, gc2), (1, gs2)):
                            K.mm(pc[rows, q * 64:(q + 1) * 64], AT[rows, k1, r, :], tab[rows, k1, :], r == 0, r == 1,
                                 ["AT", "fc_gc2", "fc_gs2"], [f"pf{kb % 2}"], skip_group_check=True)
                lat_y = ycT[:, NCTX:NTOK].rearrange("p (k2 k1) -> p k1 k2", k1=64)[:, 8 * kb:8 * kb + 8, :]
                lat_g = gz[:, NCTX:NTOK].rearrange("p (k2 k1) -> p k1 k2", k1=64)[:, 8 * kb:8 * kb + 8, :]
                K.tt("dve", lat_y, pc[:, 0:512].rearrange("p (q k) -> p q k", q=8), lat_g, ALU.mult,
                     [f"pf{kb % 2}", "fgz"], ["ycT"])
            K.dma("sp", "ycT", d["yT"][2, 2 * i + e, :, :], ycT, ["ycT"], [("yT", 2, i, e)])


ORD = (list(range(NCH)), [3, 2, 1, 0] + list(range(NCH - 1, 3, -1)))


def phase_mlstm(K, l, s):
    S, d = K.S, K.d
    K.reset()
    cf = {}
    for nm, shp in (("maskf", [128, 64]), ("maskb", [128, 64]), ("trif", [128, 128]), ("trib", [128, 128]),
                    ("chsel", [128, 2]), ("sel", [NA, NA, 128]), ("dlt", [NA, NT, NA]), ("ones2", [NA, 64])):
        cf[nm] = K.sb("m_" + nm, shp, F32)
        K.dma("sp", "mc_" + nm, cf[nm], d["c_" + nm], (), ["m_" + nm])
    wg = K.sb("wg", [128, 8, 4 * NA], BF16)
    bifb = K.sb("bifb", [128, 4 * NA], F32)
    anwb = K.sb("anwb", [128, 512], F32)
    K.dma("pool", "wg", wg, wslice(K, l, s * SUBW + 2048, 4 * NA), (), ["wg"])
    K.dma("sp", "bifb", bifb, d["bif"][l, s * 4 * NA:(s + 1) * 4 * NA].partition_broadcast(128), (), ["bifb"])
    K.dma("sp", "anwb", anwb, d["anw"][l, 512 * s:512 * s + 512].partition_broadcast(128), (), ["anwb"])
    K.ts("dve", anwb, anwb, 0.25, None, ALU.mult, None, ["anwb"], ["anwb"])
    G = 4 * NA
    Gt = K.sb("Gt", [128, NT, G], F32)
    pgt = K.pf[0][:, 0:NT * G]
    for j in range(NT):
        for k in range(8):
            K.mm(pgt[:, j * G:(j + 1) * G], K.hT[:, k, j * 128:(j + 1) * 128], wg[:, k, :], k == 0, k == 7,
                 ["wg"] + hk([j], k), ["pf0"])
    K.tt("dve", Gt, pgt.rearrange("p (j g) -> p j g", g=G), bifb.unsqueeze(1).to_broadcast([128, NT, G]), ALU.add,
         ["pf0", "bifb"], ["Gt"])
    WK = K.sb("WK", [128, 2, NT, NA], F32)
    EC = K.sb("EC", [128, 2, NT, NA], F32)
    decB = K.sb("decB", [128, 2, NA, NCH], F32)
    sh3 = [128, NT, NA]
    gA = K.sb("gA", sh3, F32); gE = K.sb("gE", sh3, F32); gL = K.sb("gL", sh3, F32); Fg = K.sb("Fg", sh3, F32)
    Bs = K.sb("Bs", sh3, F32); U = K.sb("U", sh3, F32); g1 = K.sb("g1", sh3, F32); g2 = K.sb("g2", sh3, F32)
    blastF = K.sb("blastF", [NA, NCH], F32); umaxF = K.sb("umaxF", [NA, NCH], F32); mst = K.sb("mst", [NA, NCH], F32)
    Mc = K.sb("Mc", [NA, NCH], F32); dd = K.sb("dd", [NA, NCH], F32); dec = K.sb("dec", [NA, NCH], F32)
    Rp = [K.sb(f"Rp{i}", [NA, NT, NA], F32) for i in range(2)]
    fl = lambda t_: t_.rearrange("p j h -> p (j h)")
    for D_ in range(2):
        PI = Gt[:, :, 2 * D_ * NA:2 * D_ * NA + NA]
        PF = Gt[:, :, (2 * D_ + 1) * NA:(2 * D_ + 2) * NA]
        tri = cf["trif" if D_ == 0 else "trib"]
        K.stt(gA, PF, -1.0, PF, ALU.mult, ALU.max, ["Gt"], ["gA"])
        K.act(gE, gA, AF.Exp, ["gA"], ["gE"], scale=-1.0)
        K.act(gL, gE, AF.Ln, ["gE"], ["gL"], bias=1.0)
        K.stt(Fg, PF, 0.0, gL, ALU.min, ALU.subtract, ["Gt", "gL"], ["Fg"])
        K.mm(K.pf[1][:, 0:NT * NA], tri, fl(Fg), True, True, ["Fg", "m_trif", "m_trib"], ["pf1"])
        K.cp("dve", fl(Bs), K.pf[1][:, 0:NT * NA], ["pf1"], ["Bs"])
        K.tt("dve", U, PI, Bs, ALU.subtract, ["Gt", "Bs"], ["U"])
        for j in range(NT):
            K.mm(K.pf[2][0:NA, 2 * j:2 * j + 2], Fg[:, j, :], cf["chsel"], True, True, ["Fg", "m_chsel"], ["pf2"])
        K.cp("dve", blastF, K.pf[2][0:NA, 0:NCH], ["pf2"], ["blastF"])
        for jb in range(0, NT, 4):
            nb_ = min(4, NT - jb)
            pu = K.pf[3 + (jb // 4) % 2]
            for q in range(nb_):
                K.tr(pu[0:NA, q * 128:(q + 1) * 128], U[:, jb + q, :], K.identf, ["U", "identf"],
                     [f"pf{3 + (jb // 4) % 2}"])
            K.S.op("dve", lambda h, pu=pu, jb=jb, nb_=nb_: h.tensor_reduce(
                umaxF[:, 2 * jb:2 * jb + 2 * nb_], pu[0:NA, 0:nb_ * 128].rearrange("p (c t) -> p c t", t=64),
                AX.X, ALU.max), [f"pf{3 + (jb // 4) % 2}"], ["umaxF"])
        od = ORD[D_]
        K.memset("dve", mst[:, od[0]:od[0] + 1], 0.0, ["mst"])
        for ix in range(NCH - 1):
            c, nx = od[ix], od[ix + 1]
            K.ts("dve", mst[:, nx:nx + 1], mst[:, c:c + 1], umaxF[:, c:c + 1], blastF[:, c:c + 1], ALU.max, ALU.add,
                 ["mst", "umaxF", "blastF"], ["mst"])
        K.tt("dve", Mc, mst, umaxF, ALU.max, ["mst", "umaxF"], ["Mc"])
        K.tt("dve", dd, mst, Mc, ALU.subtract, ["mst", "Mc"], ["dd"])
        K.act(dec, dd, AF.Exp, ["dd"], ["dec"])
        for h_ in range(NA):
            K.mm(K.pf[5][:, h_ * NCH:(h_ + 1) * NCH], cf["sel"][:, h_, :], dec, True, True, ["dec", "m_sel"], ["pf5"])
        K.cp("dve", decB[:, D_].rearrange("p h c -> p (h c)"), K.pf[5][:, 0:NA * NCH], ["pf5"], [("decB", D_)])
        for pi in range(2):
            K.tt("dve", Rp[pi], cf["dlt"], Mc[:, pi::2].unsqueeze(2).to_broadcast([NA, NT, NA]), ALU.mult,
                 ["Mc", "m_dlt"], [f"Rp{pi}"])
            K.mm(K.pf[1][64 * pi:64 * pi + 64, 0:NT * NA], cf["ones2"], fl(Rp[pi]), True, True,
                 [f"Rp{pi}", "m_ones2"], ["pf1"])
        mt = K.pf[1][:, 0:NT * NA]
        K.tt("dve", fl(g1), fl(U), mt, ALU.subtract, ["U", "pf1"], ["g1"])
        K.act(fl(WK[:, D_]), fl(g1), AF.Exp, ["g1"], [("WK", D_)])
        K.stt(fl(g2), fl(Bs), -1.0, mt, ALU.mult, ALU.subtract, ["Bs", "pf1"], ["g2"])
        K.act(fl(EC[:, D_]), fl(g2), AF.Exp, ["g2"], [("EC", D_)])
    wA = K.sb("wA", [128, 8, 1024], BF16)
    qT = K.sb("mqT", [128, NTOK], BF16)
    kT = K.sb("mkT", [128, NTOK], BF16)
    vaug = K.sb("mvaug", [128, NT, 257], BF16)
    Kp = [K.sb(f"Kp{i}", [128, NT, 128], BF16) for i in range(2)]
    hacc = K.sb("hacc", [128, NT, 256], F32)
    Cst = K.sb("Cst", [128, 257], F32)
    Cdb = [K.sb(f"Cdb{i}", [128, 257], BF16) for i in range(2)]
    aT = [K.sb(f"maT{i}", [128, 64], BF16) for i in range(2)]
    dm = K.sb("mdm", [128, 2], F32)
    tho = K.sb("tho", [128, 512], F32)
    w1 = K.sb("mw1", [128, 256], F32)
    w2 = K.sb("mw2", [128, 256], F32)
    y1 = K.sb("my1", [128, 256], F32)
    y2 = K.sb("my2", [128, 256], BF16)
    junk = K.sb("mjunk", [128, 256], F32)
    ssA = K.sb("ssA", [128, NT], F32)
    rsA = K.sb("rsA", [128, NT], F32)
    ybuf = [K.sb(f"mybuf{i}", [128, 2, 128], BF16) for i in range(2)]
    K.memset("pool", vaug[:, :, 256:257], 1.0, T("mvaug", range(NT)))
    TGS = [(t0, min(512, NTOK - t0)) for t0 in range(0, NTOK, 512)]
    mask = (cf["maskf"], cf["maskb"])
    for i in range(NA):
        K.dma("pool", "wA", wA, wslice(K, l, s * SUBW + i * 1024, 1024), (), ["wA"], max_dma_last_dim=4096)
        for gi, (t0, n) in enumerate(TGS):
            tl = tiles_of(t0, n)
            for k in range(8):
                K.mm(K.pf[0][:, 0:n], wA[:, k, 0:128], K.hT[:, k, t0:t0 + n], k == 0, k == 7, ["wA"] + hk(tl, k),
                     ["pf0"])
            K.cp("act", qT[:, t0:t0 + n], K.pf[0][:, 0:n], ["pf0"], T("mqT", tl))
            for k in range(8):
                K.mm(K.pf[1][:, 0:n], wA[:, k, 128:256], K.hT[:, k, t0:t0 + n], k == 0, k == 7, ["wA"] + hk(tl, k),
                     ["pf1"])
            K.ts("dve", kT[:, t0:t0 + n], K.pf[1][:, 0:n], 128 ** -0.5, None, ALU.mult, None, ["pf1"], T("mkT", tl))
        for j in range(NT):
            pkv = K.pf[2 + j % 2]
            for k in range(8):
                K.mm(pkv[:, 0:384], K.hT[:, k, j * 128:(j + 1) * 128], wA[:, k, 128:512], k == 0, k == 7,
                     ["wA"] + hk([j], k), [f"pf{2 + j % 2}"])
            for D_ in range(2):
                K.ts("dve", Kp[D_][:, j, :], pkv[:, 0:128], WK[:, D_, j, i:i + 1], 128 ** -0.5, ALU.mult, ALU.mult,
                     [f"pf{2 + j % 2}", ("WK", D_)], [("Kp", D_, j)])
            K.cp("dve", vaug[:, j, 0:256], pkv[:, 128:384], [f"pf{2 + j % 2}"], [("mvaug", j)])
        it = 0
        for D_ in range(2):
            K.memset("pool", Cst, 0.0, ["Cst"])
            for c in ORD[D_]:
                j, pi = divmod(c, 2)
                rows = slice(64 * pi, 64 * pi + 64)
                t0 = 64 * c
                b2 = it % 2
                dsc = decB[:, D_, i, c:c + 1]
                K.ts("pool", Cdb[b2], Cst, dsc, 1.0, ALU.mult, ALU.mult, ["Cst", ("decB", D_)], [f"Cdb{b2}"])
                pqk = K.pf[b2]
                K.mm(pqk[rows, 0:64], kT[:, t0:t0 + 64], qT[:, t0:t0 + 64], True, True, [("mkT", j), ("mqT", j)],
                     [f"pf{b2}"])
                pdc = K.pf[4 + b2]
                K.mm(pdc[:, 0:257], Kp[D_][rows, j, :], vaug[rows, j, :], True, True, [("Kp", D_, j), ("mvaug", j)],
                     [f"pf{4 + b2}"])
                K.stt(Cst, Cst, dsc, pdc[:, 0:257], ALU.mult, ALU.add, ["Cst", ("decB", D_), f"pf{4 + b2}"], ["Cst"])
                K.stt(aT[b2][rows, :], pqk[rows, 0:64], WK[rows, D_, j, i:i + 1], mask[D_][rows, :], ALU.mult,
                      ALU.mult, [f"pf{b2}", ("WK", D_), "m_maskf", "m_maskb"], [f"maT{b2}"])
                pnum = K.pf[2 + b2]
                K.mm(pnum[rows, 0:257], aT[b2][rows, :], vaug[rows, j, :], True, False, [f"maT{b2}", ("mvaug", j)],
                     [f"pf{2 + b2}"])
                K.mm(pnum[rows, 0:257], qT[:, t0:t0 + 64], Cdb[b2], False, True, [("mqT", j), f"Cdb{b2}"],
                     [f"pf{2 + b2}"])
                K.ts("dve", dm[rows, 0:1], pnum[rows, 256:257], EC[rows, D_, j, i:i + 1], None, ALU.max, None,
                     [f"pf{2 + b2}", ("EC", D_)], ["mdm"])
                K.stt(dm[rows, 0:1], pnum[rows, 256:257], -1.0, dm[rows, 0:1], ALU.mult, ALU.max,
                      [f"pf{2 + b2}", "mdm"], ["mdm"])
                K.S.op("dve", lambda h, rows=rows: h.reciprocal(dm[rows, 1:2], dm[rows, 0:1]), ["mdm"], ["mdm"])
                if D_ == 0:
                    K.act(hacc[rows, j, :], pnum[rows, 0:256], AF.Copy, [f"pf{2 + b2}", "mdm"], [("hacc", j)],
                          scale=dm[rows, 1:2])
                else:
                    K.stt(hacc[rows, j, :], pnum[rows, 0:256], dm[rows, 1:2], hacc[rows, j, :], ALU.mult, ALU.add,
                          [f"pf{2 + b2}", "mdm", ("hacc", j)], [("hacc", j)])
                it += 1
        for j in range(NT):
            K.act(junk, hacc[:, j, :], AF.Square, [("hacc", j)], ["mjunk"])
            K.S.op("dve", lambda h, j=j: h.tensor_reduce(ssA[:, j:j + 1], junk, AX.X, ALU.add), ["mjunk"], [("ssA", j)])
            K.rstd(rsA[:, j:j + 1], ssA[:, j:j + 1], 256, [("ssA", j)], [("rsA", j)])
            poz = K.pf[j % 2]
            for k in range(8):
                K.mm(poz[:, 0:512], K.hT[:, k, j * 128:(j + 1) * 128], wA[:, k, 512:1024], k == 0, k == 7,
                     ["wA"] + hk([j], k), [f"pf{j % 2}"])
            K.act(tho, poz[:, 0:512], AF.Tanh, [f"pf{j % 2}"], ["tho"], scale=0.5)
            K.stt(w1, tho[:, 256:512], 1.0, poz[:, 256:512], ALU.add, ALU.mult, ["tho", f"pf{j % 2}"], ["mw1"])
            K.stt(w2, tho[:, 0:256], 1.0, w1, ALU.add, ALU.mult, ["tho", "mw1"], ["mw2"])
            K.stt(y1, hacc[:, j, :], rsA[:, j:j + 1], anwb[:, i * 256:(i + 1) * 256], ALU.mult, ALU.mult,
                  [("hacc", j), ("rsA", j), "anwb"], ["my1"])
            K.tt("dve", y2, y1, w2, ALU.mult, ["my1", "mw2"], ["my2"])
            pt = K.pT[j % 2]
            for c2 in range(2):
                K.tr(pt[:, c2 * 128:(c2 + 1) * 128], y2[:, c2 * 128:(c2 + 1) * 128], K.ident, ["my2", "ident"],
                     [f"pT{j % 2}"])
            yb = ybuf[j % 2]
            K.cp("dve", yb.rearrange("p c t -> p (c t)"), pt[:, 0:256], [f"pT{j % 2}"], [f"mybuf{j % 2}"])
            K.dma("sp", f"mybuf{j % 2}", d["yT"][0, 2 * i:2 * i + 2, :, j * 128:(j + 1) * 128].rearrange("c p t -> p c t"),
                  yb, [f"mybuf{j % 2}"], [("yT", 0, i, j)])


def pair_exchange(K, tg):
    d = K.d
    groups = [[0, 1], [2, 3], [4, 5], [6, 7]]
    r0 = tg * 256
    rk = [("xp", 2 * tg), ("xp", 2 * tg + 1)]
    wk = [("xd", 2 * tg), ("xd", 2 * tg + 1)]
    K.S.dma("pool", "cc", lambda h: h.collective_compute("AllReduce", ALU.add, replica_groups=groups,
                                                         ins=[d["xpart"][r0:r0 + 256, :].opt()],
                                                         outs=[d["xcur"][r0:r0 + 256, :].opt()]),
            rk, wk, inc=1)
```

```python
import math
from contextlib import ExitStack

import numpy as np
import concourse.bass as bass
import concourse.mybir as mybir
from concourse.bass_utils import run_bass_kernel_spmd

F32 = mybir.dt.float32
BF16 = mybir.dt.bfloat16
AF = mybir.ActivationFunctionType
ALU = mybir.AluOpType
AX = mybir.AxisListType

D = 1024
NCTX = 256
NLAT = 4096
NTOK = NCTX + NLAT
NT = NTOK // 128
NCH = NTOK // 64
DEPTH = 4
EPS = 1e-6
NSUB = 2
NA = 2
NB = 4
NG = 2
SUBW = 5128
SB_BASE = 16640
SB_END = 229376


class Slot:
    def __init__(self, S, name):
        self.sem = S.new_sem(name)
        self.total = 0


class Sched:
    ENG = ("pe", "act", "dve", "pool", "sp")

    def __init__(self, nc, stack):
        self.nc = nc
        self.stack = stack
        self.ops = {e: [] for e in self.ENG}
        self.sem = {e: self.new_sem("s_" + e) for e in self.ENG}
        self.cnt = {e: 0 for e in self.ENG}
        self.waited = {}
        self.lastw = {}
        self.reads = {}
        self.slots = {}
        self.same_engine_sync = True

    def new_sem(self, name):
        return self.stack.enter_context(self.nc.semaphore(name))

    def slot(self, name):
        if name not in self.slots:
            self.slots[name] = Slot(self, "d_" + name)
        return self.slots[name]

    def _need(self, eng, toks, tok):
        if tok is None:
            return
        sem, val, src = tok
        if src == eng and (eng == "pe" or not self.same_engine_sync):
            return
        if self.waited.get((eng, id(sem)), 0) >= val:
            return
        cur = toks.get(id(sem))
        if cur is None or cur[1] < val:
            toks[id(sem)] = (sem, val)

    def _deps(self, eng, reads, writes):
        toks = {}
        for k in reads:
            self._need(eng, toks, self.lastw.get(k))
        for k in writes:
            self._need(eng, toks, self.lastw.get(k))
            for t in self.reads.get(k, ()):
                self._need(eng, toks, t)
        waits = list(toks.values())
        for sem, val in waits:
            self.waited[(eng, id(sem))] = val
        return waits

    def _commit(self, tok, reads, writes):
        for k in reads:
            self.reads.setdefault(k, []).append(tok)
        for k in writes:
            self.lastw[k] = tok
            self.reads[k] = []

    def op(self, eng, fn, reads=(), writes=()):
        waits = self._deps(eng, reads, writes)
        self.cnt[eng] += 1
        tok = (self.sem[eng], self.cnt[eng], eng)
        self.ops[eng].append((waits, fn, (self.sem[eng], 1)))
        self._commit(tok, reads, writes)
        return tok

    def dma(self, eng, slotname, fn, reads=(), writes=(), inc=16):
        slot = self.slot(slotname)
        waits = self._deps(eng, reads, writes)
        slot.total += inc
        tok = (slot.sem, slot.total, "dma")
        self.ops[eng].append((waits, fn, (slot.sem, inc)))
        self._commit(tok, reads, writes)
        return tok

    def barrier(self):
        toks = [(self.sem[e], self.cnt[e], "x") for e in self.ENG if self.cnt[e] > 0]
        toks += [(s.sem, s.total, "dma") for s in self.slots.values() if s.total > 0]
        for e in self.ENG:
            need = {}
            for t in toks:
                if t[0] is self.sem[e]:
                    continue
                self._need(e, need, t)
            waits = list(need.values())
            for sem, val in waits:
                self.waited[(e, id(sem))] = val
            if waits:
                self.ops[e].append((waits, None, None))

    def emit(self):
        nc = self.nc
        with nc.Block() as block:
            def mk(e):
                def body(h):
                    for waits, fn, inc in self.ops[e]:
                        for sem, val in waits:
                            h.wait_ge(sem, val)
                        if fn is not None:
                            fn(h).then_inc(inc[0], inc[1])
                return body
            block.tensor(mk("pe"))
            block.scalar(mk("act"))
            block.vector(mk("dve"))
            block.gpsimd(mk("pool"))
            block.sync(mk("sp"))


def T(name, idxs):
    return [(name, i) for i in idxs]


def hk(js, k):
    return [("hT", j, k) for j in js]


def tiles_of(t0, n):
    return range(t0 // 128, (t0 + n + 127) // 128)


def make_consts():
    c = {}
    p = np.arange(128)
    c["ident"] = np.eye(128, dtype=np.float32)
    tok = (np.arange(32)[None, :] * 128 + p[:, None]).astype(np.float32)
    row = np.floor(tok / 64.0).astype(np.float32)
    col = (tok - row * 64.0).astype(np.float32)
    inv = (10000.0 ** (-np.arange(0, 32, 2, dtype=np.float32) / 32.0)).astype(np.float32)
    ar = (row[..., None] * inv).astype(np.float32)
    ac = (col[..., None] * inv).astype(np.float32)
    cosT = np.concatenate([np.cos(ar), np.cos(ar), np.cos(ac), np.cos(ac)], axis=-1)
    sinT = np.concatenate([-np.sin(ar), np.sin(ar), -np.sin(ac), np.sin(ac)], axis=-1)
    c["ropec"] = cosT.astype(np.float32)
    c["ropes"] = sinT.astype(np.float32)
    s = p % 64
    t = np.arange(64)
    c["maskf"] = (s[:, None] <= t[None, :]).astype(np.float32)
    c["maskb"] = (s[:, None] >= t[None, :]).astype(np.float32)
    same = (p[:, None] // 64) == (p[None, :] // 64)
    c["trif"] = (same & (p[:, None] <= p[None, :])).astype(np.float32)
    c["trib"] = (same & (p[:, None] >= p[None, :])).astype(np.float32)
    c["chsel"] = ((p[:, None] // 64) == np.arange(2)[None, :]).astype(np.float32)
    sel = np.zeros((NA, NA, 128), np.float32)
    for h in range(NA):
        sel[h, h, :] = 1.0
    c["sel"] = sel
    dl = np.zeros((NA, NT, NA), np.float32)
    for h in range(NA):
        dl[h, :, h] = 1.0
    c["dlt"] = dl
    c["ones2"] = np.ones((NA, 64), np.float32)
    d = np.arange(256)
    ang = 2 * np.pi * np.outer(d, d) / 256.0
    f256 = np.concatenate([np.cos(ang), -np.sin(ang)], axis=1) / 16.0
    c["f256"] = f256.reshape(2, 128, 512).transpose(1, 0, 2).astype(np.float32)
    n = np.arange(64)
    a64 = 2 * np.pi * np.outer(n, n) / 64.0
    C64, S64 = np.cos(a64) / 8.0, np.sin(a64) / 8.0
    c["rb1"] = np.concatenate([C64, -S64], axis=1).astype(np.float32)
    c["rb2"] = np.concatenate([S64, C64], axis=1).astype(np.float32)
    n2 = (p % 64)[:, None, None]
    k1 = np.arange(64)[None, :, None]
    k2 = np.arange(64)[None, None, :]
    ag = 2 * np.pi * (n2 * (k1 + 64 * k2) % 4096) / 4096.0
    c["gc2"] = (np.cos(ag) / 16.0).astype(np.float32)
    c["gs2"] = (np.sin(ag) / 16.0).astype(np.float32)
    nn = np.arange(256)
    a256 = 2 * np.pi * (np.outer(nn, nn) % 256) / 256.0
    c["cc"] = (np.cos(a256) / 32.0).reshape(2, 128, 256).transpose(1, 0, 2).astype(np.float32)
    c["sc"] = (np.sin(a256) / 32.0).reshape(2, 128, 256).transpose(1, 0, 2).astype(np.float32)
    return c


def win_cols(subsets):
    cols = []
    for s in subsets:
        for i in range(NA):
            h = NA * s + i
            cols += list(range(0 + 128 * h, 128 * h + 128))
            cols += list(range(512 + 128 * h, 512 + 128 * h + 128))
            cols += list(range(1024 + 256 * h, 1024 + 256 * h + 256))
            cols += list(range(2064 + 256 * h, 2064 + 256 * h + 256))
            cols += list(range(3088 + 256 * h, 3088 + 256 * h + 256))
        for kind in range(4):
            for i in range(NA):
                cols.append(2048 + kind * 4 + NA * s + i)
        for i in range(NB):
            h = NB * s + i
            cols += list(range(4112 + 128 * h, 4112 + 128 * h + 128))
            cols += list(range(5136 + 128 * h, 5136 + 128 * h + 128))
            cols += list(range(6160 + 128 * h, 6160 + 128 * h + 128))
            cols += list(range(7184 + 128 * h, 7184 + 128 * h + 128))
        for i in range(NG):
            g = NG * s + i
            cols += list(range(8208 + 256 * g, 8208 + 256 * g + 256))
            cols += list(range(9232 + 256 * g, 9232 + 256 * g + 256))
    cols += list(range(10256, 13328))
    assert len(cols) == len(subsets) * SUBW + 3 * D
    return np.asarray(cols)


def gate_bias_idx(subsets):
    idx = []
    for s in subsets:
        for kind in range(4):
            for i in range(NA):
                idx.append(kind * 4 + NA * s + i)
    return np.asarray(idx)


def feat_rows(subsets):
    return np.concatenate([np.arange(512 * s, 512 * s + 512) for s in subsets])


class StopBuild(Exception):
    pass


class Ctx:
    def cut(self, name):
        if self.cfg.get("cut") == name:
            raise StopBuild()

    def __init__(self, nc, st, cfg):
        self.nc = nc
        self.st = st
        self.cfg = cfg
        self.S = Sched(nc, st)
        self.pair = bool(cfg.get("pair", False))
        self.nsub = 1 if self.pair else NSUB
        self.mg0 = self.nsub * SUBW
        self.pers = SB_BASE
        self.scr0 = None
        self.scr = None
        self.uid = 0
        self.d = {}
        self.ps = {}

    def _alloc(self, name, shape, dt, off):
        self.uid += 1
        t = self.nc.alloc_sbuf_tensor_at(f"{name}_{self.uid}", list(shape), dt, offset=off)
        return t.ap()

    @staticmethod
    def _bytes(shape, dt):
        n = 1
        for s in shape[1:]:
            n *= s
        b = n * (2 if dt == BF16 else 4)
        return (b + 63) // 64 * 64

    def pb(self, name, shape, dt):
        assert self.scr0 is None
        off = self.pers
        self.pers += self._bytes(shape, dt)
        assert self.pers <= SB_END, "persistent SBUF overflow"
        return self._alloc(name, shape, dt, off)

    def freeze(self):
        self.scr0 = self.pers
        self.scr = self.scr0

    def reset(self):
        self.S.barrier()
        self.scr = self.scr0

    def sb(self, name, shape, dt):
        off = self.scr
        self.scr += self._bytes(shape, dt)
        assert self.scr <= SB_END, f"scratch SBUF overflow at {name}: {self.scr - SB_END}"
        return self._alloc(name, shape, dt, off)

    def mm(self, out, lhsT, rhs, start, stop, r, w, **kw):
        return self.S.op("pe", lambda h: h.matmul(out, lhsT, rhs, start=start, stop=stop, **kw), r, w)

    def tr(self, out, in_, ident, r, w):
        return self.S.op("pe", lambda h: h.transpose(out, in_, ident), r, w)

    def act(self, out, in_, func, r, w, scale=1.0, bias=0.0, accum_out=None):
        if accum_out is None:
            return self.S.op("act", lambda h: h.activation(out=out, in_=in_, func=func, bias=bias, scale=scale), r, w)
        return self.S.op("act", lambda h: h.activation(out=out, in_=in_, func=func, bias=bias, scale=scale,
                                                       accum_out=accum_out), r, w)

    def tt(self, eng, out, in0, in1, op, r, w):
        return self.S.op(eng, lambda h: h.tensor_tensor(out, in0, in1, op), r, w)

    def ts(self, eng, out, in0, s1, s2, op0, op1, r, w):
        if s2 is None:
            return self.S.op(eng, lambda h: h.tensor_scalar(out, in0, s1, None, op0), r, w)
        return self.S.op(eng, lambda h: h.tensor_scalar(out, in0, s1, s2, op0, op1), r, w)

    def stt(self, out, in0, scalar, in1, op0, op1, r, w, accum_out=None):
        if accum_out is None:
            return self.S.op("dve", lambda h: h.scalar_tensor_tensor(out, in0, scalar, in1, op0, op1), r, w)
        return self.S.op("dve", lambda h: h.scalar_tensor_tensor(out, in0, scalar, in1, op0, op1,
                                                                 accum_out=accum_out), r, w)

    def cp(self, eng, out, in_, r, w):
        if eng == "act":
            return self.S.op("act", lambda h: h.copy(out, in_), r, w)
        return self.S.op(eng, lambda h: h.tensor_copy(out, in_), r, w)

    def memset(self, eng, ap, val, w):
        return self.S.op(eng, lambda h: h.memset(ap, val), (), w)

    def dma(self, eng, slot, out, in_, r, w, **kw):
        return self.S.dma(eng, slot, lambda h: h.dma_start(out=out, in_=in_, **kw), r, w)

    def rstd(self, out, ss, n, r, w):
        self.ts("dve", out, ss, 1.0 / n, EPS, ALU.mult, ALU.add, r, w)
        self.act(out, out, AF.Sqrt, w, w)
        return self.S.op("dve", lambda h: h.reciprocal(out, out), w, w)


def wslice(K, l, c0, n):
    return K.d["win"][l].rearrange("(k p) n -> p k n", p=128)[:, :, c0:c0 + n]


CONST_SHAPES = dict(ident=[128, 128], ropec=[128, 32, 64], ropes=[128, 32, 64], maskf=[128, 64], maskb=[128, 64],
                    trif=[128, 128], trib=[128, 128], chsel=[128, 2], sel=[NA, NA, 128], dlt=[NA, NT, NA],
                    ones2=[NA, 64], f256=[128, 2, 512], rb1=[64, 128], rb2=[64, 128], gc2=[128, 64, 64],
                    gs2=[128, 64, 64], cc=[128, 2, 256], sc=[128, 2, 256])


def declare_dram(K):
    nc, d = K.nc, K.d
    DEPTH = K.cfg.get("layers", 4)
    inp = lambda n, s: nc.dram_tensor(n, list(s), F32, kind="ExternalInput").ap()
    d["xin"] = inp("xin", [NTOK, D])
    d["cvec"] = inp("cvec", [128, 8, 2])
    d["normw"] = inp("normw", [DEPTH, 128, 8])
    d["wada"] = inp("wada", [DEPTH, D, 3 * D])
    d["bada"] = inp("bada", [DEPTH, 128, 24])
    ns = K.nsub
    d["win"] = inp("win", [DEPTH, D, ns * SUBW + 3 * D])
    d["bif"] = inp("bif", [DEPTH, ns * 4 * NA])
    d["anw"] = inp("anw", [DEPTH, ns * 512])
    d["qknw"] = inp("qknw", [DEPTH, 256])
    d["lam4"] = inp("lam4", [DEPTH, 256])
    d["subw"] = inp("subw", [DEPTH, 128])
    for n in ("wao", "wbo", "wco"):
        d[n] = inp(n, [DEPTH, ns * 512, D])
    d["wo"] = inp("wo", [DEPTH, D, D])
    for n, s in CONST_SHAPES.items():
        d["c_" + n] = inp("c_" + n, s)
    d["out"] = nc.dram_tensor("out", [NLAT, D], F32, kind="ExternalOutput").ap()
    d["xctx"] = nc.dram_tensor("xctx", [NCTX, D], F32, kind="Internal").ap()
    d["yT"] = nc.dram_tensor("yT", [3, 4, 128, NTOK], BF16, kind="Internal").ap()
    d["gsc"] = nc.dram_tensor("gsc", [2, D], F32, kind="Internal").ap()
    if K.pair:
        d["xpart"] = nc.dram_tensor("xpart", [NTOK, D], F32, kind="Internal").ap()
        d["xcur"] = nc.dram_tensor("xcur", [NTOK, D], F32, kind="Internal").ap()
    if K.cfg.get("dbg"):
        d["dbg_hT"] = nc.dram_tensor("dbg_hT", [128, 8, NTOK], BF16, kind="ExternalOutput").ap()
        d["dbg_yT"] = nc.dram_tensor("dbg_yT", [K.nsub, 3, 4, 128, NTOK], BF16, kind="ExternalOutput").ap()
        d["dbg_ctx"] = nc.dram_tensor("dbg_ctx", [NCTX, D], F32, kind="ExternalOutput").ap()


def xrows(K, l, j, write=False):
    if K.pair:
        if write:
            return K.d["xpart"][j * 128:(j + 1) * 128, :], ("xp", j)
        src = K.d["xin"] if l == 0 else K.d["xcur"]
        return src[j * 128:(j + 1) * 128, :], ("xd", j)
    if j < 2:
        src = K.d["xin"] if (l == 0 and not write) else K.d["xctx"]
        return src[j * 128:(j + 1) * 128, :], ("xd", j)
    jj = j - 2
    if l == 0 and not write:
        return K.d["xin"][NCTX + jj * 128:NCTX + (jj + 1) * 128, :], ("xd", j)
    return K.d["out"][jj * 128:(jj + 1) * 128, :], ("xd", j)


def setup(K):
    S, d = K.S, K.d
    K.hT = K.pb("hT", [128, 8, NTOK], BF16)
    K.ident = K.pb("ident", [128, 128], BF16)
    K.identf = K.pb("identf", [128, 128], F32)
    K.c_mhalf = K.pb("mhalf", [128, 64], F32)
    K.scT = K.pb("scT", [128, 8, 2], BF16)
    K.A1 = K.pb("A1", [128, 8, 2], F32)
    K.B1 = K.pb("B1", [128, 8, 2], F32)
    K.gateb = K.pb("gateb", [128, 2, D], F32)
    K.lam = K.pb("lam", [128, 1], F32)
    K.freeze()
    nc = K.nc
    K.pT = [K.st.enter_context(nc.psum_tensor(f"pT{i}", [128, 1024], BF16)) for i in range(2)]
    K.pf = [K.st.enter_context(nc.psum_tensor(f"pf{i}", [128, 512], F32)) for i in range(6)]
    K.dma("pool", "c0", K.ident, d["c_ident"], (), ["ident"])
    K.dma("sp", "c1", K.identf, d["c_ident"], (), ["identf"])
    K.memset("dve", K.c_mhalf, -0.5, ["mhalf"])
    cv = K.sb("cv", [128, 8, 2], F32)
    th = K.sb("cth", [128, 8, 2], F32)
    K.dma("sp", "c2", cv, d["cvec"], (), ["cv"])
    K.act(th, cv, AF.Tanh, ["cv"], ["cth"], scale=0.5)
    K.stt(th, th, 1.0, cv, ALU.add, ALU.mult, ["cth", "cv"], ["cth"])
    K.ts("dve", K.scT, th, 0.5, None, ALU.mult, None, ["cth"], ["scT"])


def phase_norm(K, l):
    S, d = K.S, K.d
    K.reset()
    normw = K.sb("normw", [128, 8], F32)
    bada = K.sb("bada", [128, 24], F32)
    modT = K.sb("modT", [128, 24, 2], F32)
    wad = K.sb("wad", [128, 8, D], BF16)
    K.dma("sp", "sv0", normw, d["normw"][l], (), ["normw"])
    K.dma("sp", "sv1", bada, d["bada"][l], (), ["bada"])
    pm = K.pf[0][:, 0:48]
    wv = d["wada"][l].rearrange("(k p) n -> p k n", p=128)
    for third in range(3):
        K.dma("pool", "wad", wad, wv[:, :, third * D:(third + 1) * D], (), ["wad"], max_dma_last_dim=4096)
        for m in range(8):
            g = third * 8 + m
            for k in range(8):
                K.mm(pm[:, 2 * g:2 * g + 2], wad[:, k, m * 128:(m + 1) * 128], K.scT[:, k, :], k == 0, k == 7,
                     ["wad", "scT"], ["pf0"])
    K.tt("dve", modT, K.pf[0][:, 0:48].rearrange("p (g j) -> p g j", j=2),
         bada.unsqueeze(2).to_broadcast([128, 24, 2]), ALU.add, ["pf0", "bada"], ["modT"])
    K.stt(K.A1, modT[:, 8:16, :], 1.0, normw.unsqueeze(2).to_broadcast([128, 8, 2]), ALU.add, ALU.mult,
          ["modT", "normw"], ["A1"])
    K.cp("dve", K.B1, modT[:, 0:8, :], ["modT"], ["B1"])
    gt = K.sb("gt", [128, 2, 8], F32)
    gtT = K.sb("gtT", [8, 2, 128], F32)
    for j in range(2):
        K.ts("dve", gt[:, j, :], modT[:, 16:24, j], 0.5, None, ALU.mult, None, ["modT"], ["gt"])
    for j in range(2):
        K.tr(K.pf[1][0:8, j * 128:(j + 1) * 128], gt[:, j, :], K.identf, ["gt", "identf"], ["pf1"])
    K.cp("dve", gtT, K.pf[1][0:8, 0:256].rearrange("p (j f) -> p j f", j=2), ["pf1"], ["gtT"])
    K.dma("sp", "gsc", d["gsc"].rearrange("j (k p) -> k j p", p=128), gtT, ["gtT"], ["gsc"])
    for j in range(2):
        K.dma("sp", "gsc", K.gateb[:, j, :], d["gsc"][j].partition_broadcast(128), ["gsc"], ["gateb"])
    K.cut("p0")
    NXB = 3
    xt = [K.sb(f"xt{i}", [128, D], F32) for i in range(NXB)]
    xn = [K.sb(f"xn{i}", [128, D], BF16) for i in range(2)]
    junk = K.sb("junk", [128, D], F32)
    ss = K.sb("ss", [128, NT], F32)
    rs = K.sb("rs", [128, NT], F32)
    for j in range(NT):
        b = j % NXB
        src, xkey = xrows(K, l, j)
        K.dma("sp", f"xt{b}", xt[b], src, [xkey], [f"xt{b}"])
        K.act(junk, xt[b], AF.Square, [f"xt{b}"], ["junk"])
        K.S.op("dve", lambda h, j=j: h.tensor_reduce(ss[:, j:j + 1], junk, AX.X, ALU.add), ["junk"], [("ss", j)])
        K.cut("p1a")
        K.rstd(rs[:, j:j + 1], ss[:, j:j + 1], D, [("ss", j)], [("rs", j)])
        K.cut("p1b")
        nb = j % 2
        K.ts("dve", xn[nb], xt[b], rs[:, j:j + 1], None, ALU.mult, None, [f"xt{b}", ("rs", j)], [f"xn{nb}"])
        K.cut("p1c")
        for k in range(8):
            K.tr(K.pT[k // 4][:, (k % 4) * 128:(k % 4 + 1) * 128], xn[nb][:, k * 128:(k + 1) * 128], K.ident,
                 [f"xn{nb}", "ident"], [f"pT{k // 4}"])
        K.cut("p1d")
        jc = 1 if j < 2 else 0
        for k in range(8):
            o = K.hT[:, k, j * 128:(j + 1) * 128]
            i_ = K.pT[k // 4][:, (k % 4) * 128:(k % 4 + 1) * 128]
            if k < 4:
                K.act(o, i_, AF.Identity, [f"pT{k // 4}", "A1", "B1"], [("hT", j, k)], scale=K.A1[:, k, jc:jc + 1],
                      bias=K.B1[:, k, jc:jc + 1])
            else:
                K.ts("dve", o, i_, K.A1[:, k, jc:jc + 1], K.B1[:, k, jc:jc + 1], ALU.mult, ALU.add,
                     [f"pT{k // 4}", "A1", "B1"], [("hT", j, k)])
            if k == 0:
                K.cut("p1k0")
            if k == 1:
                K.cut("p1k1")
        K.cut(f"p1e{j}")
    K.cut("p1f")


def finish(K):
    K.S.barrier()


def build(cfg):
    nc = bass.Bass("TRN2", target_bir_lowering=False)
    with ExitStack() as st:
        K = Ctx(nc, st, cfg)
        declare_dram(K)
        try:
            build_body(K, cfg)
        except StopBuild:
            pass
        finish(K)
        K.S.emit()
    return nc


def build_body(K, cfg):
    if True:
        setup(K)
        K.cut("setup")
        for l in range(cfg.get("layers", DEPTH)):
            phase_norm(K, l)
            if cfg.get("dbg") == "norm":
                K.dma("sp", "dbg", K.d["dbg_hT"], K.hT, [("hT", j, k) for j in range(NT) for k in range(8)], ["dbg"])
                break
            for s in range(K.nsub):
                if "mlstm" not in cfg.get("skip", ()):
                    phase_mlstm(K, l, s)
                if "attn" not in cfg.get("skip", ()):
                    phase_attn(K, l, s)
                if "fft" not in cfg.get("skip", ()):
                    phase_fft(K, l, s)
                if cfg.get("dbg"):
                    K.S.barrier()
                    K.dma("sp", "dbg", K.d["dbg_yT"][s], K.d["yT"], ["yT"], ["dbg"])
                if "merge" not in cfg.get("skip", ()):
                    phase_merge(K, l, s)
            if K.pair and l == cfg.get("layers", DEPTH) - 1:
                K.dma("sp", "fin", K.d["out"], K.d["xcur"][NCTX:NTOK, :], [("xd", j) for j in range(2, NT)], ["outfin"])
        if cfg.get("dbg"):
            K.S.barrier()
            K.dma("sp", "dbg", K.d["dbg_ctx"], K.d["xctx"], [("xd", 0), ("xd", 1)], ["dbg"])


def prep_shared(inp, DEPTH=DEPTH, subsets=(0, 1)):
    f = lambda a: np.ascontiguousarray(np.asarray(a, dtype=np.float32))
    inp = {k: (np.asarray(v)[:DEPTH] if k not in ("x", "c", "ctx", "c_ctx") else v) for k, v in inp.items()}
    subsets = list(subsets)
    sh = {}
    sh["normw"] = f(inp["norm_w"].reshape(DEPTH, 8, 128).transpose(0, 2, 1))
    sh["wada"] = f(inp["w_ada"])
    sh["bada"] = f(inp["b_ada"].reshape(DEPTH, 24, 128).transpose(0, 2, 1))
    sh["win"] = f(inp["w_in"][:, :, win_cols(subsets)])
    sh["bif"] = f(inp["b_if"][:, gate_bias_idx(subsets)])
    fr = feat_rows(subsets)
    sh["anw"] = f(inp["a_norm_w"][:, fr])
    sh["qknw"] = f(np.concatenate([inp["q_norm_w"], inp["q_norm_w"], inp["k_norm_w"], inp["k_norm_w"]], axis=1))
    sh["lam4"] = f(np.concatenate([inp["lambda_q1"], inp["lambda_k1"], inp["lambda_q2"], inp["lambda_k2"]], axis=1))
    sh["subw"] = f(inp["subln_w"])
    sh["wao"] = f(inp["w_a_out"][:, fr]); sh["wbo"] = f(inp["w_b_out"][:, fr]); sh["wco"] = f(inp["w_c_out"][:, fr])
    sh["wo"] = f(inp["w_out"])
    for n, v in make_consts().items():
        assert list(v.shape) == CONST_SHAPES[n], (n, v.shape)
        sh["c_" + n] = f(v)
    return sh


def prep_core(inp, b):
    m = {}
    m["xin"] = np.ascontiguousarray(np.concatenate([inp["ctx"][b], inp["x"][b]], axis=0).astype(np.float32))
    cv = np.stack([np.asarray(inp["c"][b]), np.asarray(inp["c_ctx"])], axis=-1)
    m["cvec"] = np.ascontiguousarray(cv.reshape(8, 128, 2).transpose(1, 0, 2).astype(np.float32))
    return m


_NC_CACHE = {}


PAIR = True


def kernel(**inputs):
    cfg = {"pair": PAIR}
    if "full" not in _NC_CACHE:
        _NC_CACHE["full"] = build(cfg)
    nc = _NC_CACHE["full"]
    in_maps = []
    if PAIR:
        shs = [prep_shared(inputs, DEPTH, (s,)) for s in range(NSUB)]
        cores = [prep_core(inputs, b) for b in range(4)]
        for core in range(8):
            m = dict(shs[core % 2])
            m.update(cores[core // 2])
            in_maps.append(m)
        res = run_bass_kernel_spmd(nc, in_maps, core_ids=list(range(8)))
        out = np.stack([np.asarray(res.results[2 * b]["out"]) for b in range(4)], axis=0)
    else:
        sh = prep_shared(inputs)
        for core in range(8):
            m = dict(sh)
            m.update(prep_core(inputs, core % 4))
            in_maps.append(m)
        res = run_bass_kernel_spmd(nc, in_maps, core_ids=list(range(8)))
        out = np.stack([np.asarray(res.results[b]["out"]) for b in range(4)], axis=0)
    return out.astype(np.float32)


def phase_merge(K, l, s):
    S, d = K.S, K.d
    last = (l == K.cfg.get("nlayers_total", DEPTH) - 1)
    K.reset()
    TG = 256
    wmg = K.sb("wmg", [128, 8, 3 * D], BF16)
    wbr = [K.sb(f"wbr{b}", [128, 4, D], BF16) for b in range(3)]
    wo = K.sb("wo", [128, 8, D], BF16)
    for b in range(3):
        K.dma("pool", f"wmg{b}", wmg[:, :, b * D:(b + 1) * D], wslice(K, l, K.mg0 + b * D, D), (), [("wmg", b)],
              max_dma_last_dim=4096)
        srcw = d[("wao", "wbo", "wco")[b]][l][512 * s:512 * s + 512, :].rearrange("(k p) n -> p k n", p=128)
        K.dma("pool", f"wbr{b}", wbr[b], srcw, (), [("wbr", b)], max_dma_last_dim=4096)
    K.dma("pool", "wo", wo, d["wo"][l].rearrange("(k p) n -> p k n", p=128), (), ["wo"], max_dma_last_dim=4096)
    ybr = [[K.sb(f"ybr{b}_{i}", [128, 4, TG], BF16) for b in range(3)] for i in range(2)]
    th = [K.sb(f"mth{i}", [128, TG], F32) for i in range(2)]
    tmp = [K.sb(f"mtmp{i}", [128, TG], F32) for i in range(2)]
    yacc = K.sb("yacc", [128, 8, TG], F32)
    yTb = K.sb("yTb", [128, 8, TG], BF16)
    xt = [K.sb(f"mxt{i}", [128, D], F32) for i in range(2)]
    tmp2 = K.sb("mtmp2", [128, 512], F32)
    it = 0
    xi = 0
    pend = []
    for tg in range(NTOK // TG):
        if tg == 0 and last:
            continue
        t0 = tg * TG
        jc = 1 if tg == 0 else 0
        yb = ybr[tg % 2]
        for b in range(3):
            K.dma("sp", f"ybr{b}_{tg % 2}", yb[b], d["yT"][b, :, :, t0:t0 + TG].rearrange("c p t -> p c t"),
                  ["yT"], [f"ybr{b}_{tg % 2}"])
        tls = tiles_of(t0, TG)
        for m in range(8):
            for b in range(3):
                pg = K.pf[it % 2]
                pp = K.pf[2 + it % 2]
                for k in range(8):
                    K.mm(pg[:, 0:TG], wmg[:, k, b * D + m * 128:b * D + (m + 1) * 128], K.hT[:, k, t0:t0 + TG],
                         k == 0, k == 7, [("wmg", b)] + hk(tls, k), [f"pf{it % 2}"])
                K.act(th[it % 2], pg[:, 0:TG], AF.Tanh, [f"pf{it % 2}"], [f"mth{it % 2}"], scale=0.5)
                for kc in range(4):
                    K.mm(pp[:, 0:TG], wbr[b][:, kc, m * 128:(m + 1) * 128], yb[b][:, kc, :], kc == 0, kc == 3,
                         [("wbr", b), f"ybr{b}_{tg % 2}"], [f"pf{2 + it % 2}"])
                if b == 0:
                    K.stt(yacc[:, m, :], th[it % 2], 1.0, pp[:, 0:TG], ALU.add, ALU.mult,
                          [f"mth{it % 2}", f"pf{2 + it % 2}"], [("yacc", m)])
                else:
                    K.stt(tmp[it % 2], th[it % 2], 1.0, pp[:, 0:TG], ALU.add, ALU.mult,
                          [f"mth{it % 2}", f"pf{2 + it % 2}"], [f"mtmp{it % 2}"])
                    if b == 1:
                        K.tt("pool", yacc[:, m, :], yacc[:, m, :], tmp[it % 2], ALU.add,
                             [f"mtmp{it % 2}", ("yacc", m)], [("yacc", m)])
                    else:
                        K.tt("pool", yTb[:, m, :], yacc[:, m, :], tmp[it % 2], ALU.add,
                             [f"mtmp{it % 2}", ("yacc", m)], [("yTb", m)])
                it += 1
        for tsub in range(TG // 128):
            j = t0 // 128 + tsub
            xb = xt[xi % 2]
            src, xkey = xrows(K, l if s == 0 else 99, j)
            K.dma("sp", f"mxt{xi % 2}", xb, src, [xkey], [f"mxt{xi % 2}"])
            if K.pair:
                K.ts("pool", xb, xb, 0.5, 1.0, ALU.mult, ALU.mult, [f"mxt{xi % 2}"], [f"mxt{xi % 2}"])
            for half in range(2):
                po = K.pf[4 + half]
                for m in range(8):
                    K.mm(po[:, 0:512], yTb[:, m, tsub * 128:(tsub + 1) * 128], wo[:, m, half * 512:(half + 1) * 512],
                         m == 0, m == 7, [("yTb", m), "wo"], [f"pf{4 + half}"])
                K.tt("dve", tmp2, po[:, 0:512], K.gateb[:, jc, half * 512:(half + 1) * 512], ALU.mult,
                     [f"pf{4 + half}", "gateb"], ["mtmp2"])
                K.tt("pool", xb[:, half * 512:(half + 1) * 512], xb[:, half * 512:(half + 1) * 512], tmp2, ALU.add,
                     ["mtmp2", f"mxt{xi % 2}"], [f"mxt{xi % 2}"])
            dst, xkey = xrows(K, l, j, write=True)
            K.dma("sp", f"mxt{xi % 2}", dst, xb, [f"mxt{xi % 2}"], [xkey])
            xi += 1
        if K.pair:
            if pend:
                pair_exchange(K, pend.pop())
            pend.append(tg)
    if K.pair and pend:
        pair_exchange(K, pend.pop())


def phase_attn(K, l, s):
    S, d = K.S, K.d
    last = (l == K.cfg.get("nlayers_total", DEPTH) - 1)
    lam_init = 0.8 - 0.6 * math.exp(-0.3 * l)
    K.reset()
    ropec = K.sb("ropec", [128, 32, 64], F32)
    ropes = K.sb("ropes", [128, 32, 64], F32)
    nwb = K.sb("nwb", [128, 256], F32)
    subwb = K.sb("subwb", [128, 128], F32)
    lam4 = K.sb("lam4", [128, 256], F32)
    lsum = K.sb("lsum", [128, 2], F32)
    neglam = K.sb("neglam", [128, 1], F32)
    K.dma("sp", "a0", ropec, d["c_ropec"], (), ["ropec"])
    K.dma("sp", "a1", ropes, d["c_ropes"], (), ["ropes"])
    K.dma("sp", "a2", nwb, d["qknw"][l].partition_broadcast(128), (), ["nwb"])
    K.dma("sp", "a3", subwb, d["subw"][l].partition_broadcast(128), (), ["subwb"])
    K.dma("sp", "a4", lam4, d["lam4"][l].partition_broadcast(128), (), ["lam4"])
    K.ts("dve", nwb[:, 0:128], nwb[:, 0:128], 0.125, None, ALU.mult, None, ["nwb"], ["nwb"])
    K.ts("dve", subwb, subwb, (1.0 - lam_init) * 0.5, None, ALU.mult, None, ["subwb"], ["subwb"])
    lp = K.sb("lp", [128, 2, 64], F32)
    l4 = lam4.rearrange("p (a b c) -> p a b c", a=2, b=2)
    K.tt("dve", lp, l4[:, :, 0, :], l4[:, :, 1, :], ALU.mult, ["lam4"], ["lp"])
    K.S.op("dve", lambda h: h.tensor_reduce(lsum, lp, AX.X, ALU.add), ["lp"], ["lsum"])
    K.act(lsum, lsum, AF.Exp, ["lsum"], ["lsum"])
    K.stt(neglam, lsum[:, 1:2], -lam_init, lsum[:, 0:1], ALU.add, ALU.subtract, ["lsum"], ["neglam"])
    wB = K.sb("wB", [128, 8, 512], BF16)
    qT = K.sb("aqT", [128, NTOK], BF16)
    kT = K.sb("akT", [128, NTOK], BF16)
    vaug = K.sb("avaug", [128, NT, 129], BF16)
    K.memset("pool", vaug[:, :, 128:129], 1.0, T("avaug", range(NT)))
    qk32 = K.sb("qk32", [128, 256], F32)
    sq = K.sb("asq", [128, 256], F32)
    ss4 = K.sb("ss4", [128, 4], F32)
    rs4 = K.sb("rs4", [128, 4], F32)
    xn = K.sb("axn", [128, 256], F32)
    t1 = K.sb("at1", [128, 256], F32)
    t2 = K.sb("at2", [128, 256], F32)
    qkr = [K.sb(f"qkr{i}", [128, 256], BF16) for i in range(2)]
    NE = 6
    ET = [K.sb(f"ET{i}", [128, 512], BF16) for i in range(NE)]
    rr = K.sb("arr", [128, 2], F32)
    tt_ = K.sb("att", [128, 128], F32)
    o32 = K.sb("ao32", [128, 128], F32)
    junk = K.sb("ajunk", [128, 128], F32)
    ss1 = K.sb("ass1", [128, 1], F32)
    rs1 = K.sb("ars1", [128, 1], F32)
    thz = K.sb("athz", [128, 128], F32)
    wz = K.sb("awz", [128, 128], F32)
    y1 = K.sb("ay1", [128, 128], F32)
    y2 = K.sb("ay2", [128, 128], BF16)
    ybT = [K.sb(f"aybT{i}", [128, 512], BF16) for i in range(2)]
    ei = 0
    yi = 0
    for i in range(NB):
        c0 = s * SUBW + 2056 + i * 512
        K.dma("pool", "wB", wB, wslice(K, l, c0, 512), (), ["wB"], max_dma_last_dim=2048)
        for j in range(NT):
            pqk = K.pf[5]
            for k in range(8):
                K.mm(pqk[:, 0:256], K.hT[:, k, j * 128:(j + 1) * 128], wB[:, k, 0:256], k == 0, k == 7,
                     ["wB"] + hk([j], k), ["pf5"])
            for k in range(8):
                K.mm(pqk[:, 256:384], K.hT[:, k, j * 128:(j + 1) * 128], wB[:, k, 256:384], k == 0, k == 7,
                     ["wB"] + hk([j], k), ["pf5"])
            K.cp("dve", vaug[:, j, 0:128], pqk[:, 256:384], ["pf5"], [("avaug", j)])
            K.cp("dve", qk32, pqk[:, 0:256], ["pf5"], ["qk32"])
            K.act(sq, qk32, AF.Square, ["qk32"], ["asq"])
            K.S.op("dve", lambda h: h.tensor_reduce(ss4, sq.rearrange("p (a b) -> p a b", a=4), AX.X, ALU.add),
                   ["asq"], ["ss4"])
            K.rstd(rs4, ss4, 64, ["ss4"], ["rs4"])
            K.tt("dve", xn.rearrange("p (a b) -> p a b", a=4), qk32.rearrange("p (a b) -> p a b", a=4),
                 rs4.unsqueeze(2).to_broadcast([128, 4, 64]), ALU.mult, ["qk32", "rs4"], ["axn"])
            qb = qkr[j % 2]
            if j >= 2:
                jl = j - 2
                K.tt("pool", xn, xn, nwb, ALU.mult, ["axn", "nwb"], ["axn"])
                K.tt("dve", t1.rearrange("p (a b) -> p a b", a=4), xn.rearrange("p (a b) -> p a b", a=4),
                     ropec[:, jl, :].unsqueeze(1).to_broadcast([128, 4, 64]), ALU.mult, ["axn", "ropec"], ["at1"])
                x5 = xn.rearrange("p (a h i) -> p a h i", h=2, i=16)
                t5 = t2.rearrange("p (a h i) -> p a h i", h=2, i=16)
                s5 = ropes[:, jl, :].rearrange("p (r h i) -> p r h i", h=2, i=16)
                for hh in range(2):
                    sin_b = s5[:, :, hh, :].unsqueeze(1).to_broadcast([128, 4, 2, 16])
                    K.tt("pool", t5[:, :, hh, :].rearrange("p (s r) i -> p s r i", r=2),
                         x5[:, :, 1 - hh, :].rearrange("p (s r) i -> p s r i", r=2), sin_b, ALU.mult,
                         ["axn", "ropes"], ["at2"])
                K.tt("dve", qb, t1, t2, ALU.add, ["at1", "at2"], [f"qkr{j % 2}"])
            else:
                K.tt("dve", qb, xn, nwb, ALU.mult, ["axn", "nwb"], [f"qkr{j % 2}"])
            pt = K.pT[j % 2]
            for hf in range(2):
                K.tr(pt[:, hf * 128:(hf + 1) * 128], qb[:, hf * 128:(hf + 1) * 128], K.ident,
                     [f"qkr{j % 2}", "ident"], [f"pT{j % 2}"])
            if j % 2 == 0:
                K.cp("dve", qT[:, j * 128:(j + 1) * 128], pt[:, 0:128], [f"pT{j % 2}"], [("aqT", j)])
                K.cp("dve", kT[:, j * 128:(j + 1) * 128], pt[:, 128:256], [f"pT{j % 2}"], [("akT", j)])
            else:
                K.cp("act", qT[:, j * 128:(j + 1) * 128], pt[:, 0:128], [f"pT{j % 2}"], [("aqT", j)])
                K.cp("act", kT[:, j * 128:(j + 1) * 128], pt[:, 128:256], [f"pT{j % 2}"], [("akT", j)])
        groups = [(256 + 512 * g, 4, list(range(NT))) for g in range(8)]
        if not last:
            groups.append((0, 2, [0, 1]))
        for (q0, nq, kcs) in groups:
            nqc = nq * 128
            qtl = list(tiles_of(q0, nqc))

            started = set()

            def acc(m, qs):
                a = m * 4 + qs
                first = (a // 3) not in started
                return K.pf[2 + a // 3][:, (a % 3) * 129:(a % 3) * 129 + 129], f"pf{2 + a // 3}", first
            iters = [(ki, kc, m) for ki, kc in enumerate(kcs) for m in range(2)]

            def emit_qk(it_):
                _, kc_, m_ = iters[it_]
                K.mm(K.pf[m_][:, 0:nqc], kT[64 * m_:64 * m_ + 64, kc_ * 128:(kc_ + 1) * 128],
                     qT[64 * m_:64 * m_ + 64, q0:q0 + nqc], True, True,
                     [("akT", kc_)] + T("aqT", qtl), [f"pf{m_}"])
            emit_qk(0)
            for it_ in range(len(iters)):
                if it_ + 1 < len(iters):
                    emit_qk(it_ + 1)
                ki, kc, m = iters[it_]
                pS = K.pf[m]
                e = ET[ei % NE]
                K.act(e[:, 0:nqc], pS[:, 0:nqc], AF.Exp, [f"pf{m}"], [f"ET{ei % NE}"])
                for qs in range(nq):
                    ap_, key, first = acc(m, qs)
                    if ki == 0:
                        started.add((m * 4 + qs) // 3)
                    K.mm(ap_, e[:, qs * 128:(qs + 1) * 128], vaug[:, kc, :], (ki == 0 and first),
                         ki == len(kcs) - 1, [f"ET{ei % NE}", ("avaug", kc)], [(key, m, qs)],
                         skip_group_check=True)
                ei += 1
            yb = ybT[yi % 2]
            allacc = [(acc(m_, q_)[1], m_, q_) for m_ in range(2) for q_ in range(nq)]
            for qs in range(nq):
                a0, k0, _ = acc(0, qs)
                a1, k1, _ = acc(1, qs)
                jq = q0 // 128 + qs
                K.S.op("dve", lambda h, a0=a0: h.reciprocal(rr[:, 0:1], a0[:, 128:129]), allacc, ["arr"])
                K.S.op("dve", lambda h, a1=a1: h.reciprocal(rr[:, 1:2], a1[:, 128:129]), [(k1, 1, qs)], ["arr"])
                K.tt("dve", rr[:, 1:2], rr[:, 1:2], neglam, ALU.mult, ["arr", "neglam"], ["arr"])
                K.ts("dve", tt_, a1[:, 0:128], rr[:, 1:2], None, ALU.mult, None, [(k1, 1, qs), "arr"], ["att"])
                K.stt(o32, a0[:, 0:128], rr[:, 0:1], tt_, ALU.mult, ALU.add, [(k0, 0, qs), "arr", "att"], ["ao32"])
                K.act(junk, o32, AF.Square, ["ao32"], ["ajunk"])
                K.S.op("dve", lambda h: h.tensor_reduce(ss1, junk, AX.X, ALU.add), ["ajunk"], ["ass1"])
                K.rstd(rs1, ss1, 128, ["ass1"], ["ars1"])
                pz = K.pf[5]
                for k in range(8):
                    K.mm(pz[:, 0:128], K.hT[:, k, jq * 128:(jq + 1) * 128], wB[:, k, 384:512], k == 0, k == 7,
                         ["wB"] + hk([jq], k), ["pf5"])
                K.act(thz, pz[:, 0:128], AF.Tanh, ["pf5"], ["athz"], scale=0.5)
                K.stt(wz, thz, 1.0, pz[:, 0:128], ALU.add, ALU.mult, ["athz", "pf5"], ["awz"])
                K.stt(y1, o32, rs1, subwb, ALU.mult, ALU.mult, ["ao32", "ars1", "subwb"], ["ay1"])
                K.tt("dve", y2, y1, wz, ALU.mult, ["ay1", "awz"], ["ay2"])
                pt = K.pT[qs % 2]
                K.tr(pt[:, 0:128], y2, K.ident, ["ay2", "ident"], [f"pT{qs % 2}"])
                K.cp("dve", yb[:, qs * 128:(qs + 1) * 128], pt[:, 0:128], [f"pT{qs % 2}"], [f"aybT{yi % 2}"])
            K.dma("sp", f"aybT{yi % 2}", d["yT"][1, i, :, q0:q0 + nqc], yb[:, 0:nqc], [f"aybT{yi % 2}"],
                  [("yT", 1, i, q0)])
            yi += 1


def phase_fft(K, l, s):
    S, d = K.S, K.d
    K.reset()
    f256 = K.sb("f256", [128, 2, 512], BF16)
    rb1 = K.sb("rb1", [64, 128], BF16)
    rb2 = K.sb("rb2", [64, 128], BF16)
    gc2 = K.sb("gc2", [128, 64, 64], BF16)
    gs2 = K.sb("gs2", [128, 64, 64], BF16)
    cc = K.sb("fcc", [128, 2, 256], BF16)
    sc = K.sb("fsc", [128, 2, 256], BF16)
    for nm, t_ in (("f256", f256), ("rb1", rb1), ("rb2", rb2), ("gc2", gc2), ("gs2", gs2), ("cc", cc), ("sc", sc)):
        K.dma("pool", "fc_" + nm, t_, d["c_" + nm], (), ["fc_" + nm], max_dma_last_dim=4096)
    fck = ["fc_f256", "fc_rb1", "fc_rb2", "fc_gc2", "fc_gs2", "fc_cc", "fc_sc"]
    wC = K.sb("wC", [128, 8, 512], BF16)
    cuT = K.sb("cuT", [128, 2, NTOK], BF16)
    gz = K.sb("fgz", [128, NTOK], BF16)
    thz = K.sb("fthz", [128, 512], F32)
    Zt = K.sb("Zt", [64, 2, 64, 2, 64], BF16)
    AT = K.sb("AT", [128, 64, 2, 64], BF16)
    Zc = K.sb("Zc", [128, 2, 2, 128], BF16)
    ycT = K.sb("ycT", [128, NTOK], BF16)
    TGS = [(t0, min(512, NTOK - t0)) for t0 in range(0, NTOK, 512)]
    for i in range(NG):
        c0 = s * SUBW + 4104 + i * 512
        K.dma("pool", "wC", wC, wslice(K, l, c0, 512), (), ["wC"], max_dma_last_dim=2048)
        for dc in range(2):
            for gi, (t0, n) in enumerate(TGS):
                pc = K.pf[gi % 2]
                for k in range(8):
                    K.mm(pc[:, 0:n], wC[:, k, dc * 128:(dc + 1) * 128], K.hT[:, k, t0:t0 + n], k == 0, k == 7,
                         ["wC"] + hk(tiles_of(t0, n), k), [f"pf{gi % 2}"])
                if gi % 2 == 0:
                    K.cp("act", cuT[:, dc, t0:t0 + n], pc[:, 0:n], [f"pf{gi % 2}"], [("cuT", dc)])
                else:
                    K.cp("dve", cuT[:, dc, t0:t0 + n], pc[:, 0:n], [f"pf{gi % 2}"], [("cuT", dc)])
        for e in range(2):
            for gi, (t0, n) in enumerate(TGS):
                pc = K.pf[gi % 2]
                for k in range(8):
                    K.mm(pc[:, 0:n], wC[:, k, 256 + e * 128:256 + (e + 1) * 128], K.hT[:, k, t0:t0 + n], k == 0,
                         k == 7, ["wC"] + hk(tiles_of(t0, n), k), [f"pf{gi % 2}"])
                K.act(thz[:, 0:n], pc[:, 0:n], AF.Tanh, [f"pf{gi % 2}"], ["fthz"], scale=0.5)
                K.stt(gz[:, t0:t0 + n], thz[:, 0:n], 1.0, pc[:, 0:n], ALU.add, ALU.mult, ["fthz", f"pf{gi % 2}"],
                      ["fgz"])
            rcols = f256[:, :, :].rearrange("p c (r x) -> p c r x", r=2)[:, :, :, e * 128:(e + 1) * 128]
            for jc in range(2):
                pa = K.pf[2]
                for dc in range(2):
                    K.mm(pa[:, 0:256].rearrange("p (r x) -> p r x", r=2), cuT[:, dc, jc * 128:(jc + 1) * 128],
                         rcols[:, dc], dc == 0, dc == 1, [("cuT", 0), ("cuT", 1), "fc_f256"], ["pf2"])
                K.cp("dve", Zc[:, jc].rearrange("p r x -> p (r x)"), pa[:, 0:256], ["pf2"], ["Zc"])
            pcx = K.pf[3]
            n_mm = 0
            for jc in range(2):
                for r, tab in ((0, cc), (1, sc)):
                    K.mm(pcx[:, 0:256], Zc[:, jc, r, :], tab[:, jc, :], n_mm == 0, n_mm == 3,
                         ["Zc", "fc_cc", "fc_sc"], ["pf3"])
                    n_mm += 1
            K.tt("dve", ycT[:, 0:256], pcx[:, 0:256], gz[:, 0:256], ALU.mult, ["pf3", "fgz"], ["ycT"])
            for np_ in range(32):
                pa = K.pf[2 + np_ % 2]
                for q in range(2):
                    n2 = 2 * np_ + q
                    for dc in range(2):
                        lat = cuT[:, dc, NCTX:NTOK].rearrange("p (a b) -> p a b", b=64)[:, :, n2]
                        K.mm(pa[0:64, q * 256:(q + 1) * 256].rearrange("p (r x) -> p r x", r=2), lat, rcols[:, dc],
                             dc == 0, dc == 1, [("cuT", 0), ("cuT", 1), "fc_f256"], [f"pf{2 + np_ % 2}"])
                src_ = pa[0:64, 0:512].rearrange("p (q r m x) -> p q r m x", q=2, r=2, m=2)
                for m_ in range(2):
                    dst_ = Zt[:, :, :, m_, 2 * np_:2 * np_ + 2].rearrange("p r x q -> p q r x")
                    if np_ % 2 == 0:
                        K.cp("dve", dst_, src_[:, :, :, m_, :], [f"pf{2 + np_ % 2}"], ["Zt"])
                    else:
                        K.cp("act", dst_, src_[:, :, :, m_, :], [f"pf{2 + np_ % 2}"], ["Zt"])
            for ib in range(16):
                pb_ = K.pf[4 + ib % 2]
                for q in range(4):
                    ii = 4 * ib + q
                    for r, rb in ((0, rb1), (1, rb2)):
                        lhs = Zt[:, r, ii, :, :].rearrange("p m n -> p (m n)")
                        K.mm(pb_[:, q * 128:(q + 1) * 128], lhs, rb, r == 0, r == 1, ["Zt", "fc_rb1", "fc_rb2"],
                             [f"pf{4 + ib % 2}"])
                src_ = pb_[:, 0:512].rearrange("p (q r k) -> p q r k", q=4, r=2)
                dst_ = AT[:, :, :, 4 * ib:4 * ib + 4].rearrange("p k r q -> p q r k")
                if ib % 2 == 0:
                    K.cp("dve", dst_, src_, [f"pf{4 + ib % 2}"], ["AT"])
                else:
                    K.cp("act", dst_, src_, [f"pf{4 + ib % 2}"], ["AT"])
            for kb in range(8):
                pc = K.pf[kb % 2]
                for q in range(8):
                    k1 = 8 * kb + q
                    for m in range(2):
                        rows = slice(64 * m, 64 * m + 64)
                        for r, tab in ((0, gc2), (1, gs2)):
                            K.mm(pc[rows, q * 64:(q + 1) * 64], AT[rows, k1, r, :], tab[rows, k1, :], r == 0, r == 1,
                                 ["AT", "fc_gc2", "fc_gs2"], [f"pf{kb % 2}"], skip_group_check=True)
                lat_y = ycT[:, NCTX:NTOK].rearrange("p (k2 k1) -> p k1 k2", k1=64)[:, 8 * kb:8 * kb + 8, :]
                lat_g = gz[:, NCTX:NTOK].rearrange("p (k2 k1) -> p k1 k2", k1=64)[:, 8 * kb:8 * kb + 8, :]
                K.tt("dve", lat_y, pc[:, 0:512].rearrange("p (q k) -> p q k", q=8), lat_g, ALU.mult,
                     [f"pf{kb % 2}", "fgz"], ["ycT"])
            K.dma("sp", "ycT", d["yT"][2, 2 * i + e, :, :], ycT, ["ycT"], [("yT", 2, i, e)])


ORD = (list(range(NCH)), [3, 2, 1, 0] + list(range(NCH - 1, 3, -1)))


def phase_mlstm(K, l, s):
    S, d = K.S, K.d
    K.reset()
    cf = {}
    for nm, shp in (("maskf", [128, 64]), ("maskb", [128, 64]), ("trif", [128, 128]), ("trib", [128, 128]),
                    ("chsel", [128, 2]), ("sel", [NA, NA, 128]), ("dlt", [NA, NT, NA]), ("ones2", [NA, 64])):
        cf[nm] = K.sb("m_" + nm, shp, F32)
        K.dma("sp", "mc_" + nm, cf[nm], d["c_" + nm], (), ["m_" + nm])
    wg = K.sb("wg", [128, 8, 4 * NA], BF16)
    bifb = K.sb("bifb", [128, 4 * NA], F32)
    anwb = K.sb("anwb", [128, 512], F32)
    K.dma("pool", "wg", wg, wslice(K, l, s * SUBW + 2048, 4 * NA), (), ["wg"])
    K.dma("sp", "bifb", bifb, d["bif"][l, s * 4 * NA:(s + 1) * 4 * NA].partition_broadcast(128), (), ["bifb"])
    K.dma("sp", "anwb", anwb, d["anw"][l, 512 * s:512 * s + 512].partition_broadcast(128), (), ["anwb"])
    K.ts("dve", anwb, anwb, 0.25, None, ALU.mult, None, ["anwb"], ["anwb"])
    G = 4 * NA
    Gt = K.sb("Gt", [128, NT, G], F32)
    pgt = K.pf[0][:, 0:NT * G]
    for j in range(NT):
        for k in range(8):
            K.mm(pgt[:, j * G:(j + 1) * G], K.hT[:, k, j * 128:(j + 1) * 128], wg[:, k, :], k == 0, k == 7,
                 ["wg"] + hk([j], k), ["pf0"])
    K.tt("dve", Gt, pgt.rearrange("p (j g) -> p j g", g=G), bifb.unsqueeze(1).to_broadcast([128, NT, G]), ALU.add,
         ["pf0", "bifb"], ["Gt"])
    WK = K.sb("WK", [128, 2, NT, NA], F32)
    EC = K.sb("EC", [128, 2, NT, NA], F32)
    decB = K.sb("decB", [128, 2, NA, NCH], F32)
    sh3 = [128, NT, NA]
    gA = K.sb("gA", sh3, F32); gE = K.sb("gE", sh3, F32); gL = K.sb("gL", sh3, F32); Fg = K.sb("Fg", sh3, F32)
    Bs = K.sb("Bs", sh3, F32); U = K.sb("U", sh3, F32); g1 = K.sb("g1", sh3, F32); g2 = K.sb("g2", sh3, F32)
    blastF = K.sb("blastF", [NA, NCH], F32); umaxF = K.sb("umaxF", [NA, NCH], F32); mst = K.sb("mst", [NA, NCH], F32)
    Mc = K.sb("Mc", [NA, NCH], F32); dd = K.sb("dd", [NA, NCH], F32); dec = K.sb("dec", [NA, NCH], F32)
    Rp = [K.sb(f"Rp{i}", [NA, NT, NA], F32) for i in range(2)]
    fl = lambda t_: t_.rearrange("p j h -> p (j h)")
    for D_ in range(2):
        PI = Gt[:, :, 2 * D_ * NA:2 * D_ * NA + NA]
        PF = Gt[:, :, (2 * D_ + 1) * NA:(2 * D_ + 2) * NA]
        tri = cf["trif" if D_ == 0 else "trib"]
        K.stt(gA, PF, -1.0, PF, ALU.mult, ALU.max, ["Gt"], ["gA"])
        K.act(gE, gA, AF.Exp, ["gA"], ["gE"], scale=-1.0)
        K.act(gL, gE, AF.Ln, ["gE"], ["gL"], bias=1.0)
        K.stt(Fg, PF, 0.0, gL, ALU.min, ALU.subtract, ["Gt", "gL"], ["Fg"])
        K.mm(K.pf[1][:, 0:NT * NA], tri, fl(Fg), True, True, ["Fg", "m_trif", "m_trib"], ["pf1"])
        K.cp("dve", fl(Bs), K.pf[1][:, 0:NT * NA], ["pf1"], ["Bs"])
        K.tt("dve", U, PI, Bs, ALU.subtract, ["Gt", "Bs"], ["U"])
        for j in range(NT):
            K.mm(K.pf[2][0:NA, 2 * j:2 * j + 2], Fg[:, j, :], cf["chsel"], True, True, ["Fg", "m_chsel"], ["pf2"])
        K.cp("dve", blastF, K.pf[2][0:NA, 0:NCH], ["pf2"], ["blastF"])
        for jb in range(0, NT, 4):
            nb_ = min(4, NT - jb)
            pu = K.pf[3 + (jb // 4) % 2]
            for q in range(nb_):
                K.tr(pu[0:NA, q * 128:(q + 1) * 128], U[:, jb + q, :], K.identf, ["U", "identf"],
                     [f"pf{3 + (jb // 4) % 2}"])
            K.S.op("dve", lambda h, pu=pu, jb=jb, nb_=nb_: h.tensor_reduce(
                umaxF[:, 2 * jb:2 * jb + 2 * nb_], pu[0:NA, 0:nb_ * 128].rearrange("p (c t) -> p c t", t=64),
                AX.X, ALU.max), [f"pf{3 + (jb // 4) % 2}"], ["umaxF"])
        od = ORD[D_]
        K.memset("dve", mst[:, od[0]:od[0] + 1], 0.0, ["mst"])
        for ix in range(NCH - 1):
            c, nx = od[ix], od[ix + 1]
            K.ts("dve", mst[:, nx:nx + 1], mst[:, c:c + 1], umaxF[:, c:c + 1], blastF[:, c:c + 1], ALU.max, ALU.add,
                 ["mst", "umaxF", "blastF"], ["mst"])
        K.tt("dve", Mc, mst, umaxF, ALU.max, ["mst", "umaxF"], ["Mc"])
        K.tt("dve", dd, mst, Mc, ALU.subtract, ["mst", "Mc"], ["dd"])
        K.act(dec, dd, AF.Exp, ["dd"], ["dec"])
        for h_ in range(NA):
            K.mm(K.pf[5][:, h_ * NCH:(h_ + 1) * NCH], cf["sel"][:, h_, :], dec, True, True, ["dec", "m_sel"], ["pf5"])
        K.cp("dve", decB[:, D_].rearrange("p h c -> p (h c)"), K.pf[5][:, 0:NA * NCH], ["pf5"], [("decB", D_)])
        for pi in range(2):
            K.tt("dve", Rp[pi], cf["dlt"], Mc[:, pi::2].unsqueeze(2).to_broadcast([NA, NT, NA]), ALU.mult,
                 ["Mc", "m_dlt"], [f"Rp{pi}"])
            K.mm(K.pf[1][64 * pi:64 * pi + 64, 0:NT * NA], cf["ones2"], fl(Rp[pi]), True, True,
                 [f"Rp{pi}", "m_ones2"], ["pf1"])
        mt = K.pf[1][:, 0:NT * NA]
        K.tt("dve", fl(g1), fl(U), mt, ALU.subtract, ["U", "pf1"], ["g1"])
        K.act(fl(WK[:, D_]), fl(g1), AF.Exp, ["g1"], [("WK", D_)])
        K.stt(fl(g2), fl(Bs), -1.0, mt, ALU.mult, ALU.subtract, ["Bs", "pf1"], ["g2"])
        K.act(fl(EC[:, D_]), fl(g2), AF.Exp, ["g2"], [("EC", D_)])
    wA = K.sb("wA", [128, 8, 1024], BF16)
    qT = K.sb("mqT", [128, NTOK], BF16)
    kT = K.sb("mkT", [128, NTOK], BF16)
    vaug = K.sb("mvaug", [128, NT, 257], BF16)
    Kp = [K.sb(f"Kp{i}", [128, NT, 128], BF16) for i in range(2)]
    hacc = K.sb("hacc", [128, NT, 256], F32)
    Cst = K.sb("Cst", [128, 257], F32)
    Cdb = [K.sb(f"Cdb{i}", [128, 257], BF16) for i in range(2)]
    aT = [K.sb(f"maT{i}", [128, 64], BF16) for i in range(2)]
    dm = K.sb("mdm", [128, 2], F32)
    tho = K.sb("tho", [128, 512], F32)
    w1 = K.sb("mw1", [128, 256], F32)
    w2 = K.sb("mw2", [128, 256], F32)
    y1 = K.sb("my1", [128, 256], F32)
    y2 = K.sb("my2", [128, 256], BF16)
    junk = K.sb("mjunk", [128, 256], F32)
    ssA = K.sb("ssA", [128, NT], F32)
    rsA = K.sb("rsA", [128, NT], F32)
    ybuf = [K.sb(f"mybuf{i}", [128, 2, 128], BF16) for i in range(2)]
    K.memset("pool", vaug[:, :, 256:257], 1.0, T("mvaug", range(NT)))
    TGS = [(t0, min(512, NTOK - t0)) for t0 in range(0, NTOK, 512)]
    mask = (cf["maskf"], cf["maskb"])
    for i in range(NA):
        K.dma("pool", "wA", wA, wslice(K, l, s * SUBW + i * 1024, 1024), (), ["wA"], max_dma_last_dim=4096)
        for gi, (t0, n) in enumerate(TGS):
            tl = tiles_of(t0, n)
            for k in range(8):
                K.mm(K.pf[0][:, 0:n], wA[:, k, 0:128], K.hT[:, k, t0:t0 + n], k == 0, k == 7, ["wA"] + hk(tl, k),
                     ["pf0"])
            K.cp("act", qT[:, t0:t0 + n], K.pf[0][:, 0:n], ["pf0"], T("mqT", tl))
            for k in range(8):
                K.mm(K.pf[1][:, 0:n], wA[:, k, 128:256], K.hT[:, k, t0:t0 + n], k == 0, k == 7, ["wA"] + hk(tl, k),
                     ["pf1"])
            K.ts("dve", kT[:, t0:t0 + n], K.pf[1][:, 0:n], 128 ** -0.5, None, ALU.mult, None, ["pf1"], T("mkT", tl))
        for j in range(NT):
            pkv = K.pf[2 + j % 2]
            for k in range(8):
                K.mm(pkv[:, 0:384], K.hT[:, k, j * 128:(j + 1) * 128], wA[:, k, 128:512], k == 0, k == 7,
                     ["wA"] + hk([j], k), [f"pf{2 + j % 2}"])
            for D_ in range(2):
                K.ts("dve", Kp[D_][:, j, :], pkv[:, 0:128], WK[:, D_, j, i:i + 1], 128 ** -0.5, ALU.mult, ALU.mult,
                     [f"pf{2 + j % 2}", ("WK", D_)], [("Kp", D_, j)])
            K.cp("dve", vaug[:, j, 0:256], pkv[:, 128:384], [f"pf{2 + j % 2}"], [("mvaug", j)])
        it = 0
        for D_ in range(2):
            K.memset("pool", Cst, 0.0, ["Cst"])
            for c in ORD[D_]:
                j, pi = divmod(c, 2)
                rows = slice(64 * pi, 64 * pi + 64)
                t0 = 64 * c
                b2 = it % 2
                dsc = decB[:, D_, i, c:c + 1]
                K.ts("pool", Cdb[b2], Cst, dsc, 1.0, ALU.mult, ALU.mult, ["Cst", ("decB", D_)], [f"Cdb{b2}"])
                pqk = K.pf[b2]
                K.mm(pqk[rows, 0:64], kT[:, t0:t0 + 64], qT[:, t0:t0 + 64], True, True, [("mkT", j), ("mqT", j)],
                     [f"pf{b2}"])
                pdc = K.pf[4 + b2]
                K.mm(pdc[:, 0:257], Kp[D_][rows, j, :], vaug[rows, j, :], True, True, [("Kp", D_, j), ("mvaug", j)],
                     [f"pf{4 + b2}"])
                K.stt(Cst, Cst, dsc, pdc[:, 0:257], ALU.mult, ALU.add, ["Cst", ("decB", D_), f"pf{4 + b2}"], ["Cst"])
                K.stt(aT[b2][rows, :], pqk[rows, 0:64], WK[rows, D_, j, i:i + 1], mask[D_][rows, :], ALU.mult,
                      ALU.mult, [f"pf{b2}", ("WK", D_), "m_maskf", "m_maskb"], [f"maT{b2}"])
                pnum = K.pf[2 + b2]
                K.mm(pnum[rows, 0:257], aT[b2][rows, :], vaug[rows, j, :], True, False, [f"maT{b2}", ("mvaug", j)],
                     [f"pf{2 + b2}"])
                K.mm(pnum[rows, 0:257], qT[:, t0:t0 + 64], Cdb[b2], False, True, [("mqT", j), f"Cdb{b2}"],
                     [f"pf{2 + b2}"])
                K.ts("dve", dm[rows, 0:1], pnum[rows, 256:257], EC[rows, D_, j, i:i + 1], None, ALU.max, None,
                     [f"pf{2 + b2}", ("EC", D_)], ["mdm"])
                K.stt(dm[rows, 0:1], pnum[rows, 256:257], -1.0, dm[rows, 0:1], ALU.mult, ALU.max,
                      [f"pf{2 + b2}", "mdm"], ["mdm"])
                K.S.op("dve", lambda h, rows=rows: h.reciprocal(dm[rows, 1:2], dm[rows, 0:1]), ["mdm"], ["mdm"])
                if D_ == 0:
                    K.act(hacc[rows, j, :], pnum[rows, 0:256], AF.Copy, [f"pf{2 + b2}", "mdm"], [("hacc", j)],
                          scale=dm[rows, 1:2])
                else:
                    K.stt(hacc[rows, j, :], pnum[rows, 0:256], dm[rows, 1:2], hacc[rows, j, :], ALU.mult, ALU.add,
                          [f"pf{2 + b2}", "mdm", ("hacc", j)], [("hacc", j)])
                it += 1
        for j in range(NT):
            K.act(junk, hacc[:, j, :], AF.Square, [("hacc", j)], ["mjunk"])
            K.S.op("dve", lambda h, j=j: h.tensor_reduce(ssA[:, j:j + 1], junk, AX.X, ALU.add), ["mjunk"], [("ssA", j)])
            K.rstd(rsA[:, j:j + 1], ssA[:, j:j + 1], 256, [("ssA", j)], [("rsA", j)])
            poz = K.pf[j % 2]
            for k in range(8):
                K.mm(poz[:, 0:512], K.hT[:, k, j * 128:(j + 1) * 128], wA[:, k, 512:1024], k == 0, k == 7,
                     ["wA"] + hk([j], k), [f"pf{j % 2}"])
            K.act(tho, poz[:, 0:512], AF.Tanh, [f"pf{j % 2}"], ["tho"], scale=0.5)
            K.stt(w1, tho[:, 256:512], 1.0, poz[:, 256:512], ALU.add, ALU.mult, ["tho", f"pf{j % 2}"], ["mw1"])
            K.stt(w2, tho[:, 0:256], 1.0, w1, ALU.add, ALU.mult, ["tho", "mw1"], ["mw2"])
            K.stt(y1, hacc[:, j, :], rsA[:, j:j + 1], anwb[:, i * 256:(i + 1) * 256], ALU.mult, ALU.mult,
                  [("hacc", j), ("rsA", j), "anwb"], ["my1"])
            K.tt("dve", y2, y1, w2, ALU.mult, ["my1", "mw2"], ["my2"])
            pt = K.pT[j % 2]
            for c2 in range(2):
                K.tr(pt[:, c2 * 128:(c2 + 1) * 128], y2[:, c2 * 128:(c2 + 1) * 128], K.ident, ["my2", "ident"],
                     [f"pT{j % 2}"])
            yb = ybuf[j % 2]
            K.cp("dve", yb.rearrange("p c t -> p (c t)"), pt[:, 0:256], [f"pT{j % 2}"], [f"mybuf{j % 2}"])
            K.dma("sp", f"mybuf{j % 2}", d["yT"][0, 2 * i:2 * i + 2, :, j * 128:(j + 1) * 128].rearrange("c p t -> p c t"),
                  yb, [f"mybuf{j % 2}"], [("yT", 0, i, j)])


def pair_exchange(K, tg):
    d = K.d
    groups = [[0, 1], [2, 3], [4, 5], [6, 7]]
    r0 = tg * 256
    rk = [("xp", 2 * tg), ("xp", 2 * tg + 1)]
    wk = [("xd", 2 * tg), ("xd", 2 * tg + 1)]
    K.S.dma("pool", "cc", lambda h: h.collective_compute("AllReduce", ALU.add, replica_groups=groups,
                                                         ins=[d["xpart"][r0:r0 + 256, :].opt()],
                                                         outs=[d["xcur"][r0:r0 + 256, :].opt()]),
            rk, wk, inc=1)
```
